# Optimizing a Trainium2 kernel written in Bass

```python
import math
import jax, jax.numpy as jnp
from jax import lax
import numpy as np

D_MODEL = 1024
BATCH = 2
SEQ = 8192
DEPTH = 4
DEC_BATCH = 128
DEC_SEQ = 8
PAST_LEN = 8192
PAGE_SIZE = 128

N_A_LAYERS = DEPTH // 2
N_B_LAYERS = DEPTH - N_A_LAYERS
N_META = 16
SSM_EXPAND = 2
D_INNER = SSM_EXPAND * D_MODEL
SSM_HEAD_DIM = 64
N_SSM_HEADS = D_INNER // SSM_HEAD_DIM
N_GROUPS = 8
HEADS_PER_GROUP = N_SSM_HEADS // N_GROUPS
D_STATE = 128
CONV_K = 4
CONV_DIM = D_INNER + 2 * N_GROUPS * D_STATE
IN_PROJ_DIM = D_INNER + CONV_DIM + N_SSM_HEADS
SSD_CHUNK = 128
ATTN_HEAD_DIM = 64
N_Q_HEADS = D_MODEL // ATTN_HEAD_DIM
N_KV_HEADS = 4
Q_PER_KV = N_Q_HEADS // N_KV_HEADS
WINDOW = 128
ATTN_SCALE = ATTN_HEAD_DIM ** -0.5
D_FF = 4 * D_MODEL
EPS = 1e-5

kernel_name = 'yoco_mamba2_swa_sink_hybrid_step'


def rmsnorm(x, w):
    xf = x.astype(jnp.float32)
    xf = xf * lax.rsqrt(jnp.mean(xf * xf, axis=-1, keepdims=True) + EPS)
    return (xf * w.astype(jnp.float32)).astype(x.dtype)


def gated_group_rmsnorm(y, z, w):
    g = (y * jax.nn.silu(z)).astype(jnp.float32)
    shp = g.shape
    g = g.reshape(shp[:-1] + (N_GROUPS, shp[-1] // N_GROUPS))
    g = g * lax.rsqrt(jnp.mean(g * g, axis=-1, keepdims=True) + EPS)
    return (g.reshape(shp) * w.astype(jnp.float32)).astype(y.dtype)


def sq_relu_mlp(h, norm_w, w_up, w_down):
    return h + jnp.square(jax.nn.relu(rmsnorm(h, norm_w) @ w_up)) @ w_down


def _chunk_len(length):
    return SSD_CHUNK if length % SSD_CHUNK == 0 else length


def ssd_scan(x, dt, a_neg, bm, cm, h0):
    f32 = jnp.float32
    b, L, g, r, p = x.shape
    chunk = _chunk_len(L)
    nc = L // chunk

    def blocks(t):
        return t.astype(f32).reshape((b, nc, chunk) + t.shape[2:])

    xdt = blocks(x.astype(f32) * dt[..., None])
    acs = jnp.cumsum(blocks(dt * a_neg), axis=2)
    bb, cc = blocks(bm), blocks(cm)
    causal = jnp.tril(jnp.ones((chunk, chunk), bool))[:, :, None, None]
    seg = acs[:, :, :, None] - acs[:, :, None, :]
    decay = jnp.exp(jnp.where(causal, seg, -jnp.inf))
    cb = jnp.einsum('bclgn,bcsgn->bclsg', cc, bb)
    y_diag = jnp.einsum('bclsgr,bcsgrp->bclgrp', cb[..., None] * decay, xdt)
    decay_to_end = jnp.exp(acs[:, :, -1:] - acs)
    chunk_states = jnp.einsum('bclgn,bclgrp->bcgrpn', bb, decay_to_end[..., None] * xdt)
    chunk_decay = jnp.exp(acs[:, :, -1])

    def step(h, inp):
        s, d = inp
        return d[..., None, None] * h + s, h

    h_final, h_start = lax.scan(step, h0.astype(f32),
                                (jnp.moveaxis(chunk_states, 1, 0), jnp.moveaxis(chunk_decay, 1, 0)))
    h_start = jnp.moveaxis(h_start, 0, 1)
    y_off = jnp.einsum('bclgn,bcgrpn->bclgrp', cc, h_start) * jnp.exp(acs)[..., None]
    y = (y_diag + y_off).reshape(b, L, g, r, p)
    return y, h_final


def mamba_mixer(h, conv_state, ssm_state, norm_w, w_in, conv_w, conv_b, dt_bias, a_log, d_skip, gate_w, w_out):
    b, L, _ = h.shape
    zxbcdt = rmsnorm(h, norm_w) @ w_in
    z = zxbcdt[..., :D_INNER]
    xbc = zxbcdt[..., D_INNER:D_INNER + CONV_DIM]
    dt_raw = zxbcdt[..., D_INNER + CONV_DIM:]
    xpad = jnp.concatenate([conv_state.astype(xbc.dtype), xbc], axis=1)
    new_conv = xpad[:, L:]
    conv = conv_b
    for k in range(CONV_K):
        conv = conv + xpad[:, k:k + L] * conv_w[k]
    xbc = jax.nn.silu(conv)
    xs = xbc[..., :D_INNER].reshape(b, L, N_GROUPS, HEADS_PER_GROUP, SSM_HEAD_DIM)
    bm = xbc[..., D_INNER:D_INNER + N_GROUPS * D_STATE].reshape(b, L, N_GROUPS, D_STATE)
    cm = xbc[..., D_INNER + N_GROUPS * D_STATE:].reshape(b, L, N_GROUPS, D_STATE)
    dt = jax.nn.softplus(dt_raw.astype(jnp.float32) + dt_bias.astype(jnp.float32))
    dt = dt.reshape(b, L, N_GROUPS, HEADS_PER_GROUP)
    a_neg = -jnp.exp(a_log.astype(jnp.float32)).reshape(N_GROUPS, HEADS_PER_GROUP)
    h0 = ssm_state.reshape(b, N_GROUPS, HEADS_PER_GROUP, SSM_HEAD_DIM, D_STATE)
    y, h_new = ssd_scan(xs, dt, a_neg, bm, cm, h0)
    y = y + d_skip.astype(jnp.float32).reshape(N_GROUPS, HEADS_PER_GROUP)[:, :, None] * xs.astype(jnp.float32)
    y = y.reshape(b, L, D_INNER).astype(h.dtype)
    out = gated_group_rmsnorm(y, z, gate_w) @ w_out
    return out, new_conv, h_new.reshape(b, N_SSM_HEADS, SSM_HEAD_DIM, D_STATE).astype(ssm_state.dtype)


def shared_kv(h, kv_norm_w, w_kv):
    b, L, _ = h.shape
    kv = (rmsnorm(h, kv_norm_w) @ w_kv).reshape(b, L, 2, N_KV_HEADS, ATTN_HEAD_DIM)
    return kv[:, :, 0], kv[:, :, 1]


def queries(h, norm_w, w_q):
    b, L, _ = h.shape
    return (rmsnorm(h, norm_w) @ w_q).reshape(b, L, N_KV_HEADS, Q_PER_KV, ATTN_HEAD_DIM)


def sink_softmax(scores, mask, sinks):
    s = jnp.where(mask, scores, -jnp.inf)
    sink = jnp.broadcast_to(sinks.astype(jnp.float32).reshape(N_KV_HEADS, Q_PER_KV, 1, 1), s.shape[:-1] + (1,))
    return jax.nn.softmax(jnp.concatenate([s, sink], axis=-1), axis=-1)[..., :-1]


def swa_prompt(q, k, v, k_meta, v_meta, sinks):
    b, S = q.shape[:2]
    m = k_meta.shape[0]
    nb = S // WINDOW
    qb = q.reshape(b, nb, WINDOW, N_KV_HEADS, Q_PER_KV, ATTN_HEAD_DIM)

    def band(t, t_meta):
        tb = t.reshape(b, nb, WINDOW, N_KV_HEADS, ATTN_HEAD_DIM)
        prev = jnp.concatenate([jnp.zeros_like(tb[:, :1]), tb[:, :-1]], axis=1)
        meta = jnp.broadcast_to(t_meta, (b, nb) + t_meta.shape)
        return jnp.concatenate([meta, prev, tb], axis=2)

    keys, vals = band(k, k_meta), band(v, v_meta)
    scores = jnp.einsum('bnqkrd,bnskd->bnkrqs', qb, keys, preferred_element_type=jnp.float32) * ATTN_SCALE
    qi = jnp.arange(WINDOW)[:, None]
    ci = jnp.arange(WINDOW)[None, :]
    blk = jnp.arange(nb)[:, None, None]
    meta_m = jnp.ones((nb, WINDOW, m), bool)
    prev_m = (ci > qi)[None] & (blk > 0)
    cur_m = jnp.broadcast_to((ci <= qi)[None], (nb, WINDOW, WINDOW))
    mask = jnp.concatenate([meta_m, prev_m, cur_m], axis=-1)
    probs = sink_softmax(scores, mask[None, :, None, None], sinks)
    out = jnp.einsum('bnkrqs,bnskd->bnqkrd', probs.astype(vals.dtype), vals)
    return out.reshape(b, S, N_Q_HEADS * ATTN_HEAD_DIM)


def swa_sample(q, k_new, v_new, k_buf, v_buf, k_meta, v_meta, sinks):
    b, T = q.shape[:2]
    w_buf = k_buf.shape[1]
    m = k_meta.shape[0]
    keys = jnp.concatenate([jnp.broadcast_to(k_meta, (b,) + k_meta.shape), k_buf, k_new], axis=1)
    vals = jnp.concatenate([jnp.broadcast_to(v_meta, (b,) + v_meta.shape), v_buf, v_new], axis=1)
    scores = jnp.einsum('bqkrd,bskd->bkrqs', q, keys, preferred_element_type=jnp.float32) * ATTN_SCALE
    qi = jnp.arange(T)[:, None]
    pos_q = PAST_LEN + qi
    pos_buf = PAST_LEN - w_buf + jnp.arange(w_buf)[None, :]
    buf_m = (pos_q - pos_buf < WINDOW) & (pos_buf >= m)
    ci = jnp.arange(T)[None, :]
    new_m = (ci <= qi) & (qi - ci < WINDOW)
    mask = jnp.concatenate([jnp.ones((T, m), bool), buf_m, new_m], axis=-1)
    probs = sink_softmax(scores, mask[None, None, None], sinks)
    out = jnp.einsum('bkrqs,bskd->bqkrd', probs.astype(vals.dtype), vals)
    return out.reshape(b, T, N_Q_HEADS * ATTN_HEAD_DIM)


def setup_inputs(seed: int = 0) -> dict:
    key = jax.random.key(seed)
    ks = jax.random.split(key, 32)
    f32 = jnp.float32

    def nrm(k, shape, scale):
        return scale * jax.random.normal(k, shape, f32)

    w_buf = min(WINDOW, PAST_LEN)
    na, nb = N_A_LAYERS, N_B_LAYERS
    dt0 = jnp.exp(jax.random.uniform(ks[10], (na, N_SSM_HEADS), f32, math.log(1e-3), math.log(1e-1)))
    return {
        'x_prompt': nrm(ks[0], (BATCH, SEQ, D_MODEL), 1.0),
        'x_sample': nrm(ks[1], (DEC_BATCH, DEC_SEQ, D_MODEL), 1.0),
        'state_conv': nrm(ks[2], (na, DEC_BATCH, CONV_K - 1, CONV_DIM), 1.0),
        'state_ssm': nrm(ks[3], (na, DEC_BATCH, N_SSM_HEADS, SSM_HEAD_DIM, D_STATE), 0.1),
        'cache_k_win': nrm(ks[4], (DEC_BATCH, w_buf, N_KV_HEADS, ATTN_HEAD_DIM), 1.0),
        'cache_v_win': nrm(ks[5], (DEC_BATCH, w_buf, N_KV_HEADS, ATTN_HEAD_DIM), 1.0),
        'meta_tokens': nrm(ks[6], (N_META, D_MODEL), 1.0),
        'a_norm_w': 1.0 + nrm(ks[7], (na, D_MODEL), 0.02),
        'a_in_proj': nrm(ks[8], (na, D_MODEL, IN_PROJ_DIM), D_MODEL ** -0.5),
        'a_conv_w': nrm(ks[9], (na, CONV_K, CONV_DIM), CONV_K ** -0.5),
        'a_conv_b': nrm(ks[11], (na, CONV_DIM), 0.02),
        'a_dt_bias': dt0 + jnp.log(-jnp.expm1(-dt0)),
        'a_log': jnp.log(jax.random.uniform(ks[12], (na, N_SSM_HEADS), f32, 1.0, 16.0)),
        'a_d_skip': 1.0 + nrm(ks[13], (na, N_SSM_HEADS), 0.02),
        'a_gate_norm_w': 1.0 + nrm(ks[14], (na, D_INNER), 0.02),
        'a_out_proj': nrm(ks[15], (na, D_INNER, D_MODEL), D_INNER ** -0.5),
        'kv_norm_w': 1.0 + nrm(ks[16], (D_MODEL,), 0.02),
        'w_kv': nrm(ks[17], (D_MODEL, 2 * N_KV_HEADS * ATTN_HEAD_DIM), D_MODEL ** -0.5),
        'b_norm_w': 1.0 + nrm(ks[18], (nb, D_MODEL), 0.02),
        'w_q': nrm(ks[19], (nb, D_MODEL, N_Q_HEADS * ATTN_HEAD_DIM), D_MODEL ** -0.5),
        'attn_sinks': nrm(ks[20], (nb, N_Q_HEADS), 0.5),
        'w_o': nrm(ks[21], (nb, N_Q_HEADS * ATTN_HEAD_DIM, D_MODEL), (N_Q_HEADS * ATTN_HEAD_DIM) ** -0.5),
        'mlp_norm_w': 1.0 + nrm(ks[22], (DEPTH, D_MODEL), 0.02),
        'w_up': nrm(ks[23], (DEPTH, D_MODEL, D_FF), D_MODEL ** -0.5),
        'w_down': nrm(ks[24], (DEPTH, D_FF, D_MODEL), D_FF ** -0.5),
        'final_norm_w': 1.0 + nrm(ks[25], (D_MODEL,), 0.02),
    }


def reference(x_prompt, x_sample, state_conv, state_ssm, cache_k_win, cache_v_win, meta_tokens,
              a_norm_w, a_in_proj, a_conv_w, a_conv_b, a_dt_bias, a_log, a_d_skip, a_gate_norm_w, a_out_proj,
              kv_norm_w, w_kv, b_norm_w, w_q, attn_sinks, w_o, mlp_norm_w, w_up, w_down, final_norm_w):
    n_prompt = x_prompt.shape[0]
    w_buf = cache_k_win.shape[1]
    hm = meta_tokens[None].astype(x_prompt.dtype)
    hp = x_prompt
    hs = x_sample
    conv_p_list, ssm_p_list, conv_s_list, ssm_s_list = [], [], [], []
    k_meta = v_meta = k_p = v_p = k_s = v_s = None
    for layer in range(DEPTH):
        if layer < N_A_LAYERS:
            i = layer
            prm = (a_norm_w[i], a_in_proj[i], a_conv_w[i], a_conv_b[i], a_dt_bias[i], a_log[i],
                   a_d_skip[i], a_gate_norm_w[i], a_out_proj[i])
            zero_conv = jnp.zeros((1, CONV_K - 1, CONV_DIM), hm.dtype)
            zero_ssm = jnp.zeros((1, N_SSM_HEADS, SSM_HEAD_DIM, D_STATE), hm.dtype)
            ym, conv_m, ssm_m = mamba_mixer(hm, zero_conv, zero_ssm, *prm)
            yp, conv_p, ssm_p = mamba_mixer(hp, jnp.broadcast_to(conv_m, (n_prompt,) + conv_m.shape[1:]),
                                            jnp.broadcast_to(ssm_m, (n_prompt,) + ssm_m.shape[1:]), *prm)
            ys, conv_s, ssm_s = mamba_mixer(hs, state_conv[i], state_ssm[i], *prm)
            hm, hp, hs = hm + ym, hp + yp, hs + ys
            conv_p_list.append(conv_p)
            ssm_p_list.append(ssm_p)
            conv_s_list.append(conv_s)
            ssm_s_list.append(ssm_s)
            hm = sq_relu_mlp(hm, mlp_norm_w[layer], w_up[layer], w_down[layer])
        else:
            j = layer - N_A_LAYERS
            if j == 0:
                km, vm = shared_kv(hm, kv_norm_w, w_kv)
                k_meta, v_meta = km[0], vm[0]
                k_p, v_p = shared_kv(hp, kv_norm_w, w_kv)
                k_s, v_s = shared_kv(hs, kv_norm_w, w_kv)
            q_p = queries(hp, b_norm_w[j], w_q[j])
            q_s = queries(hs, b_norm_w[j], w_q[j])
            hp = hp + swa_prompt(q_p, k_p, v_p, k_meta, v_meta, attn_sinks[j]) @ w_o[j]
            hs = hs + swa_sample(q_s, k_s, v_s, cache_k_win, cache_v_win, k_meta, v_meta, attn_sinks[j]) @ w_o[j]
        hp = sq_relu_mlp(hp, mlp_norm_w[layer], w_up[layer], w_down[layer])
        hs = sq_relu_mlp(hs, mlp_norm_w[layer], w_up[layer], w_down[layer])
    y_prompt = rmsnorm(hp, final_norm_w)
    y_sample = rmsnorm(hs, final_norm_w)
    prompt_state_conv = jnp.stack(conv_p_list)
    prompt_state_ssm = jnp.stack(ssm_p_list)
    prompt_cache_k_win = k_p[:, -w_buf:]
    prompt_cache_v_win = v_p[:, -w_buf:]
    sample_state_conv = jnp.stack(conv_s_list)
    sample_state_ssm = jnp.stack(ssm_s_list)
    sample_cache_k_win = jnp.concatenate([cache_k_win, k_s], axis=1)[:, -w_buf:]
    sample_cache_v_win = jnp.concatenate([cache_v_win, v_s], axis=1)[:, -w_buf:]
    return (y_prompt, y_sample, prompt_state_conv, prompt_state_ssm, prompt_cache_k_win, prompt_cache_v_win,
            sample_state_conv, sample_state_ssm, sample_cache_k_win, sample_cache_v_win)
```

```python
import numpy as np
import concourse.bass as bass
import concourse.mybir as mybir
from concourse.bass_utils import run_bass_kernel_spmd

F32 = mybir.dt.float32
BF16 = mybir.dt.bfloat16
AF = mybir.ActivationFunctionType
ALU = mybir.AluOpType
AX = mybir.AxisListType

D = 1024
DI = 2048
NH = 32
NG = 8
CONV_DIM = 4096
INP = 6176
DFF = 4096
EPS = 1e-5
NEG = -30000.0
SCALE = 64 ** -0.5
N_CORES = 8
SEQ = 8192


class Sched:
    def __init__(self, nc, n_slots=10):
        self.nc = nc
        self.engs = {'pe': nc.tensor, 'act': nc.scalar, 'dve': nc.vector,
                     'pool': nc.gpsimd, 'sp': nc.sync}
        self.ops = {e: [] for e in self.engs}
        self.sem = {e: nc.alloc_semaphore(name=f"sem_{e}") for e in self.engs}
        self.cnt = {e: 0 for e in self.engs}
        self.waited = {e: {} for e in self.engs}
        self.semobj = {}
        self.latest = {}
        for e in self.engs:
            self.semobj[('c', e)] = self.sem[e]
        self.slots = {}
        for q in ('sp', 'pool'):
            self.slots[q] = []
            for i in range(n_slots):
                s = nc.alloc_semaphore(name=f"dq_{q}_{i}")
                key = ('d', q, i)
                self.semobj[key] = s
                self.slots[q].append([key, 0])
        self.slot_rr = {q: 0 for q in self.slots}
        self.res = {}
        self.n_waits = 0

    def _deps(self, reads, writes):
        deps = []
        for r in reads:
            st = self.res.get(r)
            if st and st[0] is not None:
                deps.append(st[0])
        for w in writes:
            st = self.res.get(w)
            if st:
                if st[0] is not None:
                    deps.append(st[0])
                deps.extend(st[1])
        return deps

    def _emit_waits(self, eng, deps, skip_same_pe=True):
        wd = self.waited[eng]
        need = {}
        for (key, val) in deps:
            if skip_same_pe and eng == 'pe' and key == ('c', 'pe'):
                continue
            if wd.get(key, 0) >= val:
                continue
            if need.get(key, 0) < val:
                need[key] = val
        for key, val in need.items():
            wd[key] = val
            sem = self.semobj[key]
            e = self.engs[eng]
            self.ops[eng].append(lambda e=e, sem=sem, val=val: e.wait_ge(sem, val))
            self.n_waits += 1

    def _commit(self, tok, reads, writes):
        self.latest[tok[0]] = max(self.latest.get(tok[0], 0), tok[1])
        for r in reads:
            st = self.res.setdefault(r, [None, []])
            st[1].append(tok)
            if len(st[1]) > 24:
                best = {}
                for k, v in st[1]:
                    if best.get(k, 0) < v:
                        best[k] = v
                st[1] = list(best.items())
        for w in writes:
            self.res[w] = [tok, []]

    @staticmethod
    def _excl(reads, writes):
        ps = [r for r in reads if isinstance(r, tuple) and r and r[0] in ('pf', 'pb')]
        if not ps:
            return reads, writes
        return [r for r in reads if r not in ps], list(writes) + ps

    def op(self, eng, fn, reads=(), writes=()):
        reads, writes = self._excl(reads, writes)
        deps = self._deps(reads, writes)
        self._emit_waits(eng, deps)
        self.cnt[eng] += 1
        tok = (('c', eng), self.cnt[eng])
        sem = self.sem[eng]
        self.ops[eng].append(lambda fn=fn, sem=sem: fn().then_inc(sem, 1))
        self._commit(tok, reads, writes)
        return tok

    def dma(self, q, out, in_, reads=(), writes=()):
        deps = self._deps(reads, writes)
        i = self.slot_rr[q]
        self.slot_rr[q] = (i + 1) % len(self.slots[q])
        slot = self.slots[q][i]
        key = slot[0]
        if slot[1] > 0:
            deps.append((key, slot[1]))
        self._emit_waits(q, deps, skip_same_pe=False)
        slot[1] += 16
        tok = (key, slot[1])
        sem = self.semobj[key]
        e = self.engs[q]
        self.ops[q].append(lambda e=e, out=out, in_=in_, sem=sem: e.dma_start(out=out, in_=in_).then_inc(sem, 16))
        self._commit(tok, reads, writes)
        return tok

    def barrier(self):
        toks = list(self.latest.items())
        for e in self.engs:
            self._emit_waits(e, toks, skip_same_pe=False)
        self.res = {}

    def emit(self):
        nc = self.nc
        with nc.Block() as block:
            @block.tensor
            def _(e):
                for f in self.ops['pe']:
                    f()

            @block.scalar
            def _(e):
                for f in self.ops['act']:
                    f()

            @block.vector
            def _(e):
                for f in self.ops['dve']:
                    f()

            @block.gpsimd
            def _(e):
                for f in self.ops['pool']:
                    f()

            @block.sync
            def _(e):
                for f in self.ops['sp']:
                    f()


class Carver:
    def __init__(self, region_f32):
        self.r = region_f32
        self.n = region_f32.shape[1]
        self.pos = 0

    def take(self, dtype, *free):
        n = 1
        for f in free:
            n *= f
        nf32 = n if dtype == F32 else (n + 1) // 2
        assert self.pos + nf32 <= self.n, ("arena overflow", self.pos, nf32, self.n)
        v = self.r[:, self.pos:self.pos + nf32]
        self.pos += nf32
        if dtype != F32:
            v = v.bitcast(dtype)[:, 0:n]
        if len(free) == 2:
            v = v.rearrange("p (a b) -> p a b", a=free[0])
        elif len(free) == 3:
            v = v.rearrange("p (a b c) -> p a b c", a=free[0], b=free[1])
        elif len(free) == 4:
            v = v.rearrange("p (a b c d) -> p a b c d", a=free[0], b=free[1], c=free[2])
        return v


class StopBuild(Exception):
    pass


class Builder:
    def __init__(self, NT, do_sample=True, stop_at=None):
        self.NT = NT
        self.stop_at = stop_at
        self.marks = []
        self.do_sample = do_sample
        self.NTOK = NT * 512
        nc = self.nc = bass.Bass("TRN2", target_bir_lowering=False)
        self.S = Sched(nc)
        self._declare_dram()
        self._alloc()

    def _declare_dram(self):
        nc = self.nc

        def din(name, shape):
            return nc.dram_tensor(name, list(shape), F32, kind="ExternalInput").ap()

        def dout(name, shape):
            return nc.dram_tensor(name, list(shape), F32, kind="ExternalOutput").ap()

        NTOK = self.NTOK
        self.d_xT = din("xT", (D, NTOK))
        self.d_xsT = din("xsT", (D, 128))
        self.d_metaT = din("metaT", (D, 16))
        self.d_sconvT = din("sconvT", (2, CONV_DIM, 48))
        self.d_sssmT = din("sssmT", (2, 16, 128, DI))
        self.d_ck = din("ck", (16, 128, 256))
        self.d_cv = din("cv", (16, 128, 256))
        self.d_ckT = din("ckT", (16, 256, 128))
        self.d_w_in = din("w_in", (2, D, INP))
        self.d_w_out = din("w_out", (2, DI, D))
        self.d_w_kv = din("w_kv", (D, 512))
        self.d_w_q = din("w_q", (2, D, D))
        self.d_w_o = din("w_o", (2, D, D))
        self.d_w_up = din("w_up", (4, D, DFF))
        self.d_w_down = din("w_down", (4, DFF, D))
        self.d_vecd = din("vecd", (128, 10, 8))
        self.d_convw = din("convw", (128, 2, 32, 4))
        self.d_convb = din("convb", (128, 2, 32))
        self.d_gatew = din("gatew", (128, 2, 16))
        self.d_hvec = din("hvec", (128, 2, 3, 32))
        self.d_sinks = din("sinks", (128, 2, 16))
        self.d_sinkrows = din("sinkrows", (32, 2, 4))
        self.d_cm = din("cm", (128, 6, 128))
        self.d_identf = din("identf", (128, 128))
        self.d_onesf = din("onesf", (128, 128))
        self.d_seqmask = din("seqmask", (128, 16))
        self.d_amask = din("amask", (128, 2, 272))
        self.d_amask_s = din("amask_s", (32, 152))

        self.o_ypT = dout("ypT", (D, NTOK))
        self.o_ysT = dout("ysT", (D, 128))
        self.o_pconv = dout("pconv", (128, 2, 32, 3))
        self.o_pssm = dout("pssm", (2, 128, DI))
        self.o_pk = dout("pk", (128, 256))
        self.o_pv = dout("pv", (128, 256))
        self.o_sconv = dout("sconv_o", (128, 2, 32, 16, 3))
        self.o_sssm = dout("sssm_o", (2, 16, 128, DI))
        self.o_sk = dout("sk_o", (16, 128, 256))
        self.o_sv = dout("sv_o", (16, 128, 256))

    def _alloc(self):
        nc = self.nc
        total_f32 = 52100
        region = nc.alloc_sbuf_tensor("arena", [128, total_f32], F32)
        C = Carver(region[:, :])
        self.h = C.take(F32, 8, 512)
        self.xn = C.take(BF16, 8, 512)
        self.sq = C.take(BF16, 2, 512)
        self.rstd = C.take(F32, 512)
        self.big = C.take(BF16, 32, 512)
        self.ctail = C.take(F32, 2, 32, 3)
        self.zy = C.take(BF16, 4, 2048)
        self.hst = C.take(F32, 2, 2048)
        self.hbf = C.take(BF16, 2048)
        self.wbuf = C.take(BF16, 3, 4096)
        self.cacc = C.take(F32, 2, 512)
        self.kT_all = C.take(BF16, 2, 640)
        self.Vpad = C.take(BF16, 5, 4, 2, 128)
        self.kTm = C.take(BF16, 2, 16)
        self.Vpm = C.take(BF16, 4, 2, 128)
        self.vecd = C.take(F32, 10, 8)
        self.convw = C.take(F32, 2, 32, 4)
        self.convb = C.take(F32, 2, 32)
        self.gatew = C.take(F32, 2, 16)
        self.hvec = C.take(F32, 2, 3, 32)
        self.aneg = C.take(F32, 2, 32)
        self.sinks = C.take(F32, 2, 16)
        self.sinkrows = C.take(F32, 2, 4)
        self.cm = C.take(F32, 6, 128)
        self.identb = C.take(BF16, 128)
        self.onesb = C.take(BF16, 128)
        self.seqmask = C.take(F32, 16)
        self.amask = C.take(F32, 2, 272)
        self.amask_s = C.take(F32, 152)
        self.wdt = C.take(BF16, 2, 8, 32)
        self.sm = {}
        for nm in ('dtb', 'e1', 'dt', 'dtA', 'acs', 'e', 'dd', 'dte', 'cd', 'dtw'):
            self.sm[nm] = C.take(F32, 32)
        self.ss = C.take(F32, 8)
        self.rsg = C.take(F32, 8)
        self.att_small = C.take(F32, 2, 8)
        persistent_end = C.pos
        rest = region[:, persistent_end:total_f32]
        A = Carver(rest)
        self.xpre = A.take(F32, 2, 515)
        self.x_tm = A.take(BF16, 2048)
        self.xdt = A.take(BF16, 2048)
        self.xw = A.take(BF16, 2048)
        self.B_tm = A.take(BF16, 1024)
        self.Gm = A.take(BF16, 8, 128)
        self.Lg = A.take(F32, 2, 4, 128)
        self.ex = A.take(BF16, 2, 4, 128)
        self.LT = A.take(BF16, 2, 4, 128)
        self.t = A.take(F32, 2048)
        self.sqj = A.take(BF16, 256)
        self.xd = A.take(F32, 2, 256)
        self.CTm = A.take(BF16, 8, 128)
        self.Bm = A.take(BF16, 8, 128)
        self.dtAm = A.take(F32, 16, 32)
        self.cdall = A.take(F32, 16, 32)
        ssd_end = A.pos
        self.gn = self.xdt
        self.kvo = self.cacc
        B = Carver(rest)
        self.qT = B.take(BF16, 8, 512)
        self.s_sb = B.take(F32, 2, 272)
        self.pe_sb = B.take(F32, 2, 272)
        self.pn = B.take(BF16, 2, 272)
        self.pT = B.take(BF16, 4, 384)
        self.pnpad = B.take(BF16, 2, 128)
        self.kTbuf = B.take(BF16, 16, 2, 128)
        self.qS = B.take(BF16, 2, 16, 32)
        att_end = B.pos
        self.sbuf_used_f32 = persistent_end + max(ssd_end, att_end)
        self.Vbuf = self.big.rearrange("p (b x) t -> p b (x t)", b=16).rearrange(
            "p b (k h d) -> p b k h d", k=4, h=2)
        self.pf = [nc.alloc_psum_tensor(f"pf{i}", [128, 512], F32) for i in range(6)]
        self.pbt = [nc.alloc_psum_tensor(f"pb{i}", [128, 1024], BF16) for i in range(2)]
        self.pf_rr = 0
        self.pb_rr = 0
        self.held = set()
        self.w_rr = 0

    def mark(self, name):
        self.marks.append(name)
        if self.stop_at is not None and name == self.stop_at:
            raise StopBuild()

    def bank(self, hold=False):
        while True:
            i = self.pf_rr
            self.pf_rr = (self.pf_rr + 1) % 6
            if i not in self.held:
                break
        if hold:
            self.held.add(i)
        return self.pf[i], ('pf', i), i

    def release(self, i):
        self.held.discard(i)

    def pbank(self):
        i = self.pb_rr
        self.pb_rr = 1 - i
        return self.pbt[i][:, 0:512], ('pb', i)

    def mm(self, out, lhsT, rhs, start, stop, rd, wr):
        nc = self.nc
        self.S.op('pe', lambda: nc.tensor.matmul(out, lhsT=lhsT, rhs=rhs, start=start, stop=stop), rd, wr)

    def tr(self, out, in_, ident, rd, wr):
        nc = self.nc
        self.S.op('pe', lambda: nc.tensor.transpose(out, in_, ident), rd, wr)

    def act(self, func, out, in_, rd, wr, bias=None, scale=None, accum_out=None):
        nc = self.nc
        kw = {}
        if bias is not None:
            kw['bias'] = bias
        if scale is not None:
            kw['scale'] = scale
        if accum_out is not None:
            kw['accum_out'] = accum_out
        self.S.op('act', lambda: nc.scalar.activation(out=out, in_=in_, func=func, **kw), rd, wr)

    def tt(self, eng, out, in0, in1, op, rd, wr):
        E = self.S.engs[eng]
        self.S.op(eng, lambda: E.tensor_tensor(out=out, in0=in0, in1=in1, op=op), rd, wr)

    def ts(self, eng, out, in0, s1, op0, rd, wr, s2=None, op1=None):
        E = self.S.engs[eng]
        if op1 is None:
            self.S.op(eng, lambda: E.tensor_scalar(out=out, in0=in0, scalar1=s1, scalar2=None, op0=op0), rd, wr)
        else:
            self.S.op(eng, lambda: E.tensor_scalar(out=out, in0=in0, scalar1=s1, scalar2=s2, op0=op0, op1=op1),
                      rd, wr)

    def stt(self, eng, out, in0, scalar, in1, op0, op1, rd, wr):
        E = self.S.engs[eng]
        self.S.op(eng, lambda: E.scalar_tensor_tensor(out=out, in0=in0, scalar=scalar, in1=in1, op0=op0, op1=op1),
                  rd, wr)

    def cp(self, eng, out, in_, rd, wr):
        if eng == 'act':
            nc = self.nc
            self.S.op('act', lambda: nc.scalar.copy(out=out, in_=in_), rd, wr)
        else:
            E = self.S.engs[eng]
            self.S.op(eng, lambda: E.tensor_copy(out=out, in_=in_), rd, wr)

    def memset(self, eng, ap, val, wr):
        E = self.S.engs[eng]
        self.S.op(eng, lambda: E.memset(ap, val), (), wr)

    def recip(self, out, in_, rd, wr):
        nc = self.nc
        self.S.op('dve', lambda: nc.vector.reciprocal(out=out, in_=in_), rd, wr)

    def make_plan(self, kinds):
        plan = []
        sids = {}

        def add(key, src, kc, ncol):
            if key not in sids:
                sids[key] = len(sids)
            plan.append((src, kc, ncol, sids[key]))

        def layer_mlp(layer):
            for half in range(2):
                for s in range(4):
                    add(('up', layer, half, s),
                        self.d_w_up[layer, :, half * 2048 + s * 512: half * 2048 + (s + 1) * 512], 8, 512)
                for s in range(4):
                    add(('down', layer, half, s),
                        self.d_w_down[layer, half * 2048:(half + 1) * 2048, s * 256:(s + 1) * 256], 16, 256)

        for kind in kinds:
            for layer in range(2):
                for s in range(8):
                    add(('xbc', layer, s), self.d_w_in[layer, :, 2048 + s * 512: 2048 + (s + 1) * 512], 8, 512)
                for s in range(4):
                    add(('z', layer, s), self.d_w_in[layer, :, s * 512:(s + 1) * 512], 8, 512)
                for s in range(4):
                    add(('out', layer, s), self.d_w_out[layer, :, s * 256:(s + 1) * 256], 16, 256)
                layer_mlp(layer)
            add(('kv',), self.d_w_kv[:, :], 8, 512)
            if kind != 'meta':
                for j in range(2):
                    for s in range(2):
                        add(('q', j, s), self.d_w_q[j, :, s * 512:(s + 1) * 512], 8, 512)
                    for s in range(2):
                        add(('o', j, s), self.d_w_o[j, :, s * 512:(s + 1) * 512], 8, 512)
                    layer_mlp(2 + j)
        self.plan = plan
        self.plan_issued = 0
        self.plan_pos = 0
        self.sid_done = set()
        self.wsc = self.nc.dram_tensor("wsc", [len(sids), 128, 4096], BF16).ap()

    def _issue_w(self, idx):
        src, kc, ncol, sid = self.plan[idx]
        b = idx % 3
        if sid not in self.sid_done:
            dst = self.wbuf[:, b, 0:kc * ncol].rearrange("p (k n) -> p k n", k=kc)
            self.S.dma('pool', dst, src.rearrange("(k p) n -> p k n", p=128), reads=(), writes=[('w', b)])
            self.S.dma('sp', self.wsc[sid], self.wbuf[:, b, :], reads=[('w', b)], writes=[('wsc', sid)])
            self.sid_done.add(sid)
        else:
            self.S.dma('pool', self.wbuf[:, b, :], self.wsc[sid], reads=[('wsc', sid)], writes=[('w', b)])

    def wnext(self):
        idx = self.plan_pos
        while self.plan_issued < min(len(self.plan), idx + 3):
            self._issue_w(self.plan_issued)
            self.plan_issued += 1
        src, kc, ncol, sid = self.plan[idx]
        b = idx % 3
        self.plan_pos += 1
        return self.wbuf[:, b, 0:kc * ncol].rearrange("p (k n) -> p k n", k=kc), ('w', b)

    def prologue(self):
        S = self.S
        loads = [
            (self.vecd, self.d_vecd), (self.convw, self.d_convw), (self.convb, self.d_convb),
            (self.gatew, self.d_gatew), (self.hvec, self.d_hvec), (self.sinks, self.d_sinks),
            (self.cm, self.d_cm), (self.seqmask, self.d_seqmask), (self.amask, self.d_amask),
        ]
        for dst, src in loads:
            S.dma('sp', dst, src, writes=['const'])
        S.dma('sp', self.sinkrows[0:32], self.d_sinkrows, writes=['const'])
        S.dma('sp', self.amask_s[0:32], self.d_amask_s, writes=['const'])
        S.dma('pool', self.identb, self.d_identf, writes=['const'])
        S.dma('pool', self.onesb, self.d_onesf, writes=['const'])
        for layer in range(2):
            S.dma('pool', self.wdt[:, layer], self.d_w_in[layer, :, 6144:6176].rearrange("(k p) n -> p k n", p=128),
                  writes=['const'])
        self.act(AF.Exp, self.aneg, self.hvec[:, :, 1, :], ['const'], ['aneg'])
        self.ts('dve', self.aneg, self.aneg, -1.0, ALU.mult, ['aneg'], ['aneg'])
        self.memset('pool', self.hst, 0.0, [('hst', 0), ('hst', 1)])
        self.memset('pool', self.ctail, 0.0, ['ctail_all'])
        self.memset('pool', self.Vpad, 0.0, ['vpad_all'])
        self.memset('pool', self.Vpm, 0.0, ['vpm'])
        self.memset('pool', self.kT_all, 0.0, ['kT_all'])
        S.barrier()

    def rmsnorm(self, T, v, out_f32=None):
        ps, pk, _ = self.bank()
        for blk in range(8):
            i = blk % 2
            self.act(AF.Square, self.sq[:, i, :T], self.h[:, blk, :T], [('h', blk)], [('sq', i)])
            self.mm(ps[:, :T], self.onesb[:, :], self.sq[:, i, :T], blk == 0, blk == 7, [('sq', i)], [pk])
        self.act(AF.Sqrt, self.rstd[:, :T], ps[:, :T], [pk], ['rstd'], bias=EPS, scale=1.0 / D)
        self.recip(self.rstd[:, :T], self.rstd[:, :T], ['rstd'], ['rstd'])
        if out_f32 is None:
            for blk in range(8):
                self.stt('dve', self.xn[:, blk, :T], self.h[:, blk, :T], self.vecd[:, v, blk:blk + 1],
                         self.rstd[:, :T], ALU.mult, ALU.mult, [('h', blk), 'rstd'], [('xn', blk)])

    def mlp(self, T, layer):
        self.rmsnorm(T, 5 + layer)
        for half in range(2):
            for s in range(4):
                wv, wk = self.wnext()
                for j in range(4):
                    jb = 4 * s + j
                    ps, pk, _ = self.bank()
                    for k in range(8):
                        self.mm(ps[:, :T], wv[:, k, j * 128:(j + 1) * 128], self.xn[:, k, :T], k == 0, k == 7,
                                [wk, ('xn', k)], [pk])
                    i = jb % 2
                    self.act(AF.Relu, self.cacc[:, i, :T], ps[:, :T], [pk], [('cacc', i)])
                    self.tt('dve', self.big[:, jb, :T], self.cacc[:, i, :T], self.cacc[:, i, :T], ALU.mult,
                            [('cacc', i)], [('big', jb)])
            for s in range(4):
                wv, wk = self.wnext()
                for j in range(2):
                    db = 2 * s + j
                    ps, pk, _ = self.bank()
                    for kc in range(16):
                        self.mm(ps[:, :T], wv[:, kc, j * 128:(j + 1) * 128], self.big[:, kc, :T], kc == 0, kc == 15,
                                [wk, ('big', kc)], [pk])
                    self.tt('dve', self.h[:, db, :T], self.h[:, db, :T], ps[:, :T], ALU.add,
                            [pk, ('h', db)], [('h', db)])

    def zyv(self, c, blk, L):
        z4 = self.zy.rearrange("p c (b t) -> p c b t", b=16)
        return z4[:, c, blk, 0:L]

    def ssd_layer(self, T, layer, kind):
        L = min(T, 128)
        nch = (T + 127) // 128
        sample = (kind == 'sample')
        self.rmsnorm(T, layer)
        self.mark(f'{kind}.ssd{layer}.norm')
        for s in range(8):
            wv, wk = self.wnext()
            for j in range(4):
                blk = 4 * s + j
                ps, pk, _ = self.bank()
                for k in range(8):
                    self.mm(ps[:, :T], wv[:, k, j * 128:(j + 1) * 128], self.xn[:, k, :T], k == 0, k == 7,
                            [wk, ('xn', k)], [pk])
                i = blk % 2
                cw = self.convw[:, layer, blk, :]
                cb = self.convb[:, layer, blk:blk + 1]
                if not sample:
                    xp = self.xpre[:, i, :]
                    self.cp('pool', xp[:, 0:3], self.ctail[:, layer, blk, :], [('ctail', layer, blk), 'ctail_all'],
                            [('xpre', i)])
                    self.cp('act', xp[:, 3:3 + T], ps[:, :T], [pk], [('xpre', i)])
                    self.act(AF.Identity, self.cacc[:, i, :T], ps[:, :T], [pk], [('cacc', i)],
                             bias=cb, scale=cw[:, 3:4])
                    for k in range(3):
                        self.stt('dve', self.cacc[:, i, :T], xp[:, k:k + T], cw[:, k:k + 1], self.cacc[:, i, :T],
                                 ALU.mult, ALU.add, [('xpre', i), ('cacc', i)], [('cacc', i)])
                    self.act(AF.Silu, self.big[:, blk, :T], self.cacc[:, i, :T], [('cacc', i)], [('big', blk)])
                    self.cp('pool', self.ctail[:, layer, blk, :], xp[:, T:T + 3], [('xpre', i)],
                            [('ctail', layer, blk)])
                else:
                    xp3 = self.xpre[:, i, 0:176].rearrange("p (b k) -> p b k", b=16)
                    ps3 = ps[:, 0:128].rearrange("p (b t) -> p b t", b=16)
                    ca3 = self.cacc[:, i, 0:128].rearrange("p (b t) -> p b t", b=16)
                    self.S.dma('sp', xp3[:, :, 0:3],
                               self.d_sconvT[layer, blk * 128:(blk + 1) * 128, :].rearrange("p (b k) -> p b k", b=16),
                               reads=(), writes=[('xpre', i)])
                    self.cp('act', xp3[:, :, 3:11], ps3, [pk], [('xpre', i)])
                    self.act(AF.Identity, self.cacc[:, i, :128], ps[:, :128], [pk], [('cacc', i)],
                             bias=cb, scale=cw[:, 3:4])
                    for k in range(3):
                        self.stt('dve', ca3, xp3[:, :, k:k + 8], cw[:, k:k + 1], ca3,
                                 ALU.mult, ALU.add, [('xpre', i), ('cacc', i)], [('cacc', i)])
                    self.act(AF.Silu, self.big[:, blk, :128], self.cacc[:, i, :128], [('cacc', i)], [('big', blk)])
                    self.S.dma('sp', self.o_sconv[:, layer, blk, :, :], xp3[:, :, 8:11],
                               reads=[('xpre', i)], writes=[('o_sconv', layer, blk)])
        self.mark(f'{kind}.ssd{layer}.conv')
        for s in range(4):
            wv, wk = self.wnext()
            for ci in range(nch):
                ps, pk, _ = self.bank()
                for k in range(8):
                    self.mm(ps[:L, :512], self.xn[:, k, ci * 128:ci * 128 + L], wv[:, k, :], k == 0, k == 7,
                            [wk, ('xn', k)], [pk])
                self.act(AF.Silu, self.zy[:L, ci, s * 512:(s + 1) * 512], ps[:L, :512], [pk], [('zy', ci)])
        self.mark(f'{kind}.ssd{layer}.z')
        if not sample:
            self.cp('act', self.hbf[:, :], self.hst[:, layer, :], [('hst', layer)], ['hbf'])
        for ci in range(nch):
            self.ssd_chunk(layer, ci, L, sample)
            self.mark(f'{kind}.ssd{layer}.chunk{ci}')
        wo_plan = []
        for s in range(4):
            wv, wk = self.wnext()
            for j in range(2):
                db = 2 * s + j
                ps, pk, _ = self.bank()
                for c in range(nch):
                    for kc in range(16):
                        self.mm(ps[:, c * 128:c * 128 + L], wv[:, kc, j * 128:(j + 1) * 128], self.zyv(c, kc, L),
                                kc == 0, kc == 15, [wk, ('zy', c)], [pk])
                self.tt('dve', self.h[:, db, :T], self.h[:, db, :T], ps[:, :T], ALU.add, [pk, ('h', db)], [('h', db)])

    def ssd_chunk(self, layer, ci, L, sample):
        sm = self.sm
        c0 = ci * 128
        cols = slice(c0, c0 + L)
        TRI = self.cm[:, 3 if sample else 0, :]
        U = self.cm[:, 4 if sample else 1, :]
        ONES = self.cm[:, 5 if sample else 2, :]
        xnk = [('xn', k) for k in range(8)]
        ps, pk, _ = self.bank()
        for k in range(8):
            self.mm(ps[:L, 0:32], self.xn[:, k, cols], self.wdt[:, layer, k, :], k == 0, k == 7, [('xn', k)], [pk])
        self.tt('dve', sm['dtb'][:L], ps[:L, 0:32], self.hvec[:L, layer, 0, :], ALU.add, [pk], ['dtb'])
        self.act(AF.Exp, sm['e1'][:L], sm['dtb'][:L], ['dtb'], ['e1'])
        self.act(AF.Ln, sm['dt'][:L], sm['e1'][:L], ['e1'], ['dt'], bias=1.0)
        self.tt('dve', sm['dtA'][:L], sm['dt'][:L], self.aneg[:L, layer, :], ALU.mult, ['dt', 'aneg'], ['dtA'])
        self.mark(f'c{layer}.{ci}.dt')
        ps, pk, _ = self.bank()
        self.mm(ps[:L, 0:32], TRI[:L, :L], sm['dtA'][:L], True, True, ['dtA'], [pk])
        self.mm(ps[:, 32:64], ONES[:L, :], sm['dtA'][:L], True, True, ['dtA'], [pk])
        self.cp('act', sm['acs'][:L], ps[:L, 0:32], [pk], ['acs'])
        self.act(AF.Exp, sm['e'][:L], ps[:L, 0:32], [pk], ['e'])
        self.act(AF.Exp, sm['cd'][:, :], ps[:, 32:64], [pk], ['cd'])
        self.tt('dve', sm['dd'][:L], ps[:L, 32:64], sm['acs'][:L], ALU.subtract, [pk, 'acs'], ['dd'])
        self.act(AF.Exp, sm['dte'][:L], sm['dd'][:L], ['dd'], ['dte'])
        self.tt('dve', sm['dtw'][:L], sm['dt'][:L], sm['dte'][:L], ALU.mult, ['dt', 'dte'], ['dtw'])
        self.mark(f'c{layer}.{ci}.cum')
        for q in range(4):
            pt, ptk = self.pbank()
            for j in range(4):
                blk = 4 * q + j
                self.tr(pt[:L, j * 128:(j + 1) * 128], self.big[:, blk, cols], self.identb[:, :], [('big', blk)], [ptk])
            self.cp('act', self.x_tm[:L, q * 512:(q + 1) * 512], pt[:L, :512], [ptk], ['x_tm'])
            pt3 = pt[:L, :512].rearrange("p (h d) -> p h d", h=8)
            self.tt('dve', self.xdt[:L, q * 512:(q + 1) * 512].rearrange("p (h d) -> p h d", h=8), pt3,
                    sm['dt'][:L, 8 * q:8 * q + 8].unsqueeze(2).broadcast_to([L, 8, 64]), ALU.mult,
                    [ptk, 'dt'], ['xdt'])
            self.tt('dve', self.xw[:L, q * 512:(q + 1) * 512].rearrange("p (h d) -> p h d", h=8), pt3,
                    sm['dtw'][:L, 8 * q:8 * q + 8].unsqueeze(2).broadcast_to([L, 8, 64]), ALU.mult,
                    [ptk, 'dtw'], ['xw'])
        for q in range(2):
            pt, ptk = self.pbank()
            for j in range(4):
                blk = 16 + 4 * q + j
                self.tr(pt[:L, j * 128:(j + 1) * 128], self.big[:, blk, cols], self.identb[:, :], [('big', blk)], [ptk])
            self.cp('act', self.B_tm[:L, q * 512:(q + 1) * 512], pt[:L, :512], [ptk], ['B_tm'])
        self.mark(f'c{layer}.{ci}.tr')
        for half in range(2):
            ps, pk, _ = self.bank()
            for j in range(4):
                g = 4 * half + j
                self.mm(ps[:L, j * 128:j * 128 + L], self.big[:, 16 + g, cols], self.big[:, 24 + g, cols], True, True,
                        [('big', 16 + g), ('big', 24 + g)], [pk])
            self.tt('dve', self.Gm[:L, 4 * half:4 * half + 4, :L],
                    ps[:L, :].rearrange("p (j l) -> p j l", j=4)[:, :, :L],
                    TRI[:L, :L].unsqueeze(1).broadcast_to([L, 4, L]), ALU.mult, [pk], ['Gm'])
        held = []
        if sample:
            offb = []
            for q in range(4):
                ps, pk, bi = self.bank(hold=True)
                offb.append((ps, pk, bi))
            self.memset('pool', self.CTm[:, :, :], 0.0, ['CTm'])
            self.tt('dve', self.dtAm[:, :, :], sm['dtA'][:, :].unsqueeze(1).broadcast_to([128, 16, 32]),
                    self.seqmask[:, :].unsqueeze(2).broadcast_to([128, 16, 32]), ALU.mult, ['dtA'], ['dtAm'])
            ps, pk, _ = self.bank()
            self.mm(ps[:, :512], self.cm[:, 2, :], self.dtAm.rearrange("p b h -> p (b h)"), True, True, ['dtAm'], [pk])
            self.act(AF.Exp, self.cdall.rearrange("p b h -> p (b h)"), ps[:, :512], [pk], ['cdall'])
            for b in range(16):
                i = b % 2
                st = self.hst[:, i, :]
                self.S.dma('sp', st, self.d_sssmT[layer, b], reads=(), writes=[('hst', i)])
                self.cp('act', self.hbf[:, :], st, [('hst', i)], ['hbf'])
                bc8 = slice(b * 8, b * 8 + 8)
                self.cp('pool', self.CTm[:, :, bc8], self.big[:, 24:32, bc8],
                        [('big', 24 + g) for g in range(8)], ['CTm'])
                self.ts('pool', self.Bm.rearrange("p g n -> p (g n)"), self.B_tm[:, :],
                        self.seqmask[:, b:b + 1], ALU.mult, ['B_tm'], ['Bm'])
                for g in range(8):
                    ps, pk, _bi = offb[g // 2]
                    self.mm(ps[:, (g % 2) * 256:(g % 2) * 256 + 256], self.CTm[:, g, :],
                            self.hbf[:, g * 256:(g + 1) * 256], b == 0 and g % 2 == 0, b == 15, ['CTm', 'hbf'], [pk])
                self.memset('pool', self.CTm[:, :, bc8], 0.0, ['CTm'])
                for q in range(4):
                    ps, pk, _ = self.bank()
                    for gg in range(2):
                        g = 2 * q + gg
                        self.mm(ps[:, gg * 256:(gg + 1) * 256], self.Bm[:, g, :], self.xw[:, g * 256:(g + 1) * 256],
                                True, True, ['Bm', 'xw'], [pk])
                    stq = st[:, q * 512:(q + 1) * 512].rearrange("p (h d) -> p h d", h=8)
                    self.tt('dve', stq, stq,
                            self.cdall[:, b, 8 * q:8 * q + 8].unsqueeze(2).broadcast_to([128, 8, 64]), ALU.mult,
                            [('hst', i), 'cdall'], [('hst', i)])
                    self.tt('dve', st[:, q * 512:(q + 1) * 512], st[:, q * 512:(q + 1) * 512], ps[:, :512], ALU.add,
                            [pk, ('hst', i)], [('hst', i)])
                self.S.dma('sp', self.o_sssm[layer, b], st, reads=[('hst', i)], writes=[('o_sssm', layer, b)])
        self.mark(f'c{layer}.{ci}.G')
        for g in range(8):
            i = g % 2
            self.tt('pool', self.Lg[:L, i, :, :L], U[:L, :L].unsqueeze(1).broadcast_to([L, 4, L]),
                    sm['dtA'][:L, 4 * g:4 * g + 4].unsqueeze(2).broadcast_to([L, 4, L]), ALU.mult,
                    ['dtA'], [('Lg', i)])
            ps, pk, _ = self.bank()
            for j in range(4):
                self.mm(ps[:L, j * 128:j * 128 + L], self.Lg[:L, i, j, :L], TRI[:L, :L], True, True, [('Lg', i)], [pk])
            self.act(AF.Exp, self.ex[:L, i, :, :L], ps[:L, :].rearrange("p (j l) -> p j l", j=4)[:, :, :L],
                     [pk], [('ex', i)])
            self.tt('dve', self.LT[:L, i, :, :L], self.ex[:L, i, :, :L],
                    self.Gm[:L, g, :L].unsqueeze(1).broadcast_to([L, 4, L]), ALU.mult,
                    [('ex', i), 'Gm'], [('LT', i)])
            psd, pdk, _ = self.bank()
            for j in range(4):
                hh = 4 * g + j
                self.mm(psd[:L, j * 64:(j + 1) * 64], self.LT[:L, i, j, :L], self.xdt[:L, hh * 64:(hh + 1) * 64],
                        True, True, [('LT', i), 'xdt'], [pdk])
            if sample:
                pso, pok, _bi = offb[g // 2]
                off_ap = pso[:L, (g % 2) * 256:(g % 2) * 256 + 256]
            else:
                pso, pok, _ = self.bank()
                self.mm(pso[:L, 0:256], self.big[:, 24 + g, cols], self.hbf[:, g * 256:(g + 1) * 256], True, True,
                        [('big', 24 + g), 'hbf'], [pok])
                off_ap = pso[:L, 0:256]
            tg = self.t[:L, g * 256:(g + 1) * 256]
            self.cp('act', tg, psd[:L, 0:256], [pdk], [('t', g)])
            for j in range(4):
                hh = 4 * g + j
                self.stt('dve', tg[:, j * 64:(j + 1) * 64], off_ap[:, j * 64:(j + 1) * 64], sm['e'][:L, hh:hh + 1],
                         tg[:, j * 64:(j + 1) * 64], ALU.mult, ALU.add, [pok, 'e', ('t', g)], [('t', g)])
            self.tt('pool', self.xd[:L, i, :].rearrange("p (h d) -> p h d", h=4),
                    self.x_tm[:L, g * 256:(g + 1) * 256].rearrange("p (h d) -> p h d", h=4),
                    self.hvec[:L, layer, 2, 4 * g:4 * g + 4].unsqueeze(2).broadcast_to([L, 4, 64]), ALU.mult,
                    ['x_tm'], [('xd', i)])
            self.tt('pool', tg, tg, self.xd[:L, i, :], ALU.add, [('xd', i), ('t', g)], [('t', g)])
            self.tt('pool', tg, tg, self.zy[:L, ci, g * 256:(g + 1) * 256], ALU.mult, [('t', g), ('zy', ci)],
                    [('t', g)])
            self.act(AF.Square, self.sqj[:L, :], tg, [('t', g)], ['sqj', ('ss', g)], accum_out=self.ss[:L, g:g + 1])
        if sample:
            for (ps, pk, bi) in offb:
                self.release(bi)
        self.mark(f'c{layer}.{ci}.groups')
        tkeys = [('t', g) for g in range(8)]
        sskeys = [('ss', g) for g in range(8)]
        self.act(AF.Sqrt, self.rsg[:L, :], self.ss[:L, :], sskeys, ['rsg'], bias=EPS, scale=1.0 / 256)
        self.recip(self.rsg[:L, :], self.rsg[:L, :], ['rsg'], ['rsg'])
        self.tt('dve', self.gn[:L, :].rearrange("p (g c) -> p g c", g=8), self.t[:L, :].rearrange("p (g c) -> p g c", g=8),
                self.rsg[:L, :].unsqueeze(2).broadcast_to([L, 8, 256]), ALU.mult, tkeys + ['rsg'], ['xdt'])
        z4 = self.zy.rearrange("p c (b t) -> p c b t", b=16)
        for q in range(4):
            pt, ptk = self.pbank()
            for j in range(4):
                blk = 4 * q + j
                self.tr(pt[:, j * 128:j * 128 + L], self.gn[:L, blk * 128:(blk + 1) * 128], self.identb[:L, :L],
                        ['xdt'], [ptk])
            self.tt('dve', z4[:, ci, 4 * q:4 * q + 4, 0:L],
                    pt[:, :512].rearrange("p (j l) -> p j l", j=4)[:, :, :L],
                    self.gatew[:, layer, 4 * q:4 * q + 4].unsqueeze(2).broadcast_to([128, 4, L]), ALU.mult,
                    [ptk], [('zy', ci)])
        self.mark(f'c{layer}.{ci}.yn')
        if not sample:
            st = self.hst[:, layer, :]
            for q in range(4):
                ps, pk, _ = self.bank()
                for gg in range(2):
                    g = 2 * q + gg
                    self.mm(ps[:, gg * 256:(gg + 1) * 256], self.B_tm[:L, g * 128:(g + 1) * 128],
                            self.xw[:L, g * 256:(g + 1) * 256], True, True, ['B_tm', 'xw'], [pk])
                stq = st[:, q * 512:(q + 1) * 512].rearrange("p (h d) -> p h d", h=8)
                self.tt('dve', stq, stq, sm['cd'][:, 8 * q:8 * q + 8].unsqueeze(2).broadcast_to([128, 8, 64]),
                        ALU.mult, [('hst', layer), 'cd'], [('hst', layer)])
                self.tt('dve', st[:, q * 512:(q + 1) * 512], st[:, q * 512:(q + 1) * 512], ps[:, :512], ALU.add,
                        [pk, ('hst', layer)], [('hst', layer)])
            self.cp('act', self.hbf[:, :], st, [('hst', layer)], ['hbf'])

    def kv_proj(self, T, kind, last):
        L = min(T, 128)
        nch = (T + 127) // 128
        self.rmsnorm(T, 2)
        wv, wk = self.wnext()
        for kp in range(2):
            ps, pk, _ = self.bank()
            for k in range(8):
                self.mm(ps[:, :T], wv[:, k, kp * 128:(kp + 1) * 128], self.xn[:, k, :T], k == 0, k == 7,
                        [wk, ('xn', k)], [pk])
            if kind == 'meta':
                self.cp('act', self.kTm[:, kp, :T], ps[:, :T], [pk], ['kTm'])
            else:
                self.cp('act', self.kT_all[:, kp, 128:128 + T], ps[:, :T], [pk], ['kT_all'])
        for ci in range(nch):
            cols = slice(ci * 128, ci * 128 + L)
            ps, pk, _ = self.bank()
            for k in range(8):
                self.mm(ps[:L, 0:256], self.xn[:, k, cols], wv[:, k, 256:512], k == 0, k == 7, [wk, ('xn', k)], [pk])
            ps4 = ps[:L, 0:256].rearrange("p (k d) -> p k d", k=4)
            if kind == 'meta':
                self.cp('act', self.Vpm[:L, :, 0, 0:64], ps4, [pk], ['vpm'])
                self.cp('dve', self.Vpm[:L, :, 1, 64:128], ps4, [pk], ['vpm'])
            else:
                self.cp('act', self.Vpad[:L, ci + 1, :, 0, 0:64], ps4, [pk], [('vpad', ci + 1), 'vpad_all'])
                self.cp('dve', self.Vpad[:L, ci + 1, :, 1, 64:128], ps4, [pk], [('vpad', ci + 1), 'vpad_all'])
            want_out = (kind == 'sample') or (kind == 'prompt' and last and ci == nch - 1)
            if want_out:
                ps2, pk2, _ = self.bank()
                for k in range(8):
                    self.mm(ps2[:L, 0:256], self.xn[:, k, cols], wv[:, k, 0:256], k == 0, k == 7, [wk, ('xn', k)],
                            [pk2])
                self.cp('act', self.kvo[:, 0, 0:256], ps2[:, 0:256], [pk2], [('cacc', 0)])
                self.cp('dve', self.kvo[:, 1, 0:256], ps[:, 0:256], [pk], [('cacc', 1)])
                if kind == 'prompt':
                    self.S.dma('sp', self.o_pk, self.kvo[:, 0, 0:256], reads=[('cacc', 0)], writes=['o_pk'])
                    self.S.dma('sp', self.o_pv, self.kvo[:, 1, 0:256], reads=[('cacc', 1)], writes=['o_pv'])
                else:
                    for b in range(16):
                        self.S.dma('sp', self.o_sk[b, 120:128, :], self.kvo[b * 8:(b + 1) * 8, 0, 0:256],
                                   reads=[('cacc', 0)], writes=[('o_sk_new', b)])
                        self.S.dma('sp', self.o_sv[b, 120:128, :], self.kvo[b * 8:(b + 1) * 8, 1, 0:256],
                                   reads=[('cacc', 1)], writes=[('o_sv_new', b)])

    def q_proj(self, T, j):
        self.rmsnorm(T, 3 + j)
        for s in range(2):
            wv, wk = self.wnext()
            for jj in range(4):
                m = 4 * s + jj
                ps, pk, _ = self.bank()
                for k in range(8):
                    self.mm(ps[:, :T], wv[:, k, jj * 128:(jj + 1) * 128], self.xn[:, k, :T], k == 0, k == 7,
                            [wk, ('xn', k)], [pk])
                self.cp('act', self.qT[:, m, :T], ps[:, :T], [pk], [('qT', m)])

    def softmax_rows(self, M, N, i, sc, sck, maskap, sink_ap):
        a = self.att_small
        mx, negm, rsum, es, den, rden = (a[:M, i, c:c + 1] for c in range(6))
        ak = ('asm', i)
        self.stt('dve', self.s_sb[:M, i, :N], sc, SCALE, maskap, ALU.mult, ALU.add, [sck], [('s_sb', i)])
        nc = self.nc
        s_in = self.s_sb[:M, i, :N]
        self.S.op('dve', lambda: nc.vector.reduce_max(out=mx, in_=s_in, axis=AX.X), [('s_sb', i)], [ak])
        self.tt('dve', mx, mx, sink_ap, ALU.max, [ak], [ak])
        self.ts('dve', negm, mx, -1.0, ALU.mult, [ak], [ak])
        self.act(AF.Exp, self.pe_sb[:M, i, :N], self.s_sb[:M, i, :N], [('s_sb', i), ak], [('pe', i), ('rs', i)],
                 bias=negm, accum_out=rsum)
        self.act(AF.Exp, es, sink_ap, [ak], [('es', i)], bias=negm)
        self.tt('dve', den, rsum, es, ALU.add, [('rs', i), ('es', i)], [('den', i)])
        self.recip(rden, den, [('den', i)], [('den', i)])
        self.ts('dve', self.pn[:M, i, :N], self.pe_sb[:M, i, :N], rden, ALU.mult, [('pe', i), ('den', i)],
                [('pn', i)])

    def attn_prompt(self, T, j, first_tile):
        nch = T // 128
        z4 = self.zy.rearrange("p c (b t) -> p c b t", b=16)
        it = 0
        for ci in range(nch):
            cols = slice(ci * 128, ci * 128 + 128)
            which = 1 if (first_tile and ci == 0) else 0
            for m in range(8):
                kp, r = m // 4, m % 4
                pv, pvk, _ = self.bank()
                for half in range(2):
                    kvh = 2 * kp + half
                    head = kvh * 4 + r
                    i = it % 2
                    i4 = it % 4
                    it += 1
                    hs = slice(half * 64, half * 64 + 64)
                    sc, sck, _ = self.bank()
                    q = self.qT[hs, m, cols]
                    self.mm(sc[:, 0:16], q, self.kTm[hs, kp, 0:16], True, True, [('qT', m), 'kTm'], [sck])
                    self.mm(sc[:, 16:272], q, self.kT_all[hs, kp, ci * 128:ci * 128 + 256], True, True,
                            [('qT', m), 'kT_all'], [sck])
                    self.softmax_rows(128, 272, i, sc[:, 0:272], sck, self.amask[:, which, :],
                                      self.sinks[:, j, head:head + 1])
                    pt, ptk = self.pbank()
                    self.tr(pt[0:16, 0:128], self.pn[:, i, 0:16], self.identb[:, :], [('pn', i)], [ptk])
                    self.tr(pt[:, 128:256], self.pn[:, i, 16:144], self.identb[:, :], [('pn', i)], [ptk])
                    self.tr(pt[:, 256:384], self.pn[:, i, 144:272], self.identb[:, :], [('pn', i)], [ptk])
                    self.cp('act', self.pT[:, i4, :], pt[:, 0:384], [ptk], [('pT', i4)])
                    self.mm(pv[:, 0:128], self.Vpm[0:16, kvh, half, :], self.pT[0:16, i4, 0:128], half == 0, False,
                            ['vpm', ('pT', i4)], [pvk])
                    self.mm(pv[:, 0:128], self.Vpad[:, ci, kvh, half, :], self.pT[:, i4, 128:256], False, False,
                            [('vpad', ci), 'vpad_all', ('pT', i4)], [pvk])
                    self.mm(pv[:, 0:128], self.Vpad[:, ci + 1, kvh, half, :], self.pT[:, i4, 256:384], False,
                            half == 1, [('vpad', ci + 1), 'vpad_all', ('pT', i4)], [pvk])
                self.cp('act', z4[:, ci, m, 0:128], pv[:, 0:128], [pvk], [('zy', ci)])

    def attn_sample(self, j):
        z4 = self.zy.rearrange("p c (b t) -> p c b t", b=16)
        it = 0
        for kp in range(2):
            self.cp('pool', self.qS[:, kp].rearrange("p b (r t) -> p b r t", r=4),
                    self.qT[:, kp * 4:kp * 4 + 4, 0:128].rearrange("p r (b t) -> p b r t", b=16),
                    [('qT', kp * 4 + r) for r in range(4)], ['qS'])
        for b in range(16):
            bc = slice(b * 8, b * 8 + 8)
            for kp in range(2):
                pv, pvk, _ = self.bank()
                used = []
                for half in range(2):
                    kvh = 2 * kp + half
                    i = it % 2
                    i4 = it % 4
                    it += 1
                    used.append(i4)
                    hs = slice(half * 64, half * 64 + 64)
                    sc, sck, _ = self.bank()
                    q = self.qS[hs, kp, b, :]
                    self.mm(sc[:32, 0:16], q, self.kTm[hs, kp, 0:16], True, True, ['qS', 'kTm'], [sck])
                    self.mm(sc[:32, 16:144], q, self.kTbuf[hs, b, kp, :], True, True, ['kTbuf'], [sck])
                    self.mm(sc[:32, 144:152], q, self.kT_all[hs, kp, 128 + b * 8:128 + b * 8 + 8], True, True,
                            ['kT_all'], [sck])
                    self.softmax_rows(32, 152, i, sc[:32, 0:152], sck, self.amask_s[0:32, :],
                                      self.sinkrows[0:32, j, kvh:kvh + 1])
                    self.cp('pool', self.pnpad[:32, i, bc], self.pn[:32, i, 144:152], [('pn', i)], [('pnpad', i)])
                    pt, ptk = self.pbank()
                    self.tr(pt[0:16, 0:32], self.pn[:32, i, 0:16], self.identb[:32, :32], [('pn', i)], [ptk])
                    self.tr(pt[:, 32:64], self.pn[:32, i, 16:144], self.identb[:32, :32], [('pn', i)], [ptk])
                    self.tr(pt[:, 64:96], self.pnpad[:32, i, :], self.identb[:32, :32], [('pnpad', i)], [ptk])
                    self.cp('act', self.pT[:, i4, 0:96], pt[:, 0:96], [ptk], [('pT', i4)])
                    self.memset('pool', self.pnpad[:32, i, bc], 0.0, [('pnpad', i)])
                for r in range(4):
                    rc = slice(r * 8, r * 8 + 8)
                    for half in range(2):
                        kvh = 2 * kp + half
                        i4 = used[half]
                        self.mm(pv[:, rc], self.Vpm[0:16, kvh, half, :], self.pT[0:16, i4, r * 8:r * 8 + 8],
                                half == 0, False, ['vpm', ('pT', i4)], [pvk])
                        self.mm(pv[:, rc], self.Vbuf[:, b, kvh, half, :], self.pT[:, i4, 32 + r * 8:32 + r * 8 + 8],
                                False, False, [('big', 2 * b), ('big', 2 * b + 1), ('pT', i4)], [pvk])
                        self.mm(pv[:, rc], self.Vpad[:, 1, kvh, half, :], self.pT[:, i4, 64 + r * 8:64 + r * 8 + 8],
                                False, half == 1, [('vpad', 1), 'vpad_all', ('pT', i4)], [pvk])
                self.cp('act', z4[:, 0, kp * 4:kp * 4 + 4, bc],
                        pv[:, 0:32].rearrange("p (r t) -> p r t", r=4), [pvk], [('zy', 0)])

    def attn_layer(self, T, j, kind, first_tile):
        nch = (T + 127) // 128
        self.q_proj(T, j)
        if kind == 'prompt':
            self.attn_prompt(T, j, first_tile)
        else:
            self.load_vbuf()
            self.attn_sample(j)
        for s in range(2):
            wv, wk = self.wnext()
            for jj in range(4):
                db = 4 * s + jj
                ps, pk, _ = self.bank()
                for c in range(nch):
                    for mc in range(8):
                        self.mm(ps[:, c * 128:(c + 1) * 128], wv[:, mc, jj * 128:(jj + 1) * 128],
                                self.zyv(c, mc, 128), mc == 0, mc == 7, [wk, ('zy', c)], [pk])
                self.tt('dve', self.h[:, db, :T], self.h[:, db, :T], ps[:, :T], ALU.add, [pk, ('h', db)], [('h', db)])

    def final_out(self, T, dst):
        self.rmsnorm(T, 9, out_f32=True)
        for blk in range(8):
            i = blk % 2
            self.stt('dve', self.cacc[:, i, :T], self.h[:, blk, :T], self.vecd[:, 9, blk:blk + 1],
                     self.rstd[:, :T], ALU.mult, ALU.mult, [('h', blk), 'rstd'], [('cacc', i)])
            self.S.dma('sp', dst[blk * 128:(blk + 1) * 128, :], self.cacc[:, i, :T], reads=[('cacc', i)],
                       writes=[('o_y', blk, id(dst))])

    def run_tile(self, kind, T, src, dst, first_tile=False, last=False):
        S = self.S
        S.dma('sp', self.h[:, :, :T], src.rearrange("(b p) t -> p b t", p=128), reads=(),
              writes=[('h', b) for b in range(8)])
        for layer in range(2):
            self.ssd_layer(T, layer, kind)
            self.mark(f'{kind}.ssd{layer}.out')
            self.mlp(T, layer)
            self.mark(f'{kind}.mlp{layer}')
        self.kv_proj(T, kind, last)
        self.mark(f'{kind}.kv')
        if kind == 'meta':
            return
        S.barrier()
        if kind == 'sample':
            self.load_sample_cache()
        for j in range(2):
            self.attn_layer(T, j, kind, first_tile)
            self.mark(f'{kind}.attn{j}')
            self.mlp(T, 2 + j)
            self.mark(f'{kind}.mlp{2 + j}')
        self.final_out(T, dst)
        self.mark(f'{kind}.final')
        if kind == 'prompt':
            nch = T // 128
            self.cp('pool', self.kT_all[:, :, 0:128], self.kT_all[:, :, T:T + 128], ['kT_all'], ['kT_all'])
            self.cp('pool', self.Vpad[:, 0], self.Vpad[:, nch], [('vpad', nch), 'vpad_all'], [('vpad', 0), 'vpad_all'])
        S.barrier()

    def load_sample_cache(self):
        S = self.S
        self.memset('pool', self.pnpad[:, :, :], 0.0, [('pnpad', 0), ('pnpad', 1)])
        for b in range(16):
            S.dma('pool', self.kTbuf[:, b, :, :], self.d_ckT[b].rearrange("(k p) w -> p k w", p=128), reads=(),
                  writes=['kTbuf'])
        S.dma('sp', self.o_sk[:, 0:120, :], self.d_ck[:, 8:128, :], reads=(), writes=['o_sk_old'])
        S.dma('sp', self.o_sv[:, 0:120, :], self.d_cv[:, 8:128, :], reads=(), writes=['o_sv_old'])

    def load_vbuf(self):
        S = self.S
        bigk = [('big', b) for b in range(32)]
        self.memset('pool', self.big[:, :, :], 0.0, bigk)
        for b in range(16):
            v4 = self.d_cv[b].rearrange("w (k d) -> w k d", k=4)
            S.dma('pool', self.Vbuf[:, b, :, 0, 0:64], v4, reads=(), writes=[('big', 2 * b), ('big', 2 * b + 1)])
            S.dma('pool', self.Vbuf[:, b, :, 1, 64:128], v4, reads=(), writes=[('big', 2 * b), ('big', 2 * b + 1)])

    def build(self):
        S = self.S
        kinds = ['meta'] + ['prompt'] * self.NT + (['sample'] if self.do_sample else [])
        self.make_plan(kinds)
        try:
            self._build_body()
        except StopBuild:
            pass
        S.barrier()
        S.emit()
        return self.nc

    def _build_body(self):
        S = self.S
        self.prologue()
        self.mark('prologue')
        self.run_tile('meta', 16, self.d_metaT, None)
        S.barrier()
        for ti in range(self.NT):
            self.run_tile('prompt', 512, self.d_xT[:, ti * 512:(ti + 1) * 512],
                          self.o_ypT[:, ti * 512:(ti + 1) * 512], first_tile=(ti == 0), last=(ti == self.NT - 1))
        S.dma('sp', self.o_pconv, self.ctail, reads=[('ctail', l, b) for l in range(2) for b in range(32)],
              writes=['o_pconv'])
        for layer in range(2):
            S.dma('sp', self.o_pssm[layer], self.hst[:, layer, :], reads=[('hst', layer)], writes=[('o_pssm', layer)])
        S.barrier()
        if self.do_sample:
            self.run_tile('sample', 128, self.d_xsT, self.o_ysT)


def _consts():
    k = np.arange(128)
    seq = k // 8
    tri = (k[:, None] <= k[None, :]).astype(np.float32)
    U = (k[:, None] > k[None, :]).astype(np.float32)
    ones = np.ones((128, 128), np.float32)
    same = (seq[:, None] == seq[None, :]).astype(np.float32)
    cm = np.stack([tri, U, ones, tri * same, U * same, same], axis=1)
    seqmask = (seq[:, None] == np.arange(16)[None, :]).astype(np.float32)
    qi = k[:, None]
    cj = k[None, :]
    prev = np.where(cj > qi, 0.0, NEG).astype(np.float32)
    cur = np.where(cj <= qi, 0.0, NEG).astype(np.float32)
    meta0 = np.zeros((128, 16), np.float32)
    am0 = np.concatenate([meta0, prev, cur], axis=1)
    am1 = np.concatenate([meta0, np.full((128, 128), NEG, np.float32), cur], axis=1)
    amask = np.stack([am0, am1], axis=1)
    t = np.tile(np.arange(8), 4)[:, None]
    bufm = np.where(np.arange(128)[None, :] > t, 0.0, NEG).astype(np.float32)
    newm = np.where(np.arange(8)[None, :] <= t, 0.0, NEG).astype(np.float32)
    amask_s = np.concatenate([np.zeros((32, 16), np.float32), bufm, newm], axis=1)
    return dict(cm=cm, seqmask=seqmask, amask=amask, amask_s=amask_s,
                identf=np.eye(128, dtype=np.float32), onesf=ones)


def _pvec(v):
    return np.ascontiguousarray(np.asarray(v, np.float32).reshape(8, 128).T)


def prepare_inputs(inp, NT):
    f = lambda a: np.ascontiguousarray(np.asarray(a, dtype=np.float32))
    NTOK = NT * 512
    shared = _consts()
    vecs = [inp['a_norm_w'][0], inp['a_norm_w'][1], inp['kv_norm_w'], inp['b_norm_w'][0], inp['b_norm_w'][1],
            inp['mlp_norm_w'][0], inp['mlp_norm_w'][1], inp['mlp_norm_w'][2], inp['mlp_norm_w'][3],
            inp['final_norm_w']]
    shared['vecd'] = f(np.stack([_pvec(v) for v in vecs], axis=1))
    cw = np.asarray(inp['a_conv_w'], np.float32)
    shared['convw'] = f(cw.reshape(2, 4, 32, 128).transpose(3, 0, 2, 1))
    shared['convb'] = f(np.asarray(inp['a_conv_b'], np.float32).reshape(2, 32, 128).transpose(2, 0, 1))
    shared['gatew'] = f(np.asarray(inp['a_gate_norm_w'], np.float32).reshape(2, 16, 128).transpose(2, 0, 1))
    hv = np.stack([inp['a_dt_bias'], inp['a_log'], inp['a_d_skip']], axis=1)
    shared['hvec'] = f(np.broadcast_to(np.asarray(hv, np.float32)[None], (128, 2, 3, 32)))
    sk = np.asarray(inp['attn_sinks'], np.float32)
    shared['sinks'] = f(np.broadcast_to(sk[None], (128, 2, 16)))
    sr = sk.reshape(2, 4, 4)
    shared['sinkrows'] = f(np.repeat(sr.transpose(2, 0, 1), 8, axis=0))
    shared['metaT'] = f(np.asarray(inp['meta_tokens'], np.float32).T)
    shared['w_in'] = f(inp['a_in_proj'])
    shared['w_out'] = f(inp['a_out_proj'])
    shared['w_kv'] = f(inp['w_kv'])
    wq = np.asarray(inp['w_q'], np.float32).reshape(2, D, 2, 2, 4, 64)
    shared['w_q'] = f(wq.transpose(0, 1, 2, 4, 3, 5).reshape(2, D, D))
    wo = np.asarray(inp['w_o'], np.float32).reshape(2, 2, 2, 4, 64, D)
    shared['w_o'] = f(wo.transpose(0, 1, 3, 2, 4, 5).reshape(2, D, D))
    shared['w_up'] = f(inp['w_up'])
    shared['w_down'] = f(inp['w_down'])
    xp = np.asarray(inp['x_prompt'], np.float32)
    xs = np.asarray(inp['x_sample'], np.float32)
    sc = np.asarray(inp['state_conv'], np.float32)
    ss = np.asarray(inp['state_ssm'], np.float32)
    ck = np.asarray(inp['cache_k_win'], np.float32)
    cv = np.asarray(inp['cache_v_win'], np.float32)
    maps = []
    for c in range(N_CORES):
        m = dict(shared)
        m['xT'] = f(xp[c, :NTOK].T) if c < 2 else np.zeros((D, NTOK), np.float32)
        sl = slice(c * 16, (c + 1) * 16)
        m['xsT'] = f(xs[sl].reshape(128, D).T)
        m['sconvT'] = f(sc[:, sl].transpose(0, 3, 1, 2).reshape(2, CONV_DIM, 48))
        m['sssmT'] = f(ss[:, sl].reshape(2, 16, DI, 128).transpose(0, 1, 3, 2))
        m['ck'] = f(ck[sl].reshape(16, 128, 256))
        m['cv'] = f(cv[sl].reshape(16, 128, 256))
        m['ckT'] = f(ck[sl].reshape(16, 128, 256).transpose(0, 2, 1))
        maps.append(m)
    return maps


_PROGRAMS = {}


def get_program(NT, do_sample=True, stop_at=None):
    key = (NT, do_sample, stop_at)
    if key not in _PROGRAMS:
        _PROGRAMS[key] = Builder(NT, do_sample, stop_at).build()
    return _PROGRAMS[key]


def assemble(results, NT):
    NTOK = NT * 512
    y_prompt = np.stack([results[b]['ypT'].T for b in range(2)], axis=0)
    y_sample = np.concatenate([results[c]['ysT'].T.reshape(16, 8, D) for c in range(N_CORES)], axis=0)
    pconv = np.stack([results[b]['pconv'].transpose(1, 3, 2, 0).reshape(2, 3, CONV_DIM) for b in range(2)], axis=1)
    pssm = np.stack([results[b]['pssm'].transpose(0, 2, 1).reshape(2, NH, 64, 128) for b in range(2)], axis=1)
    pk = np.stack([results[b]['pk'].reshape(128, 4, 64) for b in range(2)], axis=0)
    pv = np.stack([results[b]['pv'].reshape(128, 4, 64) for b in range(2)], axis=0)
    sconv = np.concatenate(
        [results[c]['sconv_o'].transpose(1, 3, 4, 2, 0).reshape(2, 16, 3, CONV_DIM) for c in range(N_CORES)], axis=1)
    sssm = np.concatenate(
        [results[c]['sssm_o'].transpose(0, 1, 3, 2).reshape(2, 16, NH, 64, 128) for c in range(N_CORES)], axis=1)
    sk = np.concatenate([results[c]['sk_o'].reshape(16, 128, 4, 64) for c in range(N_CORES)], axis=0)
    sv = np.concatenate([results[c]['sv_o'].reshape(16, 128, 4, 64) for c in range(N_CORES)], axis=0)
    outs = (y_prompt, y_sample, pconv, pssm, pk, pv, sconv, sssm, sk, sv)
    return tuple(np.ascontiguousarray(o, dtype=np.float32) for o in outs)


def run(inputs, NT, do_sample=True, stop_at=None):
    nc = get_program(NT, do_sample, stop_at)
    maps = prepare_inputs(inputs, NT)
    res = run_bass_kernel_spmd(nc, maps, core_ids=list(range(N_CORES)))
    return assemble(res.results, NT)


def kernel(**inputs):
    return run(inputs, SEQ // 512)
```

```python
import numpy as np
import concourse.bass as bass
import concourse.mybir as mybir
from concourse.bass_utils import run_bass_kernel_spmd

F32 = mybir.dt.float32
BF16 = mybir.dt.bfloat16
AF = mybir.ActivationFunctionType
ALU = mybir.AluOpType
AX = mybir.AxisListType

D = 1024
DI = 2048
NH = 32
NG = 8
CONV_DIM = 4096
INP = 6176
DFF = 4096
EPS = 1e-5
NEG = -30000.0
SCALE = 64 ** -0.5
N_CORES = 8
SEQ = 8192


class Sched:
    def __init__(self, nc, n_slots=10):
        self.nc = nc
        self.engs = {'pe': nc.tensor, 'act': nc.scalar, 'dve': nc.vector,
                     'pool': nc.gpsimd, 'sp': nc.sync}
        self.ops = {e: [] for e in self.engs}
        self.sem = {e: nc.alloc_semaphore(name=f"sem_{e}") for e in self.engs}
        self.cnt = {e: 0 for e in self.engs}
        self.waited = {e: {} for e in self.engs}
        self.semobj = {}
        self.latest = {}
        for e in self.engs:
            self.semobj[('c', e)] = self.sem[e]
        self.slots = {}
        for q in ('sp', 'pool'):
            self.slots[q] = []
            for i in range(n_slots):
                s = nc.alloc_semaphore(name=f"dq_{q}_{i}")
                key = ('d', q, i)
                self.semobj[key] = s
                self.slots[q].append([key, 0])
        self.slot_rr = {q: 0 for q in self.slots}
        self.res = {}
        self.n_waits = 0

    def _deps(self, reads, writes):
        deps = []
        for r in reads:
            st = self.res.get(r)
            if st and st[0] is not None:
                deps.append(st[0])
        for w in writes:
            st = self.res.get(w)
            if st:
                if st[0] is not None:
                    deps.append(st[0])
                deps.extend(st[1])
        return deps

    def _emit_waits(self, eng, deps, skip_same_pe=True):
        wd = self.waited[eng]
        need = {}
        for (key, val) in deps:
            if skip_same_pe and eng == 'pe' and key == ('c', 'pe'):
                continue
            if wd.get(key, 0) >= val:
                continue
            if need.get(key, 0) < val:
                need[key] = val
        for key, val in need.items():
            wd[key] = val
            sem = self.semobj[key]
            e = self.engs[eng]
            self.ops[eng].append(lambda e=e, sem=sem, val=val: e.wait_ge(sem, val))
            self.n_waits += 1

    def _commit(self, tok, reads, writes):
        self.latest[tok[0]] = max(self.latest.get(tok[0], 0), tok[1])
        for r in reads:
            st = self.res.setdefault(r, [None, []])
            st[1].append(tok)
            if len(st[1]) > 24:
                best = {}
                for k, v in st[1]:
                    if best.get(k, 0) < v:
                        best[k] = v
                st[1] = list(best.items())
        for w in writes:
            self.res[w] = [tok, []]

    @staticmethod
    def _excl(reads, writes):
        ps = [r for r in reads if isinstance(r, tuple) and r and r[0] in ('pf', 'pb')]
        if not ps:
            return reads, writes
        return [r for r in reads if r not in ps], list(writes) + ps

    def op(self, eng, fn, reads=(), writes=()):
        reads, writes = self._excl(reads, writes)
        deps = self._deps(reads, writes)
        self._emit_waits(eng, deps)
        self.cnt[eng] += 1
        tok = (('c', eng), self.cnt[eng])
        sem = self.sem[eng]
        self.ops[eng].append(lambda fn=fn, sem=sem: fn().then_inc(sem, 1))
        self._commit(tok, reads, writes)
        return tok

    def dma(self, q, out, in_, reads=(), writes=()):
        deps = self._deps(reads, writes)
        i = self.slot_rr[q]
        self.slot_rr[q] = (i + 1) % len(self.slots[q])
        slot = self.slots[q][i]
        key = slot[0]
        if slot[1] > 0:
            deps.append((key, slot[1]))
        self._emit_waits(q, deps, skip_same_pe=False)
        slot[1] += 16
        tok = (key, slot[1])
        sem = self.semobj[key]
        e = self.engs[q]
        self.ops[q].append(lambda e=e, out=out, in_=in_, sem=sem: e.dma_start(out=out, in_=in_).then_inc(sem, 16))
        self._commit(tok, reads, writes)
        return tok

    def barrier(self):
        toks = list(self.latest.items())
        for e in self.engs:
            self._emit_waits(e, toks, skip_same_pe=False)
        self.res = {}

    def emit(self):
        nc = self.nc
        with nc.Block() as block:
            @block.tensor
            def _(e):
                for f in self.ops['pe']:
                    f()

            @block.scalar
            def _(e):
                for f in self.ops['act']:
                    f()

            @block.vector
            def _(e):
                for f in self.ops['dve']:
                    f()

            @block.gpsimd
            def _(e):
                for f in self.ops['pool']:
                    f()

            @block.sync
            def _(e):
                for f in self.ops['sp']:
                    f()


class Carver:
    def __init__(self, region_f32):
        self.r = region_f32
        self.n = region_f32.shape[1]
        self.pos = 0

    def take(self, dtype, *free):
        n = 1
        for f in free:
            n *= f
        nf32 = n if dtype == F32 else (n + 1) // 2
        assert self.pos + nf32 <= self.n, ("arena overflow", self.pos, nf32, self.n)
        v = self.r[:, self.pos:self.pos + nf32]
        self.pos += nf32
        if dtype != F32:
            v = v.bitcast(dtype)[:, 0:n]
        if len(free) == 2:
            v = v.rearrange("p (a b) -> p a b", a=free[0])
        elif len(free) == 3:
            v = v.rearrange("p (a b c) -> p a b c", a=free[0], b=free[1])
        elif len(free) == 4:
            v = v.rearrange("p (a b c d) -> p a b c d", a=free[0], b=free[1], c=free[2])
        return v


class StopBuild(Exception):
    pass


class Builder:
    def __init__(self, NT, do_sample=True, stop_at=None):
        self.NT = NT
        self.stop_at = stop_at
        self.marks = []
        self.do_sample = do_sample
        self.NTOK = NT * 512
        nc = self.nc = bass.Bass("TRN2", target_bir_lowering=False)
        self.S = Sched(nc)
        self._declare_dram()
        self._alloc()

    def _declare_dram(self):
        nc = self.nc

        def din(name, shape):
            return nc.dram_tensor(name, list(shape), F32, kind="ExternalInput").ap()

        def dout(name, shape):
            return nc.dram_tensor(name, list(shape), F32, kind="ExternalOutput").ap()

        NTOK = self.NTOK
        self.d_xT = din("xT", (D, NTOK))
        self.d_xsT = din("xsT", (D, 128))
        self.d_metaT = din("metaT", (D, 16))
        self.d_sconvT = din("sconvT", (2, CONV_DIM, 48))
        self.d_sssmT = din("sssmT", (2, 16, 128, DI))
        self.d_ck = din("ck", (16, 128, 256))
        self.d_cv = din("cv", (16, 128, 256))
        self.d_ckT = din("ckT", (16, 256, 128))
        self.d_w_in = din("w_in", (2, D, INP))
        self.d_w_out = din("w_out", (2, DI, D))
        self.d_w_kv = din("w_kv", (D, 512))
        self.d_w_q = din("w_q", (2, D, D))
        self.d_w_o = din("w_o", (2, D, D))
        self.d_w_up = din("w_up", (4, D, DFF))
        self.d_w_down = din("w_down", (4, DFF, D))
        self.d_vecd = din("vecd", (128, 10, 8))
        self.d_convw = din("convw", (128, 2, 32, 4))
        self.d_convb = din("convb", (128, 2, 32))
        self.d_gatew = din("gatew", (128, 2, 16))
        self.d_hvec = din("hvec", (128, 2, 3, 32))
        self.d_sinks = din("sinks", (128, 2, 16))
        self.d_sinkrows = din("sinkrows", (32, 2, 4))
        self.d_cm = din("cm", (128, 6, 128))
        self.d_identf = din("identf", (128, 128))
        self.d_onesf = din("onesf", (128, 128))
        self.d_seqmask = din("seqmask", (128, 16))
        self.d_amask = din("amask", (128, 2, 272))
        self.d_amask_s = din("amask_s", (32, 152))

        self.o_ypT = dout("ypT", (D, NTOK))
        self.o_ysT = dout("ysT", (D, 128))
        self.o_pconv = dout("pconv", (128, 2, 32, 3))
        self.o_pssm = dout("pssm", (2, 128, DI))
        self.o_pk = dout("pk", (128, 256))
        self.o_pv = dout("pv", (128, 256))
        self.o_sconv = dout("sconv_o", (128, 2, 32, 16, 3))
        self.o_sssm = dout("sssm_o", (2, 16, 128, DI))
        self.o_sk = dout("sk_o", (16, 128, 256))
        self.o_sv = dout("sv_o", (16, 128, 256))

    def _alloc(self):
        nc = self.nc
        total_f32 = 52100
        region = nc.alloc_sbuf_tensor("arena", [128, total_f32], F32)
        C = Carver(region[:, :])
        self.h = C.take(F32, 8, 512)
        self.xn = C.take(BF16, 8, 512)
        self.sq = C.take(BF16, 2, 512)
        self.rstd = C.take(F32, 512)
        self.big = C.take(BF16, 32, 512)
        self.ctail = C.take(F32, 2, 32, 3)
        self.zy = C.take(BF16, 4, 2048)
        self.hst = C.take(F32, 2, 2048)
        self.hbf = C.take(BF16, 2048)
        self.wbuf = C.take(BF16, 3, 4096)
        self.cacc = C.take(F32, 2, 512)
        self.kT_all = C.take(BF16, 2, 640)
        self.Vpad = C.take(BF16, 5, 4, 2, 128)
        self.kTm = C.take(BF16, 2, 16)
        self.Vpm = C.take(BF16, 4, 2, 128)
        self.vecd = C.take(F32, 10, 8)
        self.convw = C.take(F32, 2, 32, 4)
        self.convb = C.take(F32, 2, 32)
        self.gatew = C.take(F32, 2, 16)
        self.hvec = C.take(F32, 2, 3, 32)
        self.aneg = C.take(F32, 2, 32)
        self.sinks = C.take(F32, 2, 16)
        self.sinkrows = C.take(F32, 2, 4)
        self.cm = C.take(F32, 6, 128)
        self.identb = C.take(BF16, 128)
        self.onesb = C.take(BF16, 128)
        self.seqmask = C.take(F32, 16)
        self.amask = C.take(F32, 2, 272)
        self.amask_s = C.take(F32, 152)
        self.wdt = C.take(BF16, 2, 8, 32)
        self.sm = {}
        for nm in ('dtb', 'e1', 'dt', 'dtA', 'acs', 'e', 'dd', 'dte', 'cd', 'dtw'):
            self.sm[nm] = C.take(F32, 32)
        self.ss = C.take(F32, 8)
        self.rsg = C.take(F32, 8)
        self.att_small = C.take(F32, 2, 8)
        persistent_end = C.pos
        rest = region[:, persistent_end:total_f32]
        A = Carver(rest)
        self.xpre = A.take(F32, 2, 515)
        self.x_tm = A.take(BF16, 2048)
        self.xdt = A.take(BF16, 2048)
        self.xw = A.take(BF16, 2048)
        self.B_tm = A.take(BF16, 1024)
        self.Gm = A.take(BF16, 8, 128)
        self.Lg = A.take(F32, 2, 4, 128)
        self.ex = A.take(BF16, 2, 4, 128)
        self.LT = A.take(BF16, 2, 4, 128)
        self.t = A.take(F32, 2048)
        self.sqj = A.take(BF16, 2, 256)
        self.xd = A.take(F32, 2, 256)
        self.CTm = A.take(BF16, 8, 128)
        self.Bm = A.take(BF16, 8, 128)
        self.dtAm = A.take(F32, 16, 32)
        self.cdall = A.take(F32, 16, 32)
        ssd_end = A.pos
        self.gn = self.xdt
        self.kvo = self.cacc
        B = Carver(rest)
        self.qT = B.take(BF16, 8, 512)
        self.s_sb = B.take(F32, 2, 272)
        self.pe_sb = B.take(F32, 2, 272)
        self.pn = B.take(BF16, 2, 272)
        self.pT = B.take(BF16, 4, 384)
        self.pnpad = B.take(BF16, 2, 128)
        self.kTbuf = B.take(BF16, 16, 2, 128)
        self.qS = B.take(BF16, 2, 16, 32)
        att_end = B.pos
        self.sbuf_used_f32 = persistent_end + max(ssd_end, att_end)
        self.Vbuf = self.big.rearrange("p (b x) t -> p b (x t)", b=16).rearrange(
            "p b (k h d) -> p b k h d", k=4, h=2)
        self.pf = [nc.alloc_psum_tensor(f"pf{i}", [128, 512], F32) for i in range(6)]
        self.pbt = [nc.alloc_psum_tensor(f"pb{i}", [128, 1024], BF16) for i in range(2)]
        self.pf_rr = 0
        self.pb_rr = 0
        self.held = set()
        self.w_rr = 0

    def interleave(self, gens, W):
        from collections import deque
        active = deque()
        it = iter(gens)
        while True:
            while len(active) < W:
                try:
                    active.append(next(it))
                except StopIteration:
                    break
            if not active:
                break
            for _ in range(len(active)):
                g = active.popleft()
                try:
                    next(g)
                    active.append(g)
                except StopIteration:
                    pass

    def mark(self, name):
        self.marks.append(name)
        if self.stop_at is not None and name == self.stop_at:
            raise StopBuild()

    def bank(self, hold=False):
        assert len(self.held) < 6, "all PSUM banks held"
        while True:
            i = self.pf_rr
            self.pf_rr = (self.pf_rr + 1) % 6
            if i not in self.held:
                break
        if hold:
            self.held.add(i)
        return self.pf[i], ('pf', i), i

    def release(self, i):
        self.held.discard(i)

    def pbank(self):
        i = self.pb_rr
        self.pb_rr = 1 - i
        return self.pbt[i][:, 0:512], ('pb', i)

    def mm(self, out, lhsT, rhs, start, stop, rd, wr):
        nc = self.nc
        self.S.op('pe', lambda: nc.tensor.matmul(out, lhsT=lhsT, rhs=rhs, start=start, stop=stop), rd, wr)

    def tr(self, out, in_, ident, rd, wr):
        nc = self.nc
        self.S.op('pe', lambda: nc.tensor.transpose(out, in_, ident), rd, wr)

    def act(self, func, out, in_, rd, wr, bias=None, scale=None, accum_out=None):
        nc = self.nc
        kw = {}
        if bias is not None:
            kw['bias'] = bias
        if scale is not None:
            kw['scale'] = scale
        if accum_out is not None:
            kw['accum_out'] = accum_out
        self.S.op('act', lambda: nc.scalar.activation(out=out, in_=in_, func=func, **kw), rd, wr)

    def tt(self, eng, out, in0, in1, op, rd, wr):
        E = self.S.engs[eng]
        self.S.op(eng, lambda: E.tensor_tensor(out=out, in0=in0, in1=in1, op=op), rd, wr)

    def ts(self, eng, out, in0, s1, op0, rd, wr, s2=None, op1=None):
        E = self.S.engs[eng]
        if op1 is None:
            self.S.op(eng, lambda: E.tensor_scalar(out=out, in0=in0, scalar1=s1, scalar2=None, op0=op0), rd, wr)
        else:
            self.S.op(eng, lambda: E.tensor_scalar(out=out, in0=in0, scalar1=s1, scalar2=s2, op0=op0, op1=op1),
                      rd, wr)

    def stt(self, eng, out, in0, scalar, in1, op0, op1, rd, wr):
        E = self.S.engs[eng]
        self.S.op(eng, lambda: E.scalar_tensor_tensor(out=out, in0=in0, scalar=scalar, in1=in1, op0=op0, op1=op1),
                  rd, wr)

    def cp(self, eng, out, in_, rd, wr):
        if eng == 'act':
            nc = self.nc
            self.S.op('act', lambda: nc.scalar.copy(out=out, in_=in_), rd, wr)
        else:
            E = self.S.engs[eng]
            self.S.op(eng, lambda: E.tensor_copy(out=out, in_=in_), rd, wr)

    def memset(self, eng, ap, val, wr):
        E = self.S.engs[eng]
        self.S.op(eng, lambda: E.memset(ap, val), (), wr)

    def recip(self, out, in_, rd, wr):
        nc = self.nc
        self.S.op('dve', lambda: nc.vector.reciprocal(out=out, in_=in_), rd, wr)

    def make_plan(self, kinds):
        plan = []
        sids = {}

        def add(key, src, kc, ncol):
            if key not in sids:
                sids[key] = len(sids)
            plan.append((src, kc, ncol, sids[key]))

        def layer_mlp(layer):
            for half in range(2):
                for s in range(4):
                    add(('up', layer, half, s),
                        self.d_w_up[layer, :, half * 2048 + s * 512: half * 2048 + (s + 1) * 512], 8, 512)
                for s in range(4):
                    add(('down', layer, half, s),
                        self.d_w_down[layer, half * 2048:(half + 1) * 2048, s * 256:(s + 1) * 256], 16, 256)

        for kind in kinds:
            for layer in range(2):
                for s in range(8):
                    add(('xbc', layer, s), self.d_w_in[layer, :, 2048 + s * 512: 2048 + (s + 1) * 512], 8, 512)
                for s in range(4):
                    add(('z', layer, s), self.d_w_in[layer, :, s * 512:(s + 1) * 512], 8, 512)
                for s in range(4):
                    add(('out', layer, s), self.d_w_out[layer, :, s * 256:(s + 1) * 256], 16, 256)
                layer_mlp(layer)
            add(('kv',), self.d_w_kv[:, :], 8, 512)
            if kind != 'meta':
                for j in range(2):
                    for s in range(2):
                        add(('q', j, s), self.d_w_q[j, :, s * 512:(s + 1) * 512], 8, 512)
                    for s in range(2):
                        add(('o', j, s), self.d_w_o[j, :, s * 512:(s + 1) * 512], 8, 512)
                    layer_mlp(2 + j)
        self.plan = plan
        self.plan_issued = 0
        self.plan_pos = 0
        self.sid_done = set()
        self.wsc = self.nc.dram_tensor("wsc", [len(sids), 128, 4096], BF16).ap()

    def _issue_w(self, idx):
        src, kc, ncol, sid = self.plan[idx]
        b = idx % 3
        if sid not in self.sid_done:
            dst = self.wbuf[:, b, 0:kc * ncol].rearrange("p (k n) -> p k n", k=kc)
            self.S.dma('pool', dst, src.rearrange("(k p) n -> p k n", p=128), reads=(), writes=[('w', b)])
            self.S.dma('sp', self.wsc[sid], self.wbuf[:, b, :], reads=[('w', b)], writes=[('wsc', sid)])
            self.sid_done.add(sid)
        else:
            self.S.dma('pool', self.wbuf[:, b, :], self.wsc[sid], reads=[('wsc', sid)], writes=[('w', b)])

    def wnext(self):
        idx = self.plan_pos
        while self.plan_issued < min(len(self.plan), idx + 3):
            self._issue_w(self.plan_issued)
            self.plan_issued += 1
        src, kc, ncol, sid = self.plan[idx]
        b = idx % 3
        self.plan_pos += 1
        return self.wbuf[:, b, 0:kc * ncol].rearrange("p (k n) -> p k n", k=kc), ('w', b)

    def prologue(self):
        S = self.S
        loads = [
            (self.vecd, self.d_vecd), (self.convw, self.d_convw), (self.convb, self.d_convb),
            (self.gatew, self.d_gatew), (self.hvec, self.d_hvec), (self.sinks, self.d_sinks),
            (self.cm, self.d_cm), (self.seqmask, self.d_seqmask), (self.amask, self.d_amask),
        ]
        for dst, src in loads:
            S.dma('sp', dst, src, writes=['const'])
        S.dma('sp', self.sinkrows[0:32], self.d_sinkrows, writes=['const'])
        S.dma('sp', self.amask_s[0:32], self.d_amask_s, writes=['const'])
        S.dma('pool', self.identb, self.d_identf, writes=['const'])
        S.dma('pool', self.onesb, self.d_onesf, writes=['const'])
        for layer in range(2):
            S.dma('pool', self.wdt[:, layer], self.d_w_in[layer, :, 6144:6176].rearrange("(k p) n -> p k n", p=128),
                  writes=['const'])
        self.act(AF.Exp, self.aneg, self.hvec[:, :, 1, :], ['const'], ['aneg'])
        self.ts('dve', self.aneg, self.aneg, -1.0, ALU.mult, ['aneg'], ['aneg'])
        self.memset('pool', self.hst, 0.0, [('hst', 0), ('hst', 1)])
        self.memset('pool', self.ctail, 0.0, ['ctail_all'])
        self.memset('pool', self.Vpad, 0.0, ['vpad_all'])
        self.memset('pool', self.Vpm, 0.0, ['vpm'])
        self.memset('pool', self.kT_all, 0.0, ['kT_all'])
        S.barrier()

    def rmsnorm(self, T, v, out_f32=None):
        ps, pk, _ = self.bank()
        for blk in range(8):
            i = blk % 2
            self.act(AF.Square, self.sq[:, i, :T], self.h[:, blk, :T], [('h', blk)], [('sq', i)])
            self.mm(ps[:, :T], self.onesb[:, :], self.sq[:, i, :T], blk == 0, blk == 7, [('sq', i)], [pk])
        self.act(AF.Sqrt, self.rstd[:, :T], ps[:, :T], [pk], ['rstd'], bias=EPS, scale=1.0 / D)
        self.recip(self.rstd[:, :T], self.rstd[:, :T], ['rstd'], ['rstd'])
        if out_f32 is None:
            for blk in range(8):
                self.stt('dve', self.xn[:, blk, :T], self.h[:, blk, :T], self.vecd[:, v, blk:blk + 1],
                         self.rstd[:, :T], ALU.mult, ALU.mult, [('h', blk), 'rstd'], [('xn', blk)])

    def mlp(self, T, layer):
        self.rmsnorm(T, 5 + layer)
        for half in range(2):
            for s in range(4):
                wv, wk = self.wnext()
                for j in range(4):
                    jb = 4 * s + j
                    ps, pk, _ = self.bank()
                    for k in range(8):
                        self.mm(ps[:, :T], wv[:, k, j * 128:(j + 1) * 128], self.xn[:, k, :T], k == 0, k == 7,
                                [wk, ('xn', k)], [pk])
                    i = jb % 2
                    self.act(AF.Relu, self.cacc[:, i, :T], ps[:, :T], [pk], [('cacc', i)])
                    self.tt('dve', self.big[:, jb, :T], self.cacc[:, i, :T], self.cacc[:, i, :T], ALU.mult,
                            [('cacc', i)], [('big', jb)])
            for s in range(4):
                wv, wk = self.wnext()
                for j in range(2):
                    db = 2 * s + j
                    ps, pk, _ = self.bank()
                    for kc in range(16):
                        self.mm(ps[:, :T], wv[:, kc, j * 128:(j + 1) * 128], self.big[:, kc, :T], kc == 0, kc == 15,
                                [wk, ('big', kc)], [pk])
                    self.tt('dve', self.h[:, db, :T], self.h[:, db, :T], ps[:, :T], ALU.add,
                            [pk, ('h', db)], [('h', db)])

    def zyv(self, c, blk, L):
        z4 = self.zy.rearrange("p c (b t) -> p c b t", b=16)
        return z4[:, c, blk, 0:L]

    def ssd_layer(self, T, layer, kind):
        L = min(T, 128)
        nch = (T + 127) // 128
        sample = (kind == 'sample')
        self.rmsnorm(T, layer)
        self.mark(f'{kind}.ssd{layer}.norm')
        for s in range(8):
            wv, wk = self.wnext()
            for j in range(4):
                blk = 4 * s + j
                ps, pk, _ = self.bank()
                for k in range(8):
                    self.mm(ps[:, :T], wv[:, k, j * 128:(j + 1) * 128], self.xn[:, k, :T], k == 0, k == 7,
                            [wk, ('xn', k)], [pk])
                i = blk % 2
                cw = self.convw[:, layer, blk, :]
                cb = self.convb[:, layer, blk:blk + 1]
                if not sample:
                    xp = self.xpre[:, i, :]
                    self.cp('pool', xp[:, 0:3], self.ctail[:, layer, blk, :], [('ctail', layer, blk), 'ctail_all'],
                            [('xpre', i)])
                    self.cp('act', xp[:, 3:3 + T], ps[:, :T], [pk], [('xpre', i)])
                    self.act(AF.Identity, self.cacc[:, i, :T], ps[:, :T], [pk], [('cacc', i)],
                             bias=cb, scale=cw[:, 3:4])
                    for k in range(3):
                        self.stt('dve', self.cacc[:, i, :T], xp[:, k:k + T], cw[:, k:k + 1], self.cacc[:, i, :T],
                                 ALU.mult, ALU.add, [('xpre', i), ('cacc', i)], [('cacc', i)])
                    self.act(AF.Silu, self.big[:, blk, :T], self.cacc[:, i, :T], [('cacc', i)], [('big', blk)])
                    self.cp('pool', self.ctail[:, layer, blk, :], xp[:, T:T + 3], [('xpre', i)],
                            [('ctail', layer, blk)])
                else:
                    xp3 = self.xpre[:, i, 0:176].rearrange("p (b k) -> p b k", b=16)
                    ps3 = ps[:, 0:128].rearrange("p (b t) -> p b t", b=16)
                    ca3 = self.cacc[:, i, 0:128].rearrange("p (b t) -> p b t", b=16)
                    self.S.dma('sp', xp3[:, :, 0:3],
                               self.d_sconvT[layer, blk * 128:(blk + 1) * 128, :].rearrange("p (b k) -> p b k", b=16),
                               reads=(), writes=[('xpre', i)])
                    self.cp('act', xp3[:, :, 3:11], ps3, [pk], [('xpre', i)])
                    self.act(AF.Identity, self.cacc[:, i, :128], ps[:, :128], [pk], [('cacc', i)],
                             bias=cb, scale=cw[:, 3:4])
                    for k in range(3):
                        self.stt('dve', ca3, xp3[:, :, k:k + 8], cw[:, k:k + 1], ca3,
                                 ALU.mult, ALU.add, [('xpre', i), ('cacc', i)], [('cacc', i)])
                    self.act(AF.Silu, self.big[:, blk, :128], self.cacc[:, i, :128], [('cacc', i)], [('big', blk)])
                    self.S.dma('sp', self.o_sconv[:, layer, blk, :, :], xp3[:, :, 8:11],
                               reads=[('xpre', i)], writes=[('o_sconv', layer, blk)])
        self.mark(f'{kind}.ssd{layer}.conv')
        for s in range(4):
            wv, wk = self.wnext()
            for ci in range(nch):
                ps, pk, _ = self.bank()
                for k in range(8):
                    self.mm(ps[:L, :512], self.xn[:, k, ci * 128:ci * 128 + L], wv[:, k, :], k == 0, k == 7,
                            [wk, ('xn', k)], [pk])
                self.act(AF.Silu, self.zy[:L, ci, s * 512:(s + 1) * 512], ps[:L, :512], [pk], [('zy', ci)])
        self.mark(f'{kind}.ssd{layer}.z')
        if not sample:
            self.cp('act', self.hbf[:, :], self.hst[:, layer, :], [('hst', layer)], ['hbf'])
        for ci in range(nch):
            self.ssd_chunk(layer, ci, L, sample)
            self.mark(f'{kind}.ssd{layer}.chunk{ci}')
        wo_plan = []
        for s in range(4):
            wv, wk = self.wnext()
            for j in range(2):
                db = 2 * s + j
                ps, pk, _ = self.bank()
                for c in range(nch):
                    for kc in range(16):
                        self.mm(ps[:, c * 128:c * 128 + L], wv[:, kc, j * 128:(j + 1) * 128], self.zyv(c, kc, L),
                                kc == 0, kc == 15, [wk, ('zy', c)], [pk])
                self.tt('dve', self.h[:, db, :T], self.h[:, db, :T], ps[:, :T], ALU.add, [pk, ('h', db)], [('h', db)])

    def ssd_chunk(self, layer, ci, L, sample):
        sm = self.sm
        c0 = ci * 128
        cols = slice(c0, c0 + L)
        TRI = self.cm[:, 3 if sample else 0, :]
        U = self.cm[:, 4 if sample else 1, :]
        ONES = self.cm[:, 5 if sample else 2, :]
        xnk = [('xn', k) for k in range(8)]
        ps, pk, _ = self.bank()
        for k in range(8):
            self.mm(ps[:L, 0:32], self.xn[:, k, cols], self.wdt[:, layer, k, :], k == 0, k == 7, [('xn', k)], [pk])
        self.tt('dve', sm['dtb'][:L], ps[:L, 0:32], self.hvec[:L, layer, 0, :], ALU.add, [pk], ['dtb'])
        self.act(AF.Exp, sm['e1'][:L], sm['dtb'][:L], ['dtb'], ['e1'])
        self.act(AF.Ln, sm['dt'][:L], sm['e1'][:L], ['e1'], ['dt'], bias=1.0)
        self.tt('dve', sm['dtA'][:L], sm['dt'][:L], self.aneg[:L, layer, :], ALU.mult, ['dt', 'aneg'], ['dtA'])
        self.mark(f'c{layer}.{ci}.dt')
        ps, pk, _ = self.bank()
        self.mm(ps[:L, 0:32], TRI[:L, :L], sm['dtA'][:L], True, True, ['dtA'], [pk])
        self.mm(ps[:, 32:64], ONES[:L, :], sm['dtA'][:L], True, True, ['dtA'], [pk])
        self.cp('act', sm['acs'][:L], ps[:L, 0:32], [pk], ['acs'])
        self.act(AF.Exp, sm['e'][:L], ps[:L, 0:32], [pk], ['e'])
        self.act(AF.Exp, sm['cd'][:, :], ps[:, 32:64], [pk], ['cd'])
        self.tt('dve', sm['dd'][:L], ps[:L, 32:64], sm['acs'][:L], ALU.subtract, [pk, 'acs'], ['dd'])
        self.act(AF.Exp, sm['dte'][:L], sm['dd'][:L], ['dd'], ['dte'])
        self.tt('dve', sm['dtw'][:L], sm['dt'][:L], sm['dte'][:L], ALU.mult, ['dt', 'dte'], ['dtw'])
        self.mark(f'c{layer}.{ci}.cum')
        for q in range(4):
            pt, ptk = self.pbank()
            for j in range(4):
                blk = 4 * q + j
                self.tr(pt[:L, j * 128:(j + 1) * 128], self.big[:, blk, cols], self.identb[:, :], [('big', blk)], [ptk])
            self.cp('act', self.x_tm[:L, q * 512:(q + 1) * 512], pt[:L, :512], [ptk], ['x_tm'])
            pt3 = pt[:L, :512].rearrange("p (h d) -> p h d", h=8)
            self.tt('dve', self.xdt[:L, q * 512:(q + 1) * 512].rearrange("p (h d) -> p h d", h=8), pt3,
                    sm['dt'][:L, 8 * q:8 * q + 8].unsqueeze(2).broadcast_to([L, 8, 64]), ALU.mult,
                    [ptk, 'dt'], ['xdt'])
            self.tt('dve', self.xw[:L, q * 512:(q + 1) * 512].rearrange("p (h d) -> p h d", h=8), pt3,
                    sm['dtw'][:L, 8 * q:8 * q + 8].unsqueeze(2).broadcast_to([L, 8, 64]), ALU.mult,
                    [ptk, 'dtw'], ['xw'])
        for q in range(2):
            pt, ptk = self.pbank()
            for j in range(4):
                blk = 16 + 4 * q + j
                self.tr(pt[:L, j * 128:(j + 1) * 128], self.big[:, blk, cols], self.identb[:, :], [('big', blk)], [ptk])
            self.cp('act', self.B_tm[:L, q * 512:(q + 1) * 512], pt[:L, :512], [ptk], ['B_tm'])
        self.mark(f'c{layer}.{ci}.tr')
        for half in range(2):
            ps, pk, _ = self.bank()
            for j in range(4):
                g = 4 * half + j
                self.mm(ps[:L, j * 128:j * 128 + L], self.big[:, 16 + g, cols], self.big[:, 24 + g, cols], True, True,
                        [('big', 16 + g), ('big', 24 + g)], [pk])
            self.tt('dve', self.Gm[:L, 4 * half:4 * half + 4, :L],
                    ps[:L, :].rearrange("p (j l) -> p j l", j=4)[:, :, :L],
                    TRI[:L, :L].unsqueeze(1).broadcast_to([L, 4, L]), ALU.mult, [pk], ['Gm'])
        held = []
        if sample:
            offb = []
            for q in range(4):
                ps, pk, bi = self.bank(hold=True)
                offb.append((ps, pk, bi))
            self.memset('pool', self.CTm[:, :, :], 0.0, ['CTm'])
            self.tt('dve', self.dtAm[:, :, :], sm['dtA'][:, :].unsqueeze(1).broadcast_to([128, 16, 32]),
                    self.seqmask[:, :].unsqueeze(2).broadcast_to([128, 16, 32]), ALU.mult, ['dtA'], ['dtAm'])
            ps, pk, _ = self.bank()
            self.mm(ps[:, :512], self.cm[:, 2, :], self.dtAm.rearrange("p b h -> p (b h)"), True, True, ['dtAm'], [pk])
            self.act(AF.Exp, self.cdall.rearrange("p b h -> p (b h)"), ps[:, :512], [pk], ['cdall'])
            for b in range(16):
                i = b % 2
                st = self.hst[:, i, :]
                self.S.dma('sp', st, self.d_sssmT[layer, b], reads=(), writes=[('hst', i)])
                self.cp('act', self.hbf[:, :], st, [('hst', i)], ['hbf'])
                bc8 = slice(b * 8, b * 8 + 8)
                self.cp('pool', self.CTm[:, :, bc8], self.big[:, 24:32, bc8],
                        [('big', 24 + g) for g in range(8)], ['CTm'])
                self.ts('pool', self.Bm.rearrange("p g n -> p (g n)"), self.B_tm[:, :],
                        self.seqmask[:, b:b + 1], ALU.mult, ['B_tm'], ['Bm'])
                for g in range(8):
                    ps, pk, _bi = offb[g // 2]
                    self.mm(ps[:, (g % 2) * 256:(g % 2) * 256 + 256], self.CTm[:, g, :],
                            self.hbf[:, g * 256:(g + 1) * 256], b == 0 and g % 2 == 0, b == 15, ['CTm', 'hbf'], [pk])
                self.memset('pool', self.CTm[:, :, bc8], 0.0, ['CTm'])
                for q in range(4):
                    ps, pk, _ = self.bank()
                    for gg in range(2):
                        g = 2 * q + gg
                        self.mm(ps[:, gg * 256:(gg + 1) * 256], self.Bm[:, g, :], self.xw[:, g * 256:(g + 1) * 256],
                                True, True, ['Bm', 'xw'], [pk])
                    stq = st[:, q * 512:(q + 1) * 512].rearrange("p (h d) -> p h d", h=8)
                    self.tt('dve', stq, stq,
                            self.cdall[:, b, 8 * q:8 * q + 8].unsqueeze(2).broadcast_to([128, 8, 64]), ALU.mult,
                            [('hst', i), 'cdall'], [('hst', i)])
                    self.tt('dve', st[:, q * 512:(q + 1) * 512], st[:, q * 512:(q + 1) * 512], ps[:, :512], ALU.add,
                            [pk, ('hst', i)], [('hst', i)])
                self.S.dma('sp', self.o_sssm[layer, b], st, reads=[('hst', i)], writes=[('o_sssm', layer, b)])
        self.mark(f'c{layer}.{ci}.G')
        def group_chain(g):
            i = g % 2
            self.tt('pool', self.Lg[:L, i, :, :L], U[:L, :L].unsqueeze(1).broadcast_to([L, 4, L]),
                    sm['dtA'][:L, 4 * g:4 * g + 4].unsqueeze(2).broadcast_to([L, 4, L]), ALU.mult,
                    ['dtA'], [('Lg', i)])
            yield
            ps, pk, psi = self.bank(hold=True)
            for j in range(4):
                self.mm(ps[:L, j * 128:j * 128 + L], self.Lg[:L, i, j, :L], TRI[:L, :L], True, True, [('Lg', i)], [pk])
            yield
            self.act(AF.Exp, self.ex[:L, i, :, :L], ps[:L, :].rearrange("p (j l) -> p j l", j=4)[:, :, :L],
                     [pk], [('ex', i)])
            self.release(psi)
            yield
            self.tt('dve', self.LT[:L, i, :, :L], self.ex[:L, i, :, :L],
                    self.Gm[:L, g, :L].unsqueeze(1).broadcast_to([L, 4, L]), ALU.mult,
                    [('ex', i), 'Gm'], [('LT', i)])
            yield
            psd, pdk, psdi = self.bank(hold=True)
            for j in range(4):
                hh = 4 * g + j
                self.mm(psd[:L, j * 64:(j + 1) * 64], self.LT[:L, i, j, :L], self.xdt[:L, hh * 64:(hh + 1) * 64],
                        True, True, [('LT', i), 'xdt'], [pdk])
            psoi = None
            if sample:
                pso, pok, _bi = offb[g // 2]
                off_ap = pso[:L, (g % 2) * 256:(g % 2) * 256 + 256]
            else:
                pso, pok, psoi = self.bank(hold=True)
                self.mm(pso[:L, 0:256], self.big[:, 24 + g, cols], self.hbf[:, g * 256:(g + 1) * 256], True, True,
                        [('big', 24 + g), 'hbf'], [pok])
                off_ap = pso[:L, 0:256]
            yield
            tg = self.t[:L, g * 256:(g + 1) * 256]
            self.cp('act', tg, psd[:L, 0:256], [pdk], [('t', g)])
            self.release(psdi)
            self.tt('pool', self.xd[:L, i, :].rearrange("p (h d) -> p h d", h=4),
                    self.x_tm[:L, g * 256:(g + 1) * 256].rearrange("p (h d) -> p h d", h=4),
                    self.hvec[:L, layer, 2, 4 * g:4 * g + 4].unsqueeze(2).broadcast_to([L, 4, 64]), ALU.mult,
                    ['x_tm'], [('xd', i)])
            yield
            for j in range(4):
                hh = 4 * g + j
                self.stt('dve', tg[:, j * 64:(j + 1) * 64], off_ap[:, j * 64:(j + 1) * 64], sm['e'][:L, hh:hh + 1],
                         tg[:, j * 64:(j + 1) * 64], ALU.mult, ALU.add, [pok, 'e', ('t', g)], [('t', g)])
            if psoi is not None:
                self.release(psoi)
            yield
            self.tt('pool', tg, tg, self.xd[:L, i, :], ALU.add, [('xd', i), ('t', g)], [('t', g)])
            self.tt('pool', tg, tg, self.zy[:L, ci, g * 256:(g + 1) * 256], ALU.mult, [('t', g), ('zy', ci)],
                    [('t', g)])
            yield
            self.act(AF.Square, self.sqj[:L, i, :], tg, [('t', g)], [('sqj', i), ('ss', g)],
                     accum_out=self.ss[:L, g:g + 1])

        self.interleave((group_chain(g) for g in range(8)), 1 if sample else 2)
        if sample:
            for (ps, pk, bi) in offb:
                self.release(bi)
        self.mark(f'c{layer}.{ci}.groups')
        tkeys = [('t', g) for g in range(8)]
        sskeys = [('ss', g) for g in range(8)]
        self.act(AF.Sqrt, self.rsg[:L, :], self.ss[:L, :], sskeys, ['rsg'], bias=EPS, scale=1.0 / 256)
        self.recip(self.rsg[:L, :], self.rsg[:L, :], ['rsg'], ['rsg'])
        self.tt('dve', self.gn[:L, :].rearrange("p (g c) -> p g c", g=8), self.t[:L, :].rearrange("p (g c) -> p g c", g=8),
                self.rsg[:L, :].unsqueeze(2).broadcast_to([L, 8, 256]), ALU.mult, tkeys + ['rsg'], ['xdt'])
        z4 = self.zy.rearrange("p c (b t) -> p c b t", b=16)
        for q in range(4):
            pt, ptk = self.pbank()
            for j in range(4):
                blk = 4 * q + j
                self.tr(pt[:, j * 128:j * 128 + L], self.gn[:L, blk * 128:(blk + 1) * 128], self.identb[:L, :L],
                        ['xdt'], [ptk])
            self.tt('dve', z4[:, ci, 4 * q:4 * q + 4, 0:L],
                    pt[:, :512].rearrange("p (j l) -> p j l", j=4)[:, :, :L],
                    self.gatew[:, layer, 4 * q:4 * q + 4].unsqueeze(2).broadcast_to([128, 4, L]), ALU.mult,
                    [ptk], [('zy', ci)])
        self.mark(f'c{layer}.{ci}.yn')
        if not sample:
            st = self.hst[:, layer, :]
            for q in range(4):
                ps, pk, _ = self.bank()
                for gg in range(2):
                    g = 2 * q + gg
                    self.mm(ps[:, gg * 256:(gg + 1) * 256], self.B_tm[:L, g * 128:(g + 1) * 128],
                            self.xw[:L, g * 256:(g + 1) * 256], True, True, ['B_tm', 'xw'], [pk])
                stq = st[:, q * 512:(q + 1) * 512].rearrange("p (h d) -> p h d", h=8)
                self.tt('dve', stq, stq, sm['cd'][:, 8 * q:8 * q + 8].unsqueeze(2).broadcast_to([128, 8, 64]),
                        ALU.mult, [('hst', layer), 'cd'], [('hst', layer)])
                self.tt('dve', st[:, q * 512:(q + 1) * 512], st[:, q * 512:(q + 1) * 512], ps[:, :512], ALU.add,
                        [pk, ('hst', layer)], [('hst', layer)])
            self.cp('act', self.hbf[:, :], st, [('hst', layer)], ['hbf'])

    def kv_proj(self, T, kind, last):
        L = min(T, 128)
        nch = (T + 127) // 128
        self.rmsnorm(T, 2)
        wv, wk = self.wnext()
        for kp in range(2):
            ps, pk, _ = self.bank()
            for k in range(8):
                self.mm(ps[:, :T], wv[:, k, kp * 128:(kp + 1) * 128], self.xn[:, k, :T], k == 0, k == 7,
                        [wk, ('xn', k)], [pk])
            if kind == 'meta':
                self.cp('act', self.kTm[:, kp, :T], ps[:, :T], [pk], ['kTm'])
            else:
                self.cp('act', self.kT_all[:, kp, 128:128 + T], ps[:, :T], [pk], ['kT_all'])
        for ci in range(nch):
            cols = slice(ci * 128, ci * 128 + L)
            ps, pk, _ = self.bank()
            for k in range(8):
                self.mm(ps[:L, 0:256], self.xn[:, k, cols], wv[:, k, 256:512], k == 0, k == 7, [wk, ('xn', k)], [pk])
            ps4 = ps[:L, 0:256].rearrange("p (k d) -> p k d", k=4)
            if kind == 'meta':
                self.cp('act', self.Vpm[:L, :, 0, 0:64], ps4, [pk], ['vpm'])
                self.cp('dve', self.Vpm[:L, :, 1, 64:128], ps4, [pk], ['vpm'])
            else:
                self.cp('act', self.Vpad[:L, ci + 1, :, 0, 0:64], ps4, [pk], [('vpad', ci + 1), 'vpad_all'])
                self.cp('dve', self.Vpad[:L, ci + 1, :, 1, 64:128], ps4, [pk], [('vpad', ci + 1), 'vpad_all'])
            want_out = (kind == 'sample') or (kind == 'prompt' and last and ci == nch - 1)
            if want_out:
                ps2, pk2, _ = self.bank()
                for k in range(8):
                    self.mm(ps2[:L, 0:256], self.xn[:, k, cols], wv[:, k, 0:256], k == 0, k == 7, [wk, ('xn', k)],
                            [pk2])
                self.cp('act', self.kvo[:, 0, 0:256], ps2[:, 0:256], [pk2], [('cacc', 0)])
                self.cp('dve', self.kvo[:, 1, 0:256], ps[:, 0:256], [pk], [('cacc', 1)])
                if kind == 'prompt':
                    self.S.dma('sp', self.o_pk, self.kvo[:, 0, 0:256], reads=[('cacc', 0)], writes=['o_pk'])
                    self.S.dma('sp', self.o_pv, self.kvo[:, 1, 0:256], reads=[('cacc', 1)], writes=['o_pv'])
                else:
                    for b in range(16):
                        self.S.dma('sp', self.o_sk[b, 120:128, :], self.kvo[b * 8:(b + 1) * 8, 0, 0:256],
                                   reads=[('cacc', 0)], writes=[('o_sk_new', b)])
                        self.S.dma('sp', self.o_sv[b, 120:128, :], self.kvo[b * 8:(b + 1) * 8, 1, 0:256],
                                   reads=[('cacc', 1)], writes=[('o_sv_new', b)])

    def q_proj(self, T, j):
        self.rmsnorm(T, 3 + j)
        for s in range(2):
            wv, wk = self.wnext()
            for jj in range(4):
                m = 4 * s + jj
                ps, pk, _ = self.bank()
                for k in range(8):
                    self.mm(ps[:, :T], wv[:, k, jj * 128:(jj + 1) * 128], self.xn[:, k, :T], k == 0, k == 7,
                            [wk, ('xn', k)], [pk])
                self.cp('act', self.qT[:, m, :T], ps[:, :T], [pk], [('qT', m)])

    def softmax_rows(self, M, N, i, sc, sck, maskap, sink_ap):
        a = self.att_small
        mx, negm, rsum, es, den, rden = (a[:M, i, c:c + 1] for c in range(6))
        ak = ('asm', i)
        self.stt('dve', self.s_sb[:M, i, :N], sc, SCALE, maskap, ALU.mult, ALU.add, [sck], [('s_sb', i)])
        yield
        nc = self.nc
        s_in = self.s_sb[:M, i, :N]
        self.S.op('dve', lambda: nc.vector.reduce_max(out=mx, in_=s_in, axis=AX.X), [('s_sb', i)], [ak])
        self.tt('dve', mx, mx, sink_ap, ALU.max, [ak], [ak])
        self.ts('dve', negm, mx, -1.0, ALU.mult, [ak], [ak])
        yield
        self.act(AF.Exp, self.pe_sb[:M, i, :N], self.s_sb[:M, i, :N], [('s_sb', i), ak], [('pe', i), ('rs', i)],
                 bias=negm, accum_out=rsum)
        self.act(AF.Exp, es, sink_ap, [ak], [('es', i)], bias=negm)
        yield
        self.tt('dve', den, rsum, es, ALU.add, [('rs', i), ('es', i)], [('den', i)])
        self.recip(rden, den, [('den', i)], [('den', i)])
        self.ts('dve', self.pn[:M, i, :N], self.pe_sb[:M, i, :N], rden, ALU.mult, [('pe', i), ('den', i)],
                [('pn', i)])
        yield

    def attn_prompt(self, T, j, first_tile):
        nch = T // 128
        z4 = self.zy.rearrange("p c (b t) -> p c b t", b=16)
        pvs = {}
        counter = [0]

        def head_chain(ci, m, half):
            cols = slice(ci * 128, ci * 128 + 128)
            which = 1 if (first_tile and ci == 0) else 0
            kp, r = m // 4, m % 4
            kvh = 2 * kp + half
            head = kvh * 4 + r
            it = counter[0]
            counter[0] += 1
            i = it % 2
            i4 = it % 4
            hs = slice(half * 64, half * 64 + 64)
            sc, sck, sci = self.bank(hold=True)
            q = self.qT[hs, m, cols]
            self.mm(sc[:, 0:16], q, self.kTm[hs, kp, 0:16], True, True, [('qT', m), 'kTm'], [sck])
            self.mm(sc[:, 16:272], q, self.kT_all[hs, kp, ci * 128:ci * 128 + 256], True, True,
                    [('qT', m), 'kT_all'], [sck])
            yield
            first = True
            for _ in self.softmax_rows(128, 272, i, sc[:, 0:272], sck, self.amask[:, which, :],
                                       self.sinks[:, j, head:head + 1]):
                if first:
                    self.release(sci)
                    first = False
                yield
            pt, ptk = self.pbank()
            self.tr(pt[0:16, 0:128], self.pn[:, i, 0:16], self.identb[:, :], [('pn', i)], [ptk])
            self.tr(pt[:, 128:256], self.pn[:, i, 16:144], self.identb[:, :], [('pn', i)], [ptk])
            self.tr(pt[:, 256:384], self.pn[:, i, 144:272], self.identb[:, :], [('pn', i)], [ptk])
            yield
            self.cp('act', self.pT[:, i4, :], pt[:, 0:384], [ptk], [('pT', i4)])
            yield
            if (ci, m) not in pvs:
                pv, pvk, pvi = self.bank(hold=True)
                pvs[(ci, m)] = [pv, pvk, 0, pvi]
            ent = pvs[(ci, m)]
            pv, pvk = ent[0], ent[1]
            segs = [(self.Vpm[0:16, kvh, half, :], self.pT[0:16, i4, 0:128], ['vpm']),
                    (self.Vpad[:, ci, kvh, half, :], self.pT[:, i4, 128:256], [('vpad', ci), 'vpad_all']),
                    (self.Vpad[:, ci + 1, kvh, half, :], self.pT[:, i4, 256:384], [('vpad', ci + 1), 'vpad_all'])]
            for lh, rh, keys in segs:
                self.mm(pv[:, 0:128], lh, rh, ent[2] == 0, ent[2] == 5, keys + [('pT', i4)], [pvk])
                ent[2] += 1
            if ent[2] == 6:
                yield
                self.cp('act', z4[:, ci, m, 0:128], pv[:, 0:128], [pvk], [('zy', ci)])
                self.release(ent[3])

        gens = (head_chain(ci, m, half) for ci in range(nch) for m in range(8) for half in range(2))
        self.interleave(gens, 2)

    def attn_sample(self, j):
        z4 = self.zy.rearrange("p c (b t) -> p c b t", b=16)
        it = 0
        for kp in range(2):
            self.cp('pool', self.qS[:, kp].rearrange("p b (r t) -> p b r t", r=4),
                    self.qT[:, kp * 4:kp * 4 + 4, 0:128].rearrange("p r (b t) -> p b r t", b=16),
                    [('qT', kp * 4 + r) for r in range(4)], ['qS'])
        for b in range(16):
            bc = slice(b * 8, b * 8 + 8)
            for kp in range(2):
                pv, pvk, _ = self.bank()
                used = []
                for half in range(2):
                    kvh = 2 * kp + half
                    i = it % 2
                    i4 = it % 4
                    it += 1
                    used.append(i4)
                    hs = slice(half * 64, half * 64 + 64)
                    sc, sck, _ = self.bank()
                    q = self.qS[hs, kp, b, :]
                    self.mm(sc[:32, 0:16], q, self.kTm[hs, kp, 0:16], True, True, ['qS', 'kTm'], [sck])
                    self.mm(sc[:32, 16:144], q, self.kTbuf[hs, b, kp, :], True, True, ['kTbuf'], [sck])
                    self.mm(sc[:32, 144:152], q, self.kT_all[hs, kp, 128 + b * 8:128 + b * 8 + 8], True, True,
                            ['kT_all'], [sck])
                    for _ in self.softmax_rows(32, 152, i, sc[:32, 0:152], sck, self.amask_s[0:32, :],
                                               self.sinkrows[0:32, j, kvh:kvh + 1]):
                        pass
                    self.cp('pool', self.pnpad[:32, i, bc], self.pn[:32, i, 144:152], [('pn', i)], [('pnpad', i)])
                    pt, ptk = self.pbank()
                    self.tr(pt[0:16, 0:32], self.pn[:32, i, 0:16], self.identb[:32, :32], [('pn', i)], [ptk])
                    self.tr(pt[:, 32:64], self.pn[:32, i, 16:144], self.identb[:32, :32], [('pn', i)], [ptk])
                    self.tr(pt[:, 64:96], self.pnpad[:32, i, :], self.identb[:32, :32], [('pnpad', i)], [ptk])
                    self.cp('act', self.pT[:, i4, 0:96], pt[:, 0:96], [ptk], [('pT', i4)])
                    self.memset('pool', self.pnpad[:32, i, bc], 0.0, [('pnpad', i)])
                for r in range(4):
                    rc = slice(r * 8, r * 8 + 8)
                    for half in range(2):
                        kvh = 2 * kp + half
                        i4 = used[half]
                        self.mm(pv[:, rc], self.Vpm[0:16, kvh, half, :], self.pT[0:16, i4, r * 8:r * 8 + 8],
                                half == 0, False, ['vpm', ('pT', i4)], [pvk])
                        self.mm(pv[:, rc], self.Vbuf[:, b, kvh, half, :], self.pT[:, i4, 32 + r * 8:32 + r * 8 + 8],
                                False, False, [('big', 2 * b), ('big', 2 * b + 1), ('pT', i4)], [pvk])
                        self.mm(pv[:, rc], self.Vpad[:, 1, kvh, half, :], self.pT[:, i4, 64 + r * 8:64 + r * 8 + 8],
                                False, half == 1, [('vpad', 1), 'vpad_all', ('pT', i4)], [pvk])
                self.cp('act', z4[:, 0, kp * 4:kp * 4 + 4, bc],
                        pv[:, 0:32].rearrange("p (r t) -> p r t", r=4), [pvk], [('zy', 0)])

    def attn_layer(self, T, j, kind, first_tile):
        nch = (T + 127) // 128
        self.q_proj(T, j)
        if kind == 'prompt':
            self.attn_prompt(T, j, first_tile)
        else:
            self.load_vbuf()
            self.attn_sample(j)
        for s in range(2):
            wv, wk = self.wnext()
            for jj in range(4):
                db = 4 * s + jj
                ps, pk, _ = self.bank()
                for c in range(nch):
                    for mc in range(8):
                        self.mm(ps[:, c * 128:(c + 1) * 128], wv[:, mc, jj * 128:(jj + 1) * 128],
                                self.zyv(c, mc, 128), mc == 0, mc == 7, [wk, ('zy', c)], [pk])
                self.tt('dve', self.h[:, db, :T], self.h[:, db, :T], ps[:, :T], ALU.add, [pk, ('h', db)], [('h', db)])

    def final_out(self, T, dst):
        self.rmsnorm(T, 9, out_f32=True)
        for blk in range(8):
            i = blk % 2
            self.stt('dve', self.cacc[:, i, :T], self.h[:, blk, :T], self.vecd[:, 9, blk:blk + 1],
                     self.rstd[:, :T], ALU.mult, ALU.mult, [('h', blk), 'rstd'], [('cacc', i)])
            self.S.dma('sp', dst[blk * 128:(blk + 1) * 128, :], self.cacc[:, i, :T], reads=[('cacc', i)],
                       writes=[('o_y', blk, id(dst))])

    def run_tile(self, kind, T, src, dst, first_tile=False, last=False):
        S = self.S
        S.dma('sp', self.h[:, :, :T], src.rearrange("(b p) t -> p b t", p=128), reads=(),
              writes=[('h', b) for b in range(8)])
        for layer in range(2):
            self.ssd_layer(T, layer, kind)
            self.mark(f'{kind}.ssd{layer}.out')
            self.mlp(T, layer)
            self.mark(f'{kind}.mlp{layer}')
        self.kv_proj(T, kind, last)
        self.mark(f'{kind}.kv')
        if kind == 'meta':
            return
        S.barrier()
        if kind == 'sample':
            self.load_sample_cache()
        for j in range(2):
            self.attn_layer(T, j, kind, first_tile)
            self.mark(f'{kind}.attn{j}')
            self.mlp(T, 2 + j)
            self.mark(f'{kind}.mlp{2 + j}')
        self.final_out(T, dst)
        self.mark(f'{kind}.final')
        if kind == 'prompt':
            nch = T // 128
            self.cp('pool', self.kT_all[:, :, 0:128], self.kT_all[:, :, T:T + 128], ['kT_all'], ['kT_all'])
            self.cp('pool', self.Vpad[:, 0], self.Vpad[:, nch], [('vpad', nch), 'vpad_all'], [('vpad', 0), 'vpad_all'])
        S.barrier()

    def load_sample_cache(self):
        S = self.S
        self.memset('pool', self.pnpad[:, :, :], 0.0, [('pnpad', 0), ('pnpad', 1)])
        for b in range(16):
            S.dma('pool', self.kTbuf[:, b, :, :], self.d_ckT[b].rearrange("(k p) w -> p k w", p=128), reads=(),
                  writes=['kTbuf'])
        S.dma('sp', self.o_sk[:, 0:120, :], self.d_ck[:, 8:128, :], reads=(), writes=['o_sk_old'])
        S.dma('sp', self.o_sv[:, 0:120, :], self.d_cv[:, 8:128, :], reads=(), writes=['o_sv_old'])

    def load_vbuf(self):
        S = self.S
        bigk = [('big', b) for b in range(32)]
        self.memset('pool', self.big[:, :, :], 0.0, bigk)
        for b in range(16):
            v4 = self.d_cv[b].rearrange("w (k d) -> w k d", k=4)
            S.dma('pool', self.Vbuf[:, b, :, 0, 0:64], v4, reads=(), writes=[('big', 2 * b), ('big', 2 * b + 1)])
            S.dma('pool', self.Vbuf[:, b, :, 1, 64:128], v4, reads=(), writes=[('big', 2 * b), ('big', 2 * b + 1)])

    def build(self):
        S = self.S
        kinds = ['meta'] + ['prompt'] * self.NT + (['sample'] if self.do_sample else [])
        self.make_plan(kinds)
        try:
            self._build_body()
        except StopBuild:
            pass
        S.barrier()
        S.emit()
        return self.nc

    def _build_body(self):
        S = self.S
        self.prologue()
        self.mark('prologue')
        self.run_tile('meta', 16, self.d_metaT, None)
        S.barrier()
        for ti in range(self.NT):
            self.run_tile('prompt', 512, self.d_xT[:, ti * 512:(ti + 1) * 512],
                          self.o_ypT[:, ti * 512:(ti + 1) * 512], first_tile=(ti == 0), last=(ti == self.NT - 1))
        S.dma('sp', self.o_pconv, self.ctail, reads=[('ctail', l, b) for l in range(2) for b in range(32)],
              writes=['o_pconv'])
        for layer in range(2):
            S.dma('sp', self.o_pssm[layer], self.hst[:, layer, :], reads=[('hst', layer)], writes=[('o_pssm', layer)])
        S.barrier()
        if self.do_sample:
            self.run_tile('sample', 128, self.d_xsT, self.o_ysT)


def _consts():
    k = np.arange(128)
    seq = k // 8
    tri = (k[:, None] <= k[None, :]).astype(np.float32)
    U = (k[:, None] > k[None, :]).astype(np.float32)
    ones = np.ones((128, 128), np.float32)
    same = (seq[:, None] == seq[None, :]).astype(np.float32)
    cm = np.stack([tri, U, ones, tri * same, U * same, same], axis=1)
    seqmask = (seq[:, None] == np.arange(16)[None, :]).astype(np.float32)
    qi = k[:, None]
    cj = k[None, :]
    prev = np.where(cj > qi, 0.0, NEG).astype(np.float32)
    cur = np.where(cj <= qi, 0.0, NEG).astype(np.float32)
    meta0 = np.zeros((128, 16), np.float32)
    am0 = np.concatenate([meta0, prev, cur], axis=1)
    am1 = np.concatenate([meta0, np.full((128, 128), NEG, np.float32), cur], axis=1)
    amask = np.stack([am0, am1], axis=1)
    t = np.tile(np.arange(8), 4)[:, None]
    bufm = np.where(np.arange(128)[None, :] > t, 0.0, NEG).astype(np.float32)
    newm = np.where(np.arange(8)[None, :] <= t, 0.0, NEG).astype(np.float32)
    amask_s = np.concatenate([np.zeros((32, 16), np.float32), bufm, newm], axis=1)
    return dict(cm=cm, seqmask=seqmask, amask=amask, amask_s=amask_s,
                identf=np.eye(128, dtype=np.float32), onesf=ones)


def _pvec(v):
    return np.ascontiguousarray(np.asarray(v, np.float32).reshape(8, 128).T)


def prepare_inputs(inp, NT):
    f = lambda a: np.ascontiguousarray(np.asarray(a, dtype=np.float32))
    NTOK = NT * 512
    shared = _consts()
    vecs = [inp['a_norm_w'][0], inp['a_norm_w'][1], inp['kv_norm_w'], inp['b_norm_w'][0], inp['b_norm_w'][1],
            inp['mlp_norm_w'][0], inp['mlp_norm_w'][1], inp['mlp_norm_w'][2], inp['mlp_norm_w'][3],
            inp['final_norm_w']]
    shared['vecd'] = f(np.stack([_pvec(v) for v in vecs], axis=1))
    cw = np.asarray(inp['a_conv_w'], np.float32)
    shared['convw'] = f(cw.reshape(2, 4, 32, 128).transpose(3, 0, 2, 1))
    shared['convb'] = f(np.asarray(inp['a_conv_b'], np.float32).reshape(2, 32, 128).transpose(2, 0, 1))
    shared['gatew'] = f(np.asarray(inp['a_gate_norm_w'], np.float32).reshape(2, 16, 128).transpose(2, 0, 1))
    hv = np.stack([inp['a_dt_bias'], inp['a_log'], inp['a_d_skip']], axis=1)
    shared['hvec'] = f(np.broadcast_to(np.asarray(hv, np.float32)[None], (128, 2, 3, 32)))
    sk = np.asarray(inp['attn_sinks'], np.float32)
    shared['sinks'] = f(np.broadcast_to(sk[None], (128, 2, 16)))
    sr = sk.reshape(2, 4, 4)
    shared['sinkrows'] = f(np.repeat(sr.transpose(2, 0, 1), 8, axis=0))
    shared['metaT'] = f(np.asarray(inp['meta_tokens'], np.float32).T)
    shared['w_in'] = f(inp['a_in_proj'])
    shared['w_out'] = f(inp['a_out_proj'])
    shared['w_kv'] = f(inp['w_kv'])
    wq = np.asarray(inp['w_q'], np.float32).reshape(2, D, 2, 2, 4, 64)
    shared['w_q'] = f(wq.transpose(0, 1, 2, 4, 3, 5).reshape(2, D, D))
    wo = np.asarray(inp['w_o'], np.float32).reshape(2, 2, 2, 4, 64, D)
    shared['w_o'] = f(wo.transpose(0, 1, 3, 2, 4, 5).reshape(2, D, D))
    shared['w_up'] = f(inp['w_up'])
    shared['w_down'] = f(inp['w_down'])
    xp = np.asarray(inp['x_prompt'], np.float32)
    xs = np.asarray(inp['x_sample'], np.float32)
    sc = np.asarray(inp['state_conv'], np.float32)
    ss = np.asarray(inp['state_ssm'], np.float32)
    ck = np.asarray(inp['cache_k_win'], np.float32)
    cv = np.asarray(inp['cache_v_win'], np.float32)
    maps = []
    for c in range(N_CORES):
        m = dict(shared)
        m['xT'] = f(xp[c, :NTOK].T) if c < 2 else np.zeros((D, NTOK), np.float32)
        sl = slice(c * 16, (c + 1) * 16)
        m['xsT'] = f(xs[sl].reshape(128, D).T)
        m['sconvT'] = f(sc[:, sl].transpose(0, 3, 1, 2).reshape(2, CONV_DIM, 48))
        m['sssmT'] = f(ss[:, sl].reshape(2, 16, DI, 128).transpose(0, 1, 3, 2))
        m['ck'] = f(ck[sl].reshape(16, 128, 256))
        m['cv'] = f(cv[sl].reshape(16, 128, 256))
        m['ckT'] = f(ck[sl].reshape(16, 128, 256).transpose(0, 2, 1))
        maps.append(m)
    return maps


_PROGRAMS = {}


def get_program(NT, do_sample=True, stop_at=None):
    key = (NT, do_sample, stop_at)
    if key not in _PROGRAMS:
        _PROGRAMS[key] = Builder(NT, do_sample, stop_at).build()
    return _PROGRAMS[key]


def assemble(results, NT):
    NTOK = NT * 512
    y_prompt = np.stack([results[b]['ypT'].T for b in range(2)], axis=0)
    y_sample = np.concatenate([results[c]['ysT'].T.reshape(16, 8, D) for c in range(N_CORES)], axis=0)
    pconv = np.stack([results[b]['pconv'].transpose(1, 3, 2, 0).reshape(2, 3, CONV_DIM) for b in range(2)], axis=1)
    pssm = np.stack([results[b]['pssm'].transpose(0, 2, 1).reshape(2, NH, 64, 128) for b in range(2)], axis=1)
    pk = np.stack([results[b]['pk'].reshape(128, 4, 64) for b in range(2)], axis=0)
    pv = np.stack([results[b]['pv'].reshape(128, 4, 64) for b in range(2)], axis=0)
    sconv = np.concatenate(
        [results[c]['sconv_o'].transpose(1, 3, 4, 2, 0).reshape(2, 16, 3, CONV_DIM) for c in range(N_CORES)], axis=1)
    sssm = np.concatenate(
        [results[c]['sssm_o'].transpose(0, 1, 3, 2).reshape(2, 16, NH, 64, 128) for c in range(N_CORES)], axis=1)
    sk = np.concatenate([results[c]['sk_o'].reshape(16, 128, 4, 64) for c in range(N_CORES)], axis=0)
    sv = np.concatenate([results[c]['sv_o'].reshape(16, 128, 4, 64) for c in range(N_CORES)], axis=0)
    outs = (y_prompt, y_sample, pconv, pssm, pk, pv, sconv, sssm, sk, sv)
    return tuple(np.ascontiguousarray(o, dtype=np.float32) for o in outs)


def run(inputs, NT, do_sample=True, stop_at=None):
    nc = get_program(NT, do_sample, stop_at)
    maps = prepare_inputs(inputs, NT)
    res = run_bass_kernel_spmd(nc, maps, core_ids=list(range(N_CORES)))
    return assemble(res.results, NT)


def kernel(**inputs):
    return run(inputs, SEQ // 512)
```

```python
import numpy as np
import concourse.bass as bass
import concourse.mybir as mybir
from concourse.bass_utils import run_bass_kernel_spmd

F32 = mybir.dt.float32
BF16 = mybir.dt.bfloat16
AF = mybir.ActivationFunctionType
ALU = mybir.AluOpType
AX = mybir.AxisListType

D = 1024
DI = 2048
NH = 32
NG = 8
CONV_DIM = 4096
INP = 6176
DFF = 4096
EPS = 1e-5
NEG = -30000.0
SCALE = 64 ** -0.5
N_CORES = 8
SEQ = 8192


class Sched:
    def __init__(self, nc, n_slots=10):
        self.nc = nc
        self.engs = {'pe': nc.tensor, 'act': nc.scalar, 'dve': nc.vector,
                     'pool': nc.gpsimd, 'sp': nc.sync}
        self.ops = {e: [] for e in self.engs}
        self.sem = {e: nc.alloc_semaphore(name=f"sem_{e}") for e in self.engs}
        self.cnt = {e: 0 for e in self.engs}
        self.waited = {e: {} for e in self.engs}
        self.semobj = {}
        self.latest = {}
        for e in self.engs:
            self.semobj[('c', e)] = self.sem[e]
        self.slots = {}
        for q in ('sp', 'pool'):
            self.slots[q] = []
            for i in range(n_slots):
                s = nc.alloc_semaphore(name=f"dq_{q}_{i}")
                key = ('d', q, i)
                self.semobj[key] = s
                self.slots[q].append([key, 0])
        self.slot_rr = {q: 0 for q in self.slots}
        self.res = {}
        self.n_waits = 0

    def _deps(self, reads, writes):
        deps = []
        for r in reads:
            st = self.res.get(r)
            if st and st[0] is not None:
                deps.append(st[0])
        for w in writes:
            st = self.res.get(w)
            if st:
                if st[0] is not None:
                    deps.append(st[0])
                deps.extend(st[1])
        return deps

    def _emit_waits(self, eng, deps, skip_same_pe=True):
        wd = self.waited[eng]
        need = {}
        for (key, val) in deps:
            if skip_same_pe and eng == 'pe' and key == ('c', 'pe'):
                continue
            if wd.get(key, 0) >= val:
                continue
            if need.get(key, 0) < val:
                need[key] = val
        for key, val in need.items():
            wd[key] = val
            sem = self.semobj[key]
            e = self.engs[eng]
            self.ops[eng].append(lambda e=e, sem=sem, val=val: e.wait_ge(sem, val))
            self.n_waits += 1

    def _commit(self, tok, reads, writes):
        self.latest[tok[0]] = max(self.latest.get(tok[0], 0), tok[1])
        for r in reads:
            st = self.res.setdefault(r, [None, []])
            st[1].append(tok)
            if len(st[1]) > 24:
                best = {}
                for k, v in st[1]:
                    if best.get(k, 0) < v:
                        best[k] = v
                st[1] = list(best.items())
        for w in writes:
            self.res[w] = [tok, []]

    @staticmethod
    def _excl(reads, writes):
        ps = [r for r in reads if isinstance(r, tuple) and r and r[0] in ('pf', 'pb')]
        if not ps:
            return reads, writes
        return [r for r in reads if r not in ps], list(writes) + ps

    def op(self, eng, fn, reads=(), writes=()):
        reads, writes = self._excl(reads, writes)
        deps = self._deps(reads, writes)
        self._emit_waits(eng, deps)
        self.cnt[eng] += 1
        tok = (('c', eng), self.cnt[eng])
        sem = self.sem[eng]
        self.ops[eng].append(lambda fn=fn, sem=sem: fn().then_inc(sem, 1))
        self._commit(tok, reads, writes)
        return tok

    def dma(self, q, out, in_, reads=(), writes=()):
        deps = self._deps(reads, writes)
        i = self.slot_rr[q]
        self.slot_rr[q] = (i + 1) % len(self.slots[q])
        slot = self.slots[q][i]
        key = slot[0]
        if slot[1] > 0:
            deps.append((key, slot[1]))
        self._emit_waits(q, deps, skip_same_pe=False)
        slot[1] += 16
        tok = (key, slot[1])
        sem = self.semobj[key]
        e = self.engs[q]
        self.ops[q].append(lambda e=e, out=out, in_=in_, sem=sem: e.dma_start(out=out, in_=in_).then_inc(sem, 16))
        self._commit(tok, reads, writes)
        return tok

    def barrier(self):
        toks = list(self.latest.items())
        for e in self.engs:
            self._emit_waits(e, toks, skip_same_pe=False)
        self.res = {}

    def emit(self):
        nc = self.nc
        with nc.Block() as block:
            @block.tensor
            def _(e):
                for f in self.ops['pe']:
                    f()

            @block.scalar
            def _(e):
                for f in self.ops['act']:
                    f()

            @block.vector
            def _(e):
                for f in self.ops['dve']:
                    f()

            @block.gpsimd
            def _(e):
                for f in self.ops['pool']:
                    f()

            @block.sync
            def _(e):
                for f in self.ops['sp']:
                    f()


class Carver:
    def __init__(self, region_f32):
        self.r = region_f32
        self.n = region_f32.shape[1]
        self.pos = 0

    def take(self, dtype, *free):
        n = 1
        for f in free:
            n *= f
        nf32 = n if dtype == F32 else (n + 1) // 2
        assert self.pos + nf32 <= self.n, ("arena overflow", self.pos, nf32, self.n)
        v = self.r[:, self.pos:self.pos + nf32]
        self.pos += nf32
        if dtype != F32:
            v = v.bitcast(dtype)[:, 0:n]
        if len(free) == 2:
            v = v.rearrange("p (a b) -> p a b", a=free[0])
        elif len(free) == 3:
            v = v.rearrange("p (a b c) -> p a b c", a=free[0], b=free[1])
        elif len(free) == 4:
            v = v.rearrange("p (a b c d) -> p a b c d", a=free[0], b=free[1], c=free[2])
        return v


class StopBuild(Exception):
    pass


class Builder:
    def __init__(self, NT, do_sample=True, stop_at=None):
        self.NT = NT
        self.stop_at = stop_at
        self.marks = []
        self.do_sample = do_sample
        self.NTOK = NT * 512
        nc = self.nc = bass.Bass("TRN2", target_bir_lowering=False)
        self.S = Sched(nc)
        self._declare_dram()
        self._alloc()

    def _declare_dram(self):
        nc = self.nc

        def din(name, shape):
            return nc.dram_tensor(name, list(shape), F32, kind="ExternalInput").ap()

        def dout(name, shape):
            return nc.dram_tensor(name, list(shape), F32, kind="ExternalOutput").ap()

        NTOK = self.NTOK
        self.d_xT = din("xT", (D, NTOK))
        self.d_xsT = din("xsT", (D, 128))
        self.d_metaT = din("metaT", (D, 16))
        self.d_sconvT = din("sconvT", (2, CONV_DIM, 48))
        self.d_sssmT = din("sssmT", (2, 16, 128, DI))
        self.d_ck = din("ck", (16, 128, 256))
        self.d_cv = din("cv", (16, 128, 256))
        self.d_ckT = din("ckT", (16, 256, 128))
        self.d_w_in = din("w_in", (2, D, INP))
        self.d_w_out = din("w_out", (2, DI, D))
        self.d_w_kv = din("w_kv", (D, 512))
        self.d_w_q = din("w_q", (2, D, D))
        self.d_w_o = din("w_o", (2, D, D))
        self.d_w_up = din("w_up", (4, D, DFF))
        self.d_w_down = din("w_down", (4, DFF, D))
        self.d_vecd = din("vecd", (128, 10, 8))
        self.d_convw = din("convw", (128, 2, 32, 4))
        self.d_convb = din("convb", (128, 2, 32))
        self.d_gatew = din("gatew", (128, 2, 16))
        self.d_hvec = din("hvec", (128, 2, 3, 32))
        self.d_sinks = din("sinks", (128, 2, 16))
        self.d_sinkrows = din("sinkrows", (32, 2, 4))
        self.d_cm = din("cm", (128, 6, 128))
        self.d_identf = din("identf", (128, 128))
        self.d_onesf = din("onesf", (128, 128))
        self.d_seqmask = din("seqmask", (128, 16))
        self.d_amask = din("amask", (128, 2, 272))
        self.d_amask_s = din("amask_s", (32, 152))

        self.o_ypT = dout("ypT", (D, NTOK))
        self.o_ysT = dout("ysT", (D, 128))
        self.o_pconv = dout("pconv", (128, 2, 32, 3))
        self.o_pssm = dout("pssm", (2, 128, DI))
        self.o_pk = dout("pk", (128, 256))
        self.o_pv = dout("pv", (128, 256))
        self.o_sconv = dout("sconv_o", (128, 2, 32, 16, 3))
        self.o_sssm = dout("sssm_o", (2, 16, 128, DI))
        self.o_sk = dout("sk_o", (16, 128, 256))
        self.o_sv = dout("sv_o", (16, 128, 256))

    def _alloc(self):
        nc = self.nc
        total_f32 = 52100
        region = nc.alloc_sbuf_tensor("arena", [128, total_f32], F32)
        C = Carver(region[:, :])
        self.h = C.take(F32, 8, 512)
        self.xn = C.take(BF16, 8, 512)
        self.sq = C.take(BF16, 2, 512)
        self.rstd = C.take(F32, 512)
        self.big = C.take(BF16, 32, 512)
        self.ctail = C.take(F32, 2, 32, 3)
        self.zy = C.take(BF16, 4, 2048)
        self.hst = C.take(F32, 2, 2048)
        self.hbf = C.take(BF16, 2048)
        self.wbuf = C.take(BF16, 3, 4096)
        self.cacc = C.take(F32, 2, 512)
        self.kT_all = C.take(BF16, 2, 640)
        self.Vpad = C.take(BF16, 5, 4, 2, 128)
        self.kTm = C.take(BF16, 2, 16)
        self.Vpm = C.take(BF16, 4, 2, 128)
        self.vecd = C.take(F32, 10, 8)
        self.convw = C.take(F32, 2, 32, 4)
        self.convb = C.take(F32, 2, 32)
        self.gatew = C.take(F32, 2, 16)
        self.hvec = C.take(F32, 2, 3, 32)
        self.aneg = C.take(F32, 2, 32)
        self.sinks = C.take(F32, 2, 16)
        self.sinkrows = C.take(F32, 2, 4)
        self.cm = C.take(F32, 6, 128)
        self.identb = C.take(BF16, 128)
        self.onesb = C.take(BF16, 128)
        self.seqmask = C.take(F32, 16)
        self.amask = C.take(F32, 2, 272)
        self.amask_s = C.take(F32, 152)
        self.wdt = C.take(BF16, 2, 8, 32)
        self.sm = {}
        for nm in ('dtb', 'e1', 'dt', 'dtA', 'acs', 'e', 'dd', 'dte', 'cd', 'dtw'):
            self.sm[nm] = C.take(F32, 32)
        self.ss = C.take(F32, 8)
        self.rsg = C.take(F32, 8)
        self.att_small = C.take(F32, 4, 8)
        persistent_end = C.pos
        rest = region[:, persistent_end:total_f32]
        A = Carver(rest)
        self.xpre = A.take(F32, 2, 515)
        self.x_tm = A.take(BF16, 2048)
        self.xdt = A.take(BF16, 2048)
        self.xw = A.take(BF16, 2048)
        self.B_tm = A.take(BF16, 1024)
        self.Gm = A.take(BF16, 8, 128)
        self.Lg = A.take(F32, 3, 4, 128)
        self.ex = A.take(BF16, 3, 4, 128)
        self.LT = A.take(BF16, 3, 4, 128)
        self.t = A.take(F32, 2048)
        self.sqj = A.take(BF16, 3, 256)
        self.xd = A.take(F32, 3, 256)
        self.CTm = A.take(BF16, 8, 128)
        self.Bm = A.take(BF16, 8, 128)
        self.dtAm = A.take(F32, 16, 32)
        self.cdall = A.take(F32, 16, 32)
        ssd_end = A.pos
        self.gn = self.xdt
        self.kvo = self.cacc
        B = Carver(rest)
        self.qT = B.take(BF16, 8, 512)
        self.s_sb = B.take(F32, 4, 272)
        self.pe_sb = B.take(F32, 4, 272)
        self.pn = B.take(BF16, 4, 272)
        self.pT = B.take(BF16, 8, 384)
        self.pnpad = B.take(BF16, 2, 128)
        self.kTbuf = B.take(BF16, 16, 2, 128)
        self.qS = B.take(BF16, 2, 16, 32)
        att_end = B.pos
        self.sbuf_used_f32 = persistent_end + max(ssd_end, att_end)
        self.Vbuf = self.big.rearrange("p (b x) t -> p b (x t)", b=16).rearrange(
            "p b (k h d) -> p b k h d", k=4, h=2)
        self.pf = [nc.alloc_psum_tensor(f"pf{i}", [128, 512], F32) for i in range(6)]
        self.pbt = [nc.alloc_psum_tensor(f"pb{i}", [128, 1024], BF16) for i in range(2)]
        self.pf_rr = 0
        self.pb_rr = 0
        self.held = set()
        self.w_rr = 0

    def interleave(self, gens, W):
        from collections import deque
        active = deque()
        it = iter(gens)
        while True:
            while len(active) < W:
                try:
                    active.append(next(it))
                except StopIteration:
                    break
            if not active:
                break
            for _ in range(len(active)):
                g = active.popleft()
                try:
                    next(g)
                    active.append(g)
                except StopIteration:
                    pass

    def mark(self, name):
        self.marks.append(name)
        if self.stop_at is not None and name == self.stop_at:
            raise StopBuild()

    def bank(self, hold=False):
        assert len(self.held) < 6, "all PSUM banks held"
        while True:
            i = self.pf_rr
            self.pf_rr = (self.pf_rr + 1) % 6
            if i not in self.held:
                break
        if hold:
            self.held.add(i)
        return self.pf[i], ('pf', i), i

    def release(self, i):
        self.held.discard(i)

    def pbank(self):
        i = self.pb_rr
        self.pb_rr = 1 - i
        return self.pbt[i][:, 0:512], ('pb', i)

    def mm(self, out, lhsT, rhs, start, stop, rd, wr):
        nc = self.nc
        self.S.op('pe', lambda: nc.tensor.matmul(out, lhsT=lhsT, rhs=rhs, start=start, stop=stop), rd, wr)

    def tr(self, out, in_, ident, rd, wr):
        nc = self.nc
        self.S.op('pe', lambda: nc.tensor.transpose(out, in_, ident), rd, wr)

    def act(self, func, out, in_, rd, wr, bias=None, scale=None, accum_out=None):
        nc = self.nc
        kw = {}
        if bias is not None:
            kw['bias'] = bias
        if scale is not None:
            kw['scale'] = scale
        if accum_out is not None:
            kw['accum_out'] = accum_out
        self.S.op('act', lambda: nc.scalar.activation(out=out, in_=in_, func=func, **kw), rd, wr)

    def tt(self, eng, out, in0, in1, op, rd, wr):
        E = self.S.engs[eng]
        self.S.op(eng, lambda: E.tensor_tensor(out=out, in0=in0, in1=in1, op=op), rd, wr)

    def ts(self, eng, out, in0, s1, op0, rd, wr, s2=None, op1=None):
        E = self.S.engs[eng]
        if op1 is None:
            self.S.op(eng, lambda: E.tensor_scalar(out=out, in0=in0, scalar1=s1, scalar2=None, op0=op0), rd, wr)
        else:
            self.S.op(eng, lambda: E.tensor_scalar(out=out, in0=in0, scalar1=s1, scalar2=s2, op0=op0, op1=op1),
                      rd, wr)

    def stt(self, eng, out, in0, scalar, in1, op0, op1, rd, wr):
        E = self.S.engs[eng]
        self.S.op(eng, lambda: E.scalar_tensor_tensor(out=out, in0=in0, scalar=scalar, in1=in1, op0=op0, op1=op1),
                  rd, wr)

    def cp(self, eng, out, in_, rd, wr):
        if eng == 'act':
            nc = self.nc
            self.S.op('act', lambda: nc.scalar.copy(out=out, in_=in_), rd, wr)
        else:
            E = self.S.engs[eng]
            self.S.op(eng, lambda: E.tensor_copy(out=out, in_=in_), rd, wr)

    def memset(self, eng, ap, val, wr):
        E = self.S.engs[eng]
        self.S.op(eng, lambda: E.memset(ap, val), (), wr)

    def recip(self, out, in_, rd, wr):
        nc = self.nc
        self.S.op('dve', lambda: nc.vector.reciprocal(out=out, in_=in_), rd, wr)

    def make_plan(self, kinds):
        plan = []
        sids = {}

        def add(key, src, kc, ncol):
            if key not in sids:
                sids[key] = len(sids)
            plan.append((src, kc, ncol, sids[key]))

        def layer_mlp(layer):
            for half in range(2):
                for s in range(4):
                    add(('up', layer, half, s),
                        self.d_w_up[layer, :, half * 2048 + s * 512: half * 2048 + (s + 1) * 512], 8, 512)
                for s in range(4):
                    add(('down', layer, half, s),
                        self.d_w_down[layer, half * 2048:(half + 1) * 2048, s * 256:(s + 1) * 256], 16, 256)

        for kind in kinds:
            for layer in range(2):
                for s in range(8):
                    add(('xbc', layer, s), self.d_w_in[layer, :, 2048 + s * 512: 2048 + (s + 1) * 512], 8, 512)
                for s in range(4):
                    add(('z', layer, s), self.d_w_in[layer, :, s * 512:(s + 1) * 512], 8, 512)
                for s in range(4):
                    add(('out', layer, s), self.d_w_out[layer, :, s * 256:(s + 1) * 256], 16, 256)
                layer_mlp(layer)
            add(('kv',), self.d_w_kv[:, :], 8, 512)
            if kind != 'meta':
                for j in range(2):
                    for s in range(2):
                        add(('q', j, s), self.d_w_q[j, :, s * 512:(s + 1) * 512], 8, 512)
                    for s in range(2):
                        add(('o', j, s), self.d_w_o[j, :, s * 512:(s + 1) * 512], 8, 512)
                    layer_mlp(2 + j)
        self.plan = plan
        self.plan_issued = 0
        self.plan_pos = 0
        self.sid_done = set()
        self.wsc = self.nc.dram_tensor("wsc", [len(sids), 128, 4096], BF16).ap()

    def _issue_w(self, idx):
        src, kc, ncol, sid = self.plan[idx]
        b = idx % 3
        if sid not in self.sid_done:
            dst = self.wbuf[:, b, 0:kc * ncol].rearrange("p (k n) -> p k n", k=kc)
            self.S.dma('pool', dst, src.rearrange("(k p) n -> p k n", p=128), reads=(), writes=[('w', b)])
            self.S.dma('sp', self.wsc[sid], self.wbuf[:, b, :], reads=[('w', b)], writes=[('wsc', sid)])
            self.sid_done.add(sid)
        else:
            self.S.dma('pool', self.wbuf[:, b, :], self.wsc[sid], reads=[('wsc', sid)], writes=[('w', b)])

    def wnext(self):
        idx = self.plan_pos
        while self.plan_issued < min(len(self.plan), idx + 3):
            self._issue_w(self.plan_issued)
            self.plan_issued += 1
        src, kc, ncol, sid = self.plan[idx]
        b = idx % 3
        self.plan_pos += 1
        return self.wbuf[:, b, 0:kc * ncol].rearrange("p (k n) -> p k n", k=kc), ('w', b)

    def prologue(self):
        S = self.S
        loads = [
            (self.vecd, self.d_vecd), (self.convw, self.d_convw), (self.convb, self.d_convb),
            (self.gatew, self.d_gatew), (self.hvec, self.d_hvec), (self.sinks, self.d_sinks),
            (self.cm, self.d_cm), (self.seqmask, self.d_seqmask), (self.amask, self.d_amask),
        ]
        for dst, src in loads:
            S.dma('sp', dst, src, writes=['const'])
        S.dma('sp', self.sinkrows[0:32], self.d_sinkrows, writes=['const'])
        S.dma('sp', self.amask_s[0:32], self.d_amask_s, writes=['const'])
        S.dma('pool', self.identb, self.d_identf, writes=['const'])
        S.dma('pool', self.onesb, self.d_onesf, writes=['const'])
        for layer in range(2):
            S.dma('pool', self.wdt[:, layer], self.d_w_in[layer, :, 6144:6176].rearrange("(k p) n -> p k n", p=128),
                  writes=['const'])
        self.act(AF.Exp, self.aneg, self.hvec[:, :, 1, :], ['const'], ['aneg'])
        self.ts('dve', self.aneg, self.aneg, -1.0, ALU.mult, ['aneg'], ['aneg'])
        self.memset('pool', self.hst, 0.0, [('hst', 0), ('hst', 1)])
        self.memset('pool', self.ctail, 0.0, ['ctail_all'])
        self.memset('pool', self.Vpad, 0.0, ['vpad_all'])
        self.memset('pool', self.Vpm, 0.0, ['vpm'])
        self.memset('pool', self.kT_all, 0.0, ['kT_all'])
        S.barrier()

    def rmsnorm(self, T, v, out_f32=None):
        ps, pk, _ = self.bank()
        for blk in range(8):
            i = blk % 2
            self.act(AF.Square, self.sq[:, i, :T], self.h[:, blk, :T], [('h', blk)], [('sq', i)])
            self.mm(ps[:, :T], self.onesb[:, :], self.sq[:, i, :T], blk == 0, blk == 7, [('sq', i)], [pk])
        self.act(AF.Sqrt, self.rstd[:, :T], ps[:, :T], [pk], ['rstd'], bias=EPS, scale=1.0 / D)
        self.recip(self.rstd[:, :T], self.rstd[:, :T], ['rstd'], ['rstd'])
        if out_f32 is None:
            for blk in range(8):
                self.stt('dve', self.xn[:, blk, :T], self.h[:, blk, :T], self.vecd[:, v, blk:blk + 1],
                         self.rstd[:, :T], ALU.mult, ALU.mult, [('h', blk), 'rstd'], [('xn', blk)])

    def mlp(self, T, layer):
        self.rmsnorm(T, 5 + layer)
        for half in range(2):
            for s in range(4):
                wv, wk = self.wnext()
                for j in range(4):
                    jb = 4 * s + j
                    ps, pk, _ = self.bank()
                    for k in range(8):
                        self.mm(ps[:, :T], wv[:, k, j * 128:(j + 1) * 128], self.xn[:, k, :T], k == 0, k == 7,
                                [wk, ('xn', k)], [pk])
                    i = jb % 2
                    self.act(AF.Relu, self.cacc[:, i, :T], ps[:, :T], [pk], [('cacc', i)])
                    self.tt('dve', self.big[:, jb, :T], self.cacc[:, i, :T], self.cacc[:, i, :T], ALU.mult,
                            [('cacc', i)], [('big', jb)])
            for s in range(4):
                wv, wk = self.wnext()
                for j in range(2):
                    db = 2 * s + j
                    ps, pk, _ = self.bank()
                    for kc in range(16):
                        self.mm(ps[:, :T], wv[:, kc, j * 128:(j + 1) * 128], self.big[:, kc, :T], kc == 0, kc == 15,
                                [wk, ('big', kc)], [pk])
                    self.tt('dve', self.h[:, db, :T], self.h[:, db, :T], ps[:, :T], ALU.add,
                            [pk, ('h', db)], [('h', db)])

    def zyv(self, c, blk, L):
        z4 = self.zy.rearrange("p c (b t) -> p c b t", b=16)
        return z4[:, c, blk, 0:L]

    def ssd_layer(self, T, layer, kind):
        L = min(T, 128)
        nch = (T + 127) // 128
        sample = (kind == 'sample')
        self.rmsnorm(T, layer)
        self.mark(f'{kind}.ssd{layer}.norm')
        slabs = {}

        def conv_chain(blk):
            sidx, j = blk // 4, blk % 4
            if sidx not in slabs:
                slabs[sidx] = self.wnext()
            wv, wk = slabs[sidx]
            ps, pk, psi = self.bank(hold=True)
            for k in range(8):
                self.mm(ps[:, :T], wv[:, k, j * 128:(j + 1) * 128], self.xn[:, k, :T], k == 0, k == 7,
                        [wk, ('xn', k)], [pk])
            yield
            i = blk % 2
            cw = self.convw[:, layer, blk, :]
            cb = self.convb[:, layer, blk:blk + 1]
            if not sample:
                xp = self.xpre[:, i, :]
                self.cp('pool', xp[:, 0:3], self.ctail[:, layer, blk, :], [('ctail', layer, blk), 'ctail_all'],
                        [('xpre', i)])
                self.cp('act', xp[:, 3:3 + T], ps[:, :T], [pk], [('xpre', i)])
                self.act(AF.Identity, self.cacc[:, i, :T], ps[:, :T], [pk], [('cacc', i)],
                         bias=cb, scale=cw[:, 3:4])
                self.release(psi)
                yield
                for k in range(3):
                    self.stt('dve', self.cacc[:, i, :T], xp[:, k:k + T], cw[:, k:k + 1], self.cacc[:, i, :T],
                             ALU.mult, ALU.add, [('xpre', i), ('cacc', i)], [('cacc', i)])
                yield
                self.act(AF.Silu, self.big[:, blk, :T], self.cacc[:, i, :T], [('cacc', i)], [('big', blk)])
                self.cp('pool', self.ctail[:, layer, blk, :], xp[:, T:T + 3], [('xpre', i)],
                        [('ctail', layer, blk)])
            else:
                xp3 = self.xpre[:, i, 0:176].rearrange("p (b k) -> p b k", b=16)
                ps3 = ps[:, 0:128].rearrange("p (b t) -> p b t", b=16)
                ca3 = self.cacc[:, i, 0:128].rearrange("p (b t) -> p b t", b=16)
                self.S.dma('sp', xp3[:, :, 0:3],
                           self.d_sconvT[layer, blk * 128:(blk + 1) * 128, :].rearrange("p (b k) -> p b k", b=16),
                           reads=(), writes=[('xpre', i)])
                self.cp('act', xp3[:, :, 3:11], ps3, [pk], [('xpre', i)])
                self.act(AF.Identity, self.cacc[:, i, :128], ps[:, :128], [pk], [('cacc', i)],
                         bias=cb, scale=cw[:, 3:4])
                self.release(psi)
                yield
                for k in range(3):
                    self.stt('dve', ca3, xp3[:, :, k:k + 8], cw[:, k:k + 1], ca3,
                             ALU.mult, ALU.add, [('xpre', i), ('cacc', i)], [('cacc', i)])
                yield
                self.act(AF.Silu, self.big[:, blk, :128], self.cacc[:, i, :128], [('cacc', i)], [('big', blk)])
                self.S.dma('sp', self.o_sconv[:, layer, blk, :, :], xp3[:, :, 8:11],
                           reads=[('xpre', i)], writes=[('o_sconv', layer, blk)])

        self.interleave((conv_chain(blk) for blk in range(32)), 2)
        self.mark(f'{kind}.ssd{layer}.conv')
        for s in range(4):
            wv, wk = self.wnext()
            for ci in range(nch):
                ps, pk, _ = self.bank()
                for k in range(8):
                    self.mm(ps[:L, :512], self.xn[:, k, ci * 128:ci * 128 + L], wv[:, k, :], k == 0, k == 7,
                            [wk, ('xn', k)], [pk])
                self.act(AF.Silu, self.zy[:L, ci, s * 512:(s + 1) * 512], ps[:L, :512], [pk], [('zy', ci)])
        self.mark(f'{kind}.ssd{layer}.z')
        if not sample:
            self.cp('act', self.hbf[:, :], self.hst[:, layer, :], [('hst', layer)], ['hbf'])
        for ci in range(nch):
            self.ssd_chunk(layer, ci, L, sample)
            self.mark(f'{kind}.ssd{layer}.chunk{ci}')
        wo_plan = []
        for s in range(4):
            wv, wk = self.wnext()
            for j in range(2):
                db = 2 * s + j
                ps, pk, _ = self.bank()
                for c in range(nch):
                    for kc in range(16):
                        self.mm(ps[:, c * 128:c * 128 + L], wv[:, kc, j * 128:(j + 1) * 128], self.zyv(c, kc, L),
                                kc == 0, kc == 15, [wk, ('zy', c)], [pk])
                self.tt('dve', self.h[:, db, :T], self.h[:, db, :T], ps[:, :T], ALU.add, [pk, ('h', db)], [('h', db)])

    def ssd_chunk(self, layer, ci, L, sample):
        sm = self.sm
        c0 = ci * 128
        cols = slice(c0, c0 + L)
        TRI = self.cm[:, 3 if sample else 0, :]
        U = self.cm[:, 4 if sample else 1, :]
        ONES = self.cm[:, 5 if sample else 2, :]
        xnk = [('xn', k) for k in range(8)]
        ps, pk, _ = self.bank()
        for k in range(8):
            self.mm(ps[:L, 0:32], self.xn[:, k, cols], self.wdt[:, layer, k, :], k == 0, k == 7, [('xn', k)], [pk])
        self.tt('dve', sm['dtb'][:L], ps[:L, 0:32], self.hvec[:L, layer, 0, :], ALU.add, [pk], ['dtb'])
        self.act(AF.Exp, sm['e1'][:L], sm['dtb'][:L], ['dtb'], ['e1'])
        self.act(AF.Ln, sm['dt'][:L], sm['e1'][:L], ['e1'], ['dt'], bias=1.0)
        self.tt('dve', sm['dtA'][:L], sm['dt'][:L], self.aneg[:L, layer, :], ALU.mult, ['dt', 'aneg'], ['dtA'])
        self.mark(f'c{layer}.{ci}.dt')
        ps, pk, _ = self.bank()
        self.mm(ps[:L, 0:32], TRI[:L, :L], sm['dtA'][:L], True, True, ['dtA'], [pk])
        self.mm(ps[:, 32:64], ONES[:L, :], sm['dtA'][:L], True, True, ['dtA'], [pk])
        self.cp('act', sm['acs'][:L], ps[:L, 0:32], [pk], ['acs'])
        self.act(AF.Exp, sm['e'][:L], ps[:L, 0:32], [pk], ['e'])
        self.act(AF.Exp, sm['cd'][:, :], ps[:, 32:64], [pk], ['cd'])
        self.tt('dve', sm['dd'][:L], ps[:L, 32:64], sm['acs'][:L], ALU.subtract, [pk, 'acs'], ['dd'])
        self.act(AF.Exp, sm['dte'][:L], sm['dd'][:L], ['dd'], ['dte'])
        self.tt('dve', sm['dtw'][:L], sm['dt'][:L], sm['dte'][:L], ALU.mult, ['dt', 'dte'], ['dtw'])
        self.mark(f'c{layer}.{ci}.cum')
        for q in range(4):
            pt, ptk = self.pbank()
            for j in range(4):
                blk = 4 * q + j
                self.tr(pt[:L, j * 128:(j + 1) * 128], self.big[:, blk, cols], self.identb[:, :], [('big', blk)], [ptk])
            self.cp('act', self.x_tm[:L, q * 512:(q + 1) * 512], pt[:L, :512], [ptk], ['x_tm'])
            pt3 = pt[:L, :512].rearrange("p (h d) -> p h d", h=8)
            self.tt('dve', self.xdt[:L, q * 512:(q + 1) * 512].rearrange("p (h d) -> p h d", h=8), pt3,
                    sm['dt'][:L, 8 * q:8 * q + 8].unsqueeze(2).broadcast_to([L, 8, 64]), ALU.mult,
                    [ptk, 'dt'], ['xdt'])
            self.tt('dve', self.xw[:L, q * 512:(q + 1) * 512].rearrange("p (h d) -> p h d", h=8), pt3,
                    sm['dtw'][:L, 8 * q:8 * q + 8].unsqueeze(2).broadcast_to([L, 8, 64]), ALU.mult,
                    [ptk, 'dtw'], ['xw'])
        for q in range(2):
            pt, ptk = self.pbank()
            for j in range(4):
                blk = 16 + 4 * q + j
                self.tr(pt[:L, j * 128:(j + 1) * 128], self.big[:, blk, cols], self.identb[:, :], [('big', blk)], [ptk])
            self.cp('act', self.B_tm[:L, q * 512:(q + 1) * 512], pt[:L, :512], [ptk], ['B_tm'])
        self.mark(f'c{layer}.{ci}.tr')
        for half in range(2):
            ps, pk, _ = self.bank()
            for j in range(4):
                g = 4 * half + j
                self.mm(ps[:L, j * 128:j * 128 + L], self.big[:, 16 + g, cols], self.big[:, 24 + g, cols], True, True,
                        [('big', 16 + g), ('big', 24 + g)], [pk])
            self.tt('dve', self.Gm[:L, 4 * half:4 * half + 4, :L],
                    ps[:L, :].rearrange("p (j l) -> p j l", j=4)[:, :, :L],
                    TRI[:L, :L].unsqueeze(1).broadcast_to([L, 4, L]), ALU.mult, [pk], ['Gm'])
        held = []
        if sample:
            offb = []
            for q in range(4):
                ps, pk, bi = self.bank(hold=True)
                offb.append((ps, pk, bi))
            self.memset('pool', self.CTm[:, :, :], 0.0, ['CTm'])
            self.tt('dve', self.dtAm[:, :, :], sm['dtA'][:, :].unsqueeze(1).broadcast_to([128, 16, 32]),
                    self.seqmask[:, :].unsqueeze(2).broadcast_to([128, 16, 32]), ALU.mult, ['dtA'], ['dtAm'])
            ps, pk, _ = self.bank()
            self.mm(ps[:, :512], self.cm[:, 2, :], self.dtAm.rearrange("p b h -> p (b h)"), True, True, ['dtAm'], [pk])
            self.act(AF.Exp, self.cdall.rearrange("p b h -> p (b h)"), ps[:, :512], [pk], ['cdall'])
            for b in range(16):
                i = b % 2
                st = self.hst[:, i, :]
                self.S.dma('sp', st, self.d_sssmT[layer, b], reads=(), writes=[('hst', i)])
                self.cp('act', self.hbf[:, :], st, [('hst', i)], ['hbf'])
                bc8 = slice(b * 8, b * 8 + 8)
                self.cp('pool', self.CTm[:, :, bc8], self.big[:, 24:32, bc8],
                        [('big', 24 + g) for g in range(8)], ['CTm'])
                self.ts('pool', self.Bm.rearrange("p g n -> p (g n)"), self.B_tm[:, :],
                        self.seqmask[:, b:b + 1], ALU.mult, ['B_tm'], ['Bm'])
                for g in range(8):
                    ps, pk, _bi = offb[g // 2]
                    self.mm(ps[:, (g % 2) * 256:(g % 2) * 256 + 256], self.CTm[:, g, :],
                            self.hbf[:, g * 256:(g + 1) * 256], b == 0 and g % 2 == 0, b == 15, ['CTm', 'hbf'], [pk])
                self.memset('pool', self.CTm[:, :, bc8], 0.0, ['CTm'])
                for q in range(4):
                    ps, pk, _ = self.bank()
                    for gg in range(2):
                        g = 2 * q + gg
                        self.mm(ps[:, gg * 256:(gg + 1) * 256], self.Bm[:, g, :], self.xw[:, g * 256:(g + 1) * 256],
                                True, True, ['Bm', 'xw'], [pk])
                    stq = st[:, q * 512:(q + 1) * 512].rearrange("p (h d) -> p h d", h=8)
                    self.tt('dve', stq, stq,
                            self.cdall[:, b, 8 * q:8 * q + 8].unsqueeze(2).broadcast_to([128, 8, 64]), ALU.mult,
                            [('hst', i), 'cdall'], [('hst', i)])
                    self.tt('dve', st[:, q * 512:(q + 1) * 512], st[:, q * 512:(q + 1) * 512], ps[:, :512], ALU.add,
                            [pk, ('hst', i)], [('hst', i)])
                self.S.dma('sp', self.o_sssm[layer, b], st, reads=[('hst', i)], writes=[('o_sssm', layer, b)])
        self.mark(f'c{layer}.{ci}.G')
        def group_chain(g):
            i = g % 3
            self.tt('pool', self.Lg[:L, i, :, :L], U[:L, :L].unsqueeze(1).broadcast_to([L, 4, L]),
                    sm['dtA'][:L, 4 * g:4 * g + 4].unsqueeze(2).broadcast_to([L, 4, L]), ALU.mult,
                    ['dtA'], [('Lg', i)])
            yield
            ps, pk, psi = self.bank(hold=True)
            for j in range(4):
                self.mm(ps[:L, j * 128:j * 128 + L], self.Lg[:L, i, j, :L], TRI[:L, :L], True, True, [('Lg', i)], [pk])
            yield
            self.act(AF.Exp, self.ex[:L, i, :, :L], ps[:L, :].rearrange("p (j l) -> p j l", j=4)[:, :, :L],
                     [pk], [('ex', i)])
            self.release(psi)
            yield
            self.tt('dve', self.LT[:L, i, :, :L], self.ex[:L, i, :, :L],
                    self.Gm[:L, g, :L].unsqueeze(1).broadcast_to([L, 4, L]), ALU.mult,
                    [('ex', i), 'Gm'], [('LT', i)])
            yield
            psd, pdk, psdi = self.bank(hold=True)
            for j in range(4):
                hh = 4 * g + j
                self.mm(psd[:L, j * 64:(j + 1) * 64], self.LT[:L, i, j, :L], self.xdt[:L, hh * 64:(hh + 1) * 64],
                        True, True, [('LT', i), 'xdt'], [pdk])
            psoi = None
            if sample:
                pso, pok, _bi = offb[g // 2]
                off_ap = pso[:L, (g % 2) * 256:(g % 2) * 256 + 256]
            else:
                pso, pok, psoi = self.bank(hold=True)
                self.mm(pso[:L, 0:256], self.big[:, 24 + g, cols], self.hbf[:, g * 256:(g + 1) * 256], True, True,
                        [('big', 24 + g), 'hbf'], [pok])
                off_ap = pso[:L, 0:256]
            yield
            tg = self.t[:L, g * 256:(g + 1) * 256]
            self.cp('act', tg, psd[:L, 0:256], [pdk], [('t', g)])
            self.release(psdi)
            self.tt('pool', self.xd[:L, i, :].rearrange("p (h d) -> p h d", h=4),
                    self.x_tm[:L, g * 256:(g + 1) * 256].rearrange("p (h d) -> p h d", h=4),
                    self.hvec[:L, layer, 2, 4 * g:4 * g + 4].unsqueeze(2).broadcast_to([L, 4, 64]), ALU.mult,
                    ['x_tm'], [('xd', i)])
            yield
            for j in range(4):
                hh = 4 * g + j
                self.stt('dve', tg[:, j * 64:(j + 1) * 64], off_ap[:, j * 64:(j + 1) * 64], sm['e'][:L, hh:hh + 1],
                         tg[:, j * 64:(j + 1) * 64], ALU.mult, ALU.add, [pok, 'e', ('t', g)], [('t', g)])
            if psoi is not None:
                self.release(psoi)
            yield
            self.tt('pool', tg, tg, self.xd[:L, i, :], ALU.add, [('xd', i), ('t', g)], [('t', g)])
            self.tt('pool', tg, tg, self.zy[:L, ci, g * 256:(g + 1) * 256], ALU.mult, [('t', g), ('zy', ci)],
                    [('t', g)])
            yield
            self.act(AF.Square, self.sqj[:L, i, :], tg, [('t', g)], [('sqj', i), ('ss', g)],
                     accum_out=self.ss[:L, g:g + 1])

        self.interleave((group_chain(g) for g in range(8)), 1 if sample else 3)
        if sample:
            for (ps, pk, bi) in offb:
                self.release(bi)
        self.mark(f'c{layer}.{ci}.groups')
        tkeys = [('t', g) for g in range(8)]
        sskeys = [('ss', g) for g in range(8)]
        self.act(AF.Sqrt, self.rsg[:L, :], self.ss[:L, :], sskeys, ['rsg'], bias=EPS, scale=1.0 / 256)
        self.recip(self.rsg[:L, :], self.rsg[:L, :], ['rsg'], ['rsg'])
        self.tt('dve', self.gn[:L, :].rearrange("p (g c) -> p g c", g=8), self.t[:L, :].rearrange("p (g c) -> p g c", g=8),
                self.rsg[:L, :].unsqueeze(2).broadcast_to([L, 8, 256]), ALU.mult, tkeys + ['rsg'], ['xdt'])
        z4 = self.zy.rearrange("p c (b t) -> p c b t", b=16)
        for q in range(4):
            pt, ptk = self.pbank()
            for j in range(4):
                blk = 4 * q + j
                self.tr(pt[:, j * 128:j * 128 + L], self.gn[:L, blk * 128:(blk + 1) * 128], self.identb[:L, :L],
                        ['xdt'], [ptk])
            self.tt('dve', z4[:, ci, 4 * q:4 * q + 4, 0:L],
                    pt[:, :512].rearrange("p (j l) -> p j l", j=4)[:, :, :L],
                    self.gatew[:, layer, 4 * q:4 * q + 4].unsqueeze(2).broadcast_to([128, 4, L]), ALU.mult,
                    [ptk], [('zy', ci)])
        self.mark(f'c{layer}.{ci}.yn')
        if not sample:
            st = self.hst[:, layer, :]
            for q in range(4):
                ps, pk, _ = self.bank()
                for gg in range(2):
                    g = 2 * q + gg
                    self.mm(ps[:, gg * 256:(gg + 1) * 256], self.B_tm[:L, g * 128:(g + 1) * 128],
                            self.xw[:L, g * 256:(g + 1) * 256], True, True, ['B_tm', 'xw'], [pk])
                stq = st[:, q * 512:(q + 1) * 512].rearrange("p (h d) -> p h d", h=8)
                self.tt('dve', stq, stq, sm['cd'][:, 8 * q:8 * q + 8].unsqueeze(2).broadcast_to([128, 8, 64]),
                        ALU.mult, [('hst', layer), 'cd'], [('hst', layer)])
                self.tt('dve', st[:, q * 512:(q + 1) * 512], st[:, q * 512:(q + 1) * 512], ps[:, :512], ALU.add,
                        [pk, ('hst', layer)], [('hst', layer)])
            self.cp('act', self.hbf[:, :], st, [('hst', layer)], ['hbf'])

    def kv_proj(self, T, kind, last):
        L = min(T, 128)
        nch = (T + 127) // 128
        self.rmsnorm(T, 2)
        wv, wk = self.wnext()
        for kp in range(2):
            ps, pk, _ = self.bank()
            for k in range(8):
                self.mm(ps[:, :T], wv[:, k, kp * 128:(kp + 1) * 128], self.xn[:, k, :T], k == 0, k == 7,
                        [wk, ('xn', k)], [pk])
            if kind == 'meta':
                self.cp('act', self.kTm[:, kp, :T], ps[:, :T], [pk], ['kTm'])
            else:
                self.cp('act', self.kT_all[:, kp, 128:128 + T], ps[:, :T], [pk], ['kT_all'])
        for ci in range(nch):
            cols = slice(ci * 128, ci * 128 + L)
            ps, pk, _ = self.bank()
            for k in range(8):
                self.mm(ps[:L, 0:256], self.xn[:, k, cols], wv[:, k, 256:512], k == 0, k == 7, [wk, ('xn', k)], [pk])
            ps4 = ps[:L, 0:256].rearrange("p (k d) -> p k d", k=4)
            if kind == 'meta':
                self.cp('act', self.Vpm[:L, :, 0, 0:64], ps4, [pk], ['vpm'])
                self.cp('dve', self.Vpm[:L, :, 1, 64:128], ps4, [pk], ['vpm'])
            else:
                self.cp('act', self.Vpad[:L, ci + 1, :, 0, 0:64], ps4, [pk], [('vpad', ci + 1), 'vpad_all'])
                self.cp('dve', self.Vpad[:L, ci + 1, :, 1, 64:128], ps4, [pk], [('vpad', ci + 1), 'vpad_all'])
            want_out = (kind == 'sample') or (kind == 'prompt' and last and ci == nch - 1)
            if want_out:
                ps2, pk2, _ = self.bank()
                for k in range(8):
                    self.mm(ps2[:L, 0:256], self.xn[:, k, cols], wv[:, k, 0:256], k == 0, k == 7, [wk, ('xn', k)],
                            [pk2])
                self.cp('act', self.kvo[:, 0, 0:256], ps2[:, 0:256], [pk2], [('cacc', 0)])
                self.cp('dve', self.kvo[:, 1, 0:256], ps[:, 0:256], [pk], [('cacc', 1)])
                if kind == 'prompt':
                    self.S.dma('sp', self.o_pk, self.kvo[:, 0, 0:256], reads=[('cacc', 0)], writes=['o_pk'])
                    self.S.dma('sp', self.o_pv, self.kvo[:, 1, 0:256], reads=[('cacc', 1)], writes=['o_pv'])
                else:
                    for b in range(16):
                        self.S.dma('sp', self.o_sk[b, 120:128, :], self.kvo[b * 8:(b + 1) * 8, 0, 0:256],
                                   reads=[('cacc', 0)], writes=[('o_sk_new', b)])
                        self.S.dma('sp', self.o_sv[b, 120:128, :], self.kvo[b * 8:(b + 1) * 8, 1, 0:256],
                                   reads=[('cacc', 1)], writes=[('o_sv_new', b)])

    def q_proj(self, T, j):
        self.rmsnorm(T, 3 + j)
        for s in range(2):
            wv, wk = self.wnext()
            for jj in range(4):
                m = 4 * s + jj
                ps, pk, _ = self.bank()
                for k in range(8):
                    self.mm(ps[:, :T], wv[:, k, jj * 128:(jj + 1) * 128], self.xn[:, k, :T], k == 0, k == 7,
                            [wk, ('xn', k)], [pk])
                self.cp('act', self.qT[:, m, :T], ps[:, :T], [pk], [('qT', m)])

    def softmax_rows(self, M, N, i, sc, sck, maskap, sink_ap):
        a = self.att_small
        mx, negm, rsum, es, den, rden = (a[:M, i, c:c + 1] for c in range(6))
        ak = ('asm', i)
        self.stt('dve', self.s_sb[:M, i, :N], sc, SCALE, maskap, ALU.mult, ALU.add, [sck], [('s_sb', i)])
        yield
        nc = self.nc
        s_in = self.s_sb[:M, i, :N]
        self.S.op('dve', lambda: nc.vector.reduce_max(out=mx, in_=s_in, axis=AX.X), [('s_sb', i)], [ak])
        self.tt('dve', mx, mx, sink_ap, ALU.max, [ak], [ak])
        self.ts('dve', negm, mx, -1.0, ALU.mult, [ak], [ak])
        yield
        self.act(AF.Exp, self.pe_sb[:M, i, :N], self.s_sb[:M, i, :N], [('s_sb', i), ak], [('pe', i), ('rs', i)],
                 bias=negm, accum_out=rsum)
        self.act(AF.Exp, es, sink_ap, [ak], [('es', i)], bias=negm)
        yield
        self.tt('dve', den, rsum, es, ALU.add, [('rs', i), ('es', i)], [('den', i)])
        self.recip(rden, den, [('den', i)], [('den', i)])
        self.ts('dve', self.pn[:M, i, :N], self.pe_sb[:M, i, :N], rden, ALU.mult, [('pe', i), ('den', i)],
                [('pn', i)])
        yield

    def attn_prompt(self, T, j, first_tile):
        nch = T // 128
        z4 = self.zy.rearrange("p c (b t) -> p c b t", b=16)
        pvs = {}
        counter = [0]

        def head_chain(ci, m, half):
            cols = slice(ci * 128, ci * 128 + 128)
            which = 1 if (first_tile and ci == 0) else 0
            kp, r = m // 4, m % 4
            kvh = 2 * kp + half
            head = kvh * 4 + r
            it = counter[0]
            counter[0] += 1
            i = it % 4
            i4 = it % 8
            hs = slice(half * 64, half * 64 + 64)
            sc, sck, sci = self.bank(hold=True)
            q = self.qT[hs, m, cols]
            self.mm(sc[:, 0:16], q, self.kTm[hs, kp, 0:16], True, True, [('qT', m), 'kTm'], [sck])
            self.mm(sc[:, 16:272], q, self.kT_all[hs, kp, ci * 128:ci * 128 + 256], True, True,
                    [('qT', m), 'kT_all'], [sck])
            yield
            first = True
            for _ in self.softmax_rows(128, 272, i, sc[:, 0:272], sck, self.amask[:, which, :],
                                       self.sinks[:, j, head:head + 1]):
                if first:
                    self.release(sci)
                    first = False
                yield
            pt, ptk = self.pbank()
            self.tr(pt[0:16, 0:128], self.pn[:, i, 0:16], self.identb[:, :], [('pn', i)], [ptk])
            self.tr(pt[:, 128:256], self.pn[:, i, 16:144], self.identb[:, :], [('pn', i)], [ptk])
            self.tr(pt[:, 256:384], self.pn[:, i, 144:272], self.identb[:, :], [('pn', i)], [ptk])
            self.cp('act', self.pT[:, i4, :], pt[:, 0:384], [ptk], [('pT', i4)])
            yield
            if (ci, m) not in pvs:
                pv, pvk, pvi = self.bank(hold=True)
                pvs[(ci, m)] = [pv, pvk, 0, pvi]
            ent = pvs[(ci, m)]
            pv, pvk = ent[0], ent[1]
            segs = [(self.Vpm[0:16, kvh, half, :], self.pT[0:16, i4, 0:128], ['vpm']),
                    (self.Vpad[:, ci, kvh, half, :], self.pT[:, i4, 128:256], [('vpad', ci), 'vpad_all']),
                    (self.Vpad[:, ci + 1, kvh, half, :], self.pT[:, i4, 256:384], [('vpad', ci + 1), 'vpad_all'])]
            for lh, rh, keys in segs:
                self.mm(pv[:, 0:128], lh, rh, ent[2] == 0, ent[2] == 5, keys + [('pT', i4)], [pvk])
                ent[2] += 1
            if ent[2] == 6:
                yield
                self.cp('act', z4[:, ci, m, 0:128], pv[:, 0:128], [pvk], [('zy', ci)])
                self.release(ent[3])

        gens = (head_chain(ci, m, half) for ci in range(nch) for m in range(8) for half in range(2))
        self.interleave(gens, 4)

    def attn_sample(self, j):
        z4 = self.zy.rearrange("p c (b t) -> p c b t", b=16)
        it = 0
        for kp in range(2):
            self.cp('pool', self.qS[:, kp].rearrange("p b (r t) -> p b r t", r=4),
                    self.qT[:, kp * 4:kp * 4 + 4, 0:128].rearrange("p r (b t) -> p b r t", b=16),
                    [('qT', kp * 4 + r) for r in range(4)], ['qS'])
        for b in range(16):
            bc = slice(b * 8, b * 8 + 8)
            for kp in range(2):
                pv, pvk, _ = self.bank()
                used = []
                for half in range(2):
                    kvh = 2 * kp + half
                    i = it % 2
                    i4 = it % 4
                    it += 1
                    used.append(i4)
                    hs = slice(half * 64, half * 64 + 64)
                    sc, sck, _ = self.bank()
                    q = self.qS[hs, kp, b, :]
                    self.mm(sc[:32, 0:16], q, self.kTm[hs, kp, 0:16], True, True, ['qS', 'kTm'], [sck])
                    self.mm(sc[:32, 16:144], q, self.kTbuf[hs, b, kp, :], True, True, ['kTbuf'], [sck])
                    self.mm(sc[:32, 144:152], q, self.kT_all[hs, kp, 128 + b * 8:128 + b * 8 + 8], True, True,
                            ['kT_all'], [sck])
                    for _ in self.softmax_rows(32, 152, i, sc[:32, 0:152], sck, self.amask_s[0:32, :],
                                               self.sinkrows[0:32, j, kvh:kvh + 1]):
                        pass
                    self.cp('pool', self.pnpad[:32, i, bc], self.pn[:32, i, 144:152], [('pn', i)], [('pnpad', i)])
                    pt, ptk = self.pbank()
                    self.tr(pt[0:16, 0:32], self.pn[:32, i, 0:16], self.identb[:32, :32], [('pn', i)], [ptk])
                    self.tr(pt[:, 32:64], self.pn[:32, i, 16:144], self.identb[:32, :32], [('pn', i)], [ptk])
                    self.tr(pt[:, 64:96], self.pnpad[:32, i, :], self.identb[:32, :32], [('pnpad', i)], [ptk])
                    self.cp('act', self.pT[:, i4, 0:96], pt[:, 0:96], [ptk], [('pT', i4)])
                    self.memset('pool', self.pnpad[:32, i, bc], 0.0, [('pnpad', i)])
                for r in range(4):
                    rc = slice(r * 8, r * 8 + 8)
                    for half in range(2):
                        kvh = 2 * kp + half
                        i4 = used[half]
                        self.mm(pv[:, rc], self.Vpm[0:16, kvh, half, :], self.pT[0:16, i4, r * 8:r * 8 + 8],
                                half == 0, False, ['vpm', ('pT', i4)], [pvk])
                        self.mm(pv[:, rc], self.Vbuf[:, b, kvh, half, :], self.pT[:, i4, 32 + r * 8:32 + r * 8 + 8],
                                False, False, [('big', 2 * b), ('big', 2 * b + 1), ('pT', i4)], [pvk])
                        self.mm(pv[:, rc], self.Vpad[:, 1, kvh, half, :], self.pT[:, i4, 64 + r * 8:64 + r * 8 + 8],
                                False, half == 1, [('vpad', 1), 'vpad_all', ('pT', i4)], [pvk])
                self.cp('act', z4[:, 0, kp * 4:kp * 4 + 4, bc],
                        pv[:, 0:32].rearrange("p (r t) -> p r t", r=4), [pvk], [('zy', 0)])

    def attn_layer(self, T, j, kind, first_tile):
        nch = (T + 127) // 128
        self.q_proj(T, j)
        if kind == 'prompt':
            self.attn_prompt(T, j, first_tile)
        else:
            self.load_vbuf()
            self.attn_sample(j)
        for s in range(2):
            wv, wk = self.wnext()
            for jj in range(4):
                db = 4 * s + jj
                ps, pk, _ = self.bank()
                for c in range(nch):
                    for mc in range(8):
                        self.mm(ps[:, c * 128:(c + 1) * 128], wv[:, mc, jj * 128:(jj + 1) * 128],
                                self.zyv(c, mc, 128), mc == 0, mc == 7, [wk, ('zy', c)], [pk])
                self.tt('dve', self.h[:, db, :T], self.h[:, db, :T], ps[:, :T], ALU.add, [pk, ('h', db)], [('h', db)])

    def final_out(self, T, dst):
        self.rmsnorm(T, 9, out_f32=True)
        for blk in range(8):
            i = blk % 2
            self.stt('dve', self.cacc[:, i, :T], self.h[:, blk, :T], self.vecd[:, 9, blk:blk + 1],
                     self.rstd[:, :T], ALU.mult, ALU.mult, [('h', blk), 'rstd'], [('cacc', i)])
            self.S.dma('sp', dst[blk * 128:(blk + 1) * 128, :], self.cacc[:, i, :T], reads=[('cacc', i)],
                       writes=[('o_y', blk, id(dst))])

    def run_tile(self, kind, T, src, dst, first_tile=False, last=False):
        S = self.S
        S.dma('sp', self.h[:, :, :T], src.rearrange("(b p) t -> p b t", p=128), reads=(),
              writes=[('h', b) for b in range(8)])
        for layer in range(2):
            self.ssd_layer(T, layer, kind)
            self.mark(f'{kind}.ssd{layer}.out')
            self.mlp(T, layer)
            self.mark(f'{kind}.mlp{layer}')
        self.kv_proj(T, kind, last)
        self.mark(f'{kind}.kv')
        if kind == 'meta':
            return
        S.barrier()
        if kind == 'sample':
            self.load_sample_cache()
        for j in range(2):
            self.attn_layer(T, j, kind, first_tile)
            self.mark(f'{kind}.attn{j}')
            self.mlp(T, 2 + j)
            self.mark(f'{kind}.mlp{2 + j}')
        self.final_out(T, dst)
        self.mark(f'{kind}.final')
        if kind == 'prompt':
            nch = T // 128
            self.cp('pool', self.kT_all[:, :, 0:128], self.kT_all[:, :, T:T + 128], ['kT_all'], ['kT_all'])
            self.cp('pool', self.Vpad[:, 0], self.Vpad[:, nch], [('vpad', nch), 'vpad_all'], [('vpad', 0), 'vpad_all'])
        S.barrier()

    def load_sample_cache(self):
        S = self.S
        self.memset('pool', self.pnpad[:, :, :], 0.0, [('pnpad', 0), ('pnpad', 1)])
        for b in range(16):
            S.dma('pool', self.kTbuf[:, b, :, :], self.d_ckT[b].rearrange("(k p) w -> p k w", p=128), reads=(),
                  writes=['kTbuf'])
        S.dma('sp', self.o_sk[:, 0:120, :], self.d_ck[:, 8:128, :], reads=(), writes=['o_sk_old'])
        S.dma('sp', self.o_sv[:, 0:120, :], self.d_cv[:, 8:128, :], reads=(), writes=['o_sv_old'])

    def load_vbuf(self):
        S = self.S
        bigk = [('big', b) for b in range(32)]
        self.memset('pool', self.big[:, :, :], 0.0, bigk)
        for b in range(16):
            v4 = self.d_cv[b].rearrange("w (k d) -> w k d", k=4)
            S.dma('pool', self.Vbuf[:, b, :, 0, 0:64], v4, reads=(), writes=[('big', 2 * b), ('big', 2 * b + 1)])
            S.dma('pool', self.Vbuf[:, b, :, 1, 64:128], v4, reads=(), writes=[('big', 2 * b), ('big', 2 * b + 1)])

    def build(self):
        S = self.S
        kinds = ['meta'] + ['prompt'] * self.NT + (['sample'] if self.do_sample else [])
        self.make_plan(kinds)
        try:
            self._build_body()
        except StopBuild:
            pass
        S.barrier()
        S.emit()
        return self.nc

    def _build_body(self):
        S = self.S
        self.prologue()
        self.mark('prologue')
        self.run_tile('meta', 16, self.d_metaT, None)
        S.barrier()
        for ti in range(self.NT):
            self.run_tile('prompt', 512, self.d_xT[:, ti * 512:(ti + 1) * 512],
                          self.o_ypT[:, ti * 512:(ti + 1) * 512], first_tile=(ti == 0), last=(ti == self.NT - 1))
        S.dma('sp', self.o_pconv, self.ctail, reads=[('ctail', l, b) for l in range(2) for b in range(32)],
              writes=['o_pconv'])
        for layer in range(2):
            S.dma('sp', self.o_pssm[layer], self.hst[:, layer, :], reads=[('hst', layer)], writes=[('o_pssm', layer)])
        S.barrier()
        if self.do_sample:
            self.run_tile('sample', 128, self.d_xsT, self.o_ysT)


def _consts():
    k = np.arange(128)
    seq = k // 8
    tri = (k[:, None] <= k[None, :]).astype(np.float32)
    U = (k[:, None] > k[None, :]).astype(np.float32)
    ones = np.ones((128, 128), np.float32)
    same = (seq[:, None] == seq[None, :]).astype(np.float32)
    cm = np.stack([tri, U, ones, tri * same, U * same, same], axis=1)
    seqmask = (seq[:, None] == np.arange(16)[None, :]).astype(np.float32)
    qi = k[:, None]
    cj = k[None, :]
    prev = np.where(cj > qi, 0.0, NEG).astype(np.float32)
    cur = np.where(cj <= qi, 0.0, NEG).astype(np.float32)
    meta0 = np.zeros((128, 16), np.float32)
    am0 = np.concatenate([meta0, prev, cur], axis=1)
    am1 = np.concatenate([meta0, np.full((128, 128), NEG, np.float32), cur], axis=1)
    amask = np.stack([am0, am1], axis=1)
    t = np.tile(np.arange(8), 4)[:, None]
    bufm = np.where(np.arange(128)[None, :] > t, 0.0, NEG).astype(np.float32)
    newm = np.where(np.arange(8)[None, :] <= t, 0.0, NEG).astype(np.float32)
    amask_s = np.concatenate([np.zeros((32, 16), np.float32), bufm, newm], axis=1)
    return dict(cm=cm, seqmask=seqmask, amask=amask, amask_s=amask_s,
                identf=np.eye(128, dtype=np.float32), onesf=ones)


def _pvec(v):
    return np.ascontiguousarray(np.asarray(v, np.float32).reshape(8, 128).T)


def prepare_inputs(inp, NT):
    f = lambda a: np.ascontiguousarray(np.asarray(a, dtype=np.float32))
    NTOK = NT * 512
    shared = _consts()
    vecs = [inp['a_norm_w'][0], inp['a_norm_w'][1], inp['kv_norm_w'], inp['b_norm_w'][0], inp['b_norm_w'][1],
            inp['mlp_norm_w'][0], inp['mlp_norm_w'][1], inp['mlp_norm_w'][2], inp['mlp_norm_w'][3],
            inp['final_norm_w']]
    shared['vecd'] = f(np.stack([_pvec(v) for v in vecs], axis=1))
    cw = np.asarray(inp['a_conv_w'], np.float32)
    shared['convw'] = f(cw.reshape(2, 4, 32, 128).transpose(3, 0, 2, 1))
    shared['convb'] = f(np.asarray(inp['a_conv_b'], np.float32).reshape(2, 32, 128).transpose(2, 0, 1))
    shared['gatew'] = f(np.asarray(inp['a_gate_norm_w'], np.float32).reshape(2, 16, 128).transpose(2, 0, 1))
    hv = np.stack([inp['a_dt_bias'], inp['a_log'], inp['a_d_skip']], axis=1)
    shared['hvec'] = f(np.broadcast_to(np.asarray(hv, np.float32)[None], (128, 2, 3, 32)))
    sk = np.asarray(inp['attn_sinks'], np.float32)
    shared['sinks'] = f(np.broadcast_to(sk[None], (128, 2, 16)))
    sr = sk.reshape(2, 4, 4)
    shared['sinkrows'] = f(np.repeat(sr.transpose(2, 0, 1), 8, axis=0))
    shared['metaT'] = f(np.asarray(inp['meta_tokens'], np.float32).T)
    shared['w_in'] = f(inp['a_in_proj'])
    shared['w_out'] = f(inp['a_out_proj'])
    shared['w_kv'] = f(inp['w_kv'])
    wq = np.asarray(inp['w_q'], np.float32).reshape(2, D, 2, 2, 4, 64)
    shared['w_q'] = f(wq.transpose(0, 1, 2, 4, 3, 5).reshape(2, D, D))
    wo = np.asarray(inp['w_o'], np.float32).reshape(2, 2, 2, 4, 64, D)
    shared['w_o'] = f(wo.transpose(0, 1, 3, 2, 4, 5).reshape(2, D, D))
    shared['w_up'] = f(inp['w_up'])
    shared['w_down'] = f(inp['w_down'])
    xp = np.asarray(inp['x_prompt'], np.float32)
    xs = np.asarray(inp['x_sample'], np.float32)
    sc = np.asarray(inp['state_conv'], np.float32)
    ss = np.asarray(inp['state_ssm'], np.float32)
    ck = np.asarray(inp['cache_k_win'], np.float32)
    cv = np.asarray(inp['cache_v_win'], np.float32)
    maps = []
    for c in range(N_CORES):
        m = dict(shared)
        m['xT'] = f(xp[c, :NTOK].T) if c < 2 else np.zeros((D, NTOK), np.float32)
        sl = slice(c * 16, (c + 1) * 16)
        m['xsT'] = f(xs[sl].reshape(128, D).T)
        m['sconvT'] = f(sc[:, sl].transpose(0, 3, 1, 2).reshape(2, CONV_DIM, 48))
        m['sssmT'] = f(ss[:, sl].reshape(2, 16, DI, 128).transpose(0, 1, 3, 2))
        m['ck'] = f(ck[sl].reshape(16, 128, 256))
        m['cv'] = f(cv[sl].reshape(16, 128, 256))
        m['ckT'] = f(ck[sl].reshape(16, 128, 256).transpose(0, 2, 1))
        maps.append(m)
    return maps


_PROGRAMS = {}


def get_program(NT, do_sample=True, stop_at=None):
    key = (NT, do_sample, stop_at)
    if key not in _PROGRAMS:
        _PROGRAMS[key] = Builder(NT, do_sample, stop_at).build()
    return _PROGRAMS[key]


def assemble(results, NT):
    NTOK = NT * 512
    y_prompt = np.stack([results[b]['ypT'].T for b in range(2)], axis=0)
    y_sample = np.concatenate([results[c]['ysT'].T.reshape(16, 8, D) for c in range(N_CORES)], axis=0)
    pconv = np.stack([results[b]['pconv'].transpose(1, 3, 2, 0).reshape(2, 3, CONV_DIM) for b in range(2)], axis=1)
    pssm = np.stack([results[b]['pssm'].transpose(0, 2, 1).reshape(2, NH, 64, 128) for b in range(2)], axis=1)
    pk = np.stack([results[b]['pk'].reshape(128, 4, 64) for b in range(2)], axis=0)
    pv = np.stack([results[b]['pv'].reshape(128, 4, 64) for b in range(2)], axis=0)
    sconv = np.concatenate(
        [results[c]['sconv_o'].transpose(1, 3, 4, 2, 0).reshape(2, 16, 3, CONV_DIM) for c in range(N_CORES)], axis=1)
    sssm = np.concatenate(
        [results[c]['sssm_o'].transpose(0, 1, 3, 2).reshape(2, 16, NH, 64, 128) for c in range(N_CORES)], axis=1)
    sk = np.concatenate([results[c]['sk_o'].reshape(16, 128, 4, 64) for c in range(N_CORES)], axis=0)
    sv = np.concatenate([results[c]['sv_o'].reshape(16, 128, 4, 64) for c in range(N_CORES)], axis=0)
    outs = (y_prompt, y_sample, pconv, pssm, pk, pv, sconv, sssm, sk, sv)
    return tuple(np.ascontiguousarray(o, dtype=np.float32) for o in outs)


def run(inputs, NT, do_sample=True, stop_at=None):
    nc = get_program(NT, do_sample, stop_at)
    maps = prepare_inputs(inputs, NT)
    res = run_bass_kernel_spmd(nc, maps, core_ids=list(range(N_CORES)))
    return assemble(res.results, NT)


def kernel(**inputs):
    return run(inputs, SEQ // 512)
```

```python
import numpy as np
import concourse.bass as bass
import concourse.mybir as mybir
from concourse.bass_utils import run_bass_kernel_spmd

F32 = mybir.dt.float32
BF16 = mybir.dt.bfloat16
AF = mybir.ActivationFunctionType
ALU = mybir.AluOpType
AX = mybir.AxisListType

D = 1024
DI = 2048
NH = 32
NG = 8
CONV_DIM = 4096
INP = 6176
DFF = 4096
EPS = 1e-5
NEG = -30000.0
SCALE = 64 ** -0.5
N_CORES = 8
SEQ = 8192


class Sched:
    def __init__(self, nc, n_slots=10):
        self.nc = nc
        self.engs = {'pe': nc.tensor, 'act': nc.scalar, 'dve': nc.vector,
                     'pool': nc.gpsimd, 'sp': nc.sync}
        self.ops = {e: [] for e in self.engs}
        self.sem = {e: nc.alloc_semaphore(name=f"sem_{e}") for e in self.engs}
        self.cnt = {e: 0 for e in self.engs}
        self.waited = {e: {} for e in self.engs}
        self.semobj = {}
        self.latest = {}
        for e in self.engs:
            self.semobj[('c', e)] = self.sem[e]
        self.slots = {}
        for q in ('sp', 'pool'):
            self.slots[q] = []
            for i in range(n_slots):
                s = nc.alloc_semaphore(name=f"dq_{q}_{i}")
                key = ('d', q, i)
                self.semobj[key] = s
                self.slots[q].append([key, 0])
        self.slot_rr = {q: 0 for q in self.slots}
        self.res = {}
        self.n_waits = 0

    def _deps(self, reads, writes):
        deps = []
        for r in reads:
            st = self.res.get(r)
            if st and st[0] is not None:
                deps.append(st[0])
        for w in writes:
            st = self.res.get(w)
            if st:
                if st[0] is not None:
                    deps.append(st[0])
                deps.extend(st[1])
        return deps

    def _emit_waits(self, eng, deps, skip_same_pe=True):
        wd = self.waited[eng]
        need = {}
        for (key, val) in deps:
            if skip_same_pe and eng == 'pe' and key == ('c', 'pe'):
                continue
            if wd.get(key, 0) >= val:
                continue
            if need.get(key, 0) < val:
                need[key] = val
        for key, val in need.items():
            wd[key] = val
            sem = self.semobj[key]
            e = self.engs[eng]
            self.ops[eng].append(lambda e=e, sem=sem, val=val: e.wait_ge(sem, val))
            self.n_waits += 1

    def _commit(self, tok, reads, writes):
        self.latest[tok[0]] = max(self.latest.get(tok[0], 0), tok[1])
        for r in reads:
            st = self.res.setdefault(r, [None, []])
            st[1].append(tok)
            if len(st[1]) > 24:
                best = {}
                for k, v in st[1]:
                    if best.get(k, 0) < v:
                        best[k] = v
                st[1] = list(best.items())
        for w in writes:
            self.res[w] = [tok, []]

    @staticmethod
    def _excl(reads, writes):
        ps = [r for r in reads if isinstance(r, tuple) and r and r[0] in ('pf', 'pb')]
        if not ps:
            return reads, writes
        return [r for r in reads if r not in ps], list(writes) + ps

    def op(self, eng, fn, reads=(), writes=()):
        reads, writes = self._excl(reads, writes)
        deps = self._deps(reads, writes)
        self._emit_waits(eng, deps)
        self.cnt[eng] += 1
        tok = (('c', eng), self.cnt[eng])
        sem = self.sem[eng]
        self.ops[eng].append(lambda fn=fn, sem=sem: fn().then_inc(sem, 1))
        self._commit(tok, reads, writes)
        return tok

    def dma(self, q, out, in_, reads=(), writes=()):
        deps = self._deps(reads, writes)
        i = self.slot_rr[q]
        self.slot_rr[q] = (i + 1) % len(self.slots[q])
        slot = self.slots[q][i]
        key = slot[0]
        if slot[1] > 0:
            deps.append((key, slot[1]))
        self._emit_waits(q, deps, skip_same_pe=False)
        slot[1] += 16
        tok = (key, slot[1])
        sem = self.semobj[key]
        e = self.engs[q]
        self.ops[q].append(lambda e=e, out=out, in_=in_, sem=sem: e.dma_start(out=out, in_=in_).then_inc(sem, 16))
        self._commit(tok, reads, writes)
        return tok

    def barrier(self):
        toks = list(self.latest.items())
        for e in self.engs:
            self._emit_waits(e, toks, skip_same_pe=False)
        self.res = {}

    def emit(self):
        nc = self.nc
        with nc.Block() as block:
            @block.tensor
            def _(e):
                for f in self.ops['pe']:
                    f()

            @block.scalar
            def _(e):
                for f in self.ops['act']:
                    f()

            @block.vector
            def _(e):
                for f in self.ops['dve']:
                    f()

            @block.gpsimd
            def _(e):
                for f in self.ops['pool']:
                    f()

            @block.sync
            def _(e):
                for f in self.ops['sp']:
                    f()


class Carver:
    def __init__(self, region_f32):
        self.r = region_f32
        self.n = region_f32.shape[1]
        self.pos = 0

    def take(self, dtype, *free):
        n = 1
        for f in free:
            n *= f
        nf32 = n if dtype == F32 else (n + 1) // 2
        assert self.pos + nf32 <= self.n, ("arena overflow", self.pos, nf32, self.n)
        v = self.r[:, self.pos:self.pos + nf32]
        self.pos += nf32
        if dtype != F32:
            v = v.bitcast(dtype)[:, 0:n]
        if len(free) == 2:
            v = v.rearrange("p (a b) -> p a b", a=free[0])
        elif len(free) == 3:
            v = v.rearrange("p (a b c) -> p a b c", a=free[0], b=free[1])
        elif len(free) == 4:
            v = v.rearrange("p (a b c d) -> p a b c d", a=free[0], b=free[1], c=free[2])
        return v


class StopBuild(Exception):
    pass


class Builder:
    def __init__(self, NT, do_sample=True, stop_at=None):
        self.NT = NT
        self.stop_at = stop_at
        self.marks = []
        self.do_sample = do_sample
        self.NTOK = NT * 512
        nc = self.nc = bass.Bass("TRN2", target_bir_lowering=False)
        self.S = Sched(nc)
        self._declare_dram()
        self._alloc()

    def _declare_dram(self):
        nc = self.nc

        def din(name, shape):
            return nc.dram_tensor(name, list(shape), F32, kind="ExternalInput").ap()

        def dout(name, shape):
            return nc.dram_tensor(name, list(shape), F32, kind="ExternalOutput").ap()

        NTOK = self.NTOK
        self.d_xT = din("xT", (D, NTOK))
        self.d_xsT = din("xsT", (D, 128))
        self.d_metaT = din("metaT", (D, 16))
        self.d_sconvT = din("sconvT", (2, CONV_DIM, 48))
        self.d_sssmT = din("sssmT", (2, 16, 128, DI))
        self.d_ck = din("ck", (16, 128, 256))
        self.d_cv = din("cv", (16, 128, 256))
        self.d_ckT = din("ckT", (16, 256, 128))
        self.d_w_in = din("w_in", (2, D, INP))
        self.d_w_out = din("w_out", (2, DI, D))
        self.d_w_kv = din("w_kv", (D, 512))
        self.d_w_q = din("w_q", (2, D, D))
        self.d_w_o = din("w_o", (2, D, D))
        self.d_w_up = din("w_up", (4, D, DFF))
        self.d_w_down = din("w_down", (4, DFF, D))
        self.d_vecd = din("vecd", (128, 10, 8))
        self.d_convw = din("convw", (128, 2, 32, 4))
        self.d_convb = din("convb", (128, 2, 32))
        self.d_gatew = din("gatew", (128, 2, 16))
        self.d_hvec = din("hvec", (128, 2, 3, 32))
        self.d_sinks = din("sinks", (128, 2, 16))
        self.d_sinkrows = din("sinkrows", (32, 2, 4))
        self.d_cm = din("cm", (128, 6, 128))
        self.d_identf = din("identf", (128, 128))
        self.d_onesf = din("onesf", (128, 128))
        self.d_seqmask = din("seqmask", (128, 16))
        self.d_amask = din("amask", (128, 2, 272))
        self.d_amask_s = din("amask_s", (32, 152))

        self.o_ypT = dout("ypT", (D, NTOK))
        self.o_ysT = dout("ysT", (D, 128))
        self.o_pconv = dout("pconv", (128, 2, 32, 3))
        self.o_pssm = dout("pssm", (2, 128, DI))
        self.o_pk = dout("pk", (128, 256))
        self.o_pv = dout("pv", (128, 256))
        self.o_sconv = dout("sconv_o", (128, 2, 32, 16, 3))
        self.o_sssm = dout("sssm_o", (2, 16, 128, DI))
        self.o_sk = dout("sk_o", (16, 128, 256))
        self.o_sv = dout("sv_o", (16, 128, 256))

    def _alloc(self):
        nc = self.nc
        total_f32 = 52100
        region = nc.alloc_sbuf_tensor("arena", [128, total_f32], F32)
        C = Carver(region[:, :])
        self.h = C.take(F32, 8, 512)
        self.xn = C.take(BF16, 8, 512)
        self.sq = C.take(BF16, 2, 512)
        self.rstd = C.take(F32, 512)
        self.big = C.take(BF16, 32, 512)
        self.ctail = C.take(F32, 2, 32, 3)
        self.zy = C.take(BF16, 4, 2048)
        self.hst = C.take(F32, 2, 2048)
        self.hbf = C.take(BF16, 2048)
        self.wbuf = C.take(BF16, 3, 4096)
        self.cacc = C.take(F32, 2, 512)
        self.kT_all = C.take(BF16, 2, 640)
        self.Vpad = C.take(BF16, 5, 4, 2, 128)
        self.kTm = C.take(BF16, 2, 16)
        self.Vpm = C.take(BF16, 4, 2, 128)
        self.vecd = C.take(F32, 10, 8)
        self.convw = C.take(F32, 2, 32, 4)
        self.convb = C.take(F32, 2, 32)
        self.gatew = C.take(F32, 2, 16)
        self.hvec = C.take(F32, 2, 3, 32)
        self.aneg = C.take(F32, 2, 32)
        self.sinks = C.take(F32, 2, 16)
        self.sinkrows = C.take(F32, 2, 4)
        self.cm = C.take(F32, 6, 128)
        self.identb = C.take(BF16, 128)
        self.onesb = C.take(BF16, 128)
        self.seqmask = C.take(F32, 16)
        self.amask = C.take(F32, 2, 272)
        self.amask_s = C.take(F32, 152)
        self.wdt = C.take(BF16, 2, 8, 32)
        self.sm = {}
        for nm in ('dtb', 'e1', 'dt', 'dtA', 'acs', 'e', 'dd', 'dte', 'cd', 'dtw'):
            self.sm[nm] = C.take(F32, 32)
        self.ss = C.take(F32, 8)
        self.rsg = C.take(F32, 8)
        self.att_small = C.take(F32, 4, 8)
        persistent_end = C.pos
        rest = region[:, persistent_end:total_f32]
        A = Carver(rest)
        self.xpre = A.take(F32, 2, 515)
        self.x_tm = A.take(BF16, 2048)
        self.xdt = A.take(BF16, 2048)
        self.xw = A.take(BF16, 2048)
        self.B_tm = A.take(BF16, 1024)
        self.Gm = A.take(BF16, 8, 128)
        self.Lg = A.take(F32, 3, 4, 128)
        self.ex = A.take(BF16, 3, 4, 128)
        self.LT = A.take(BF16, 3, 4, 128)
        self.t = A.take(F32, 2048)
        self.sqj = A.take(BF16, 3, 256)
        self.xd = A.take(F32, 3, 256)
        self.CTm = A.take(BF16, 8, 128)
        self.Bm = A.take(BF16, 8, 128)
        self.dtAm = A.take(F32, 16, 32)
        self.cdall = A.take(F32, 16, 32)
        ssd_end = A.pos
        self.gn = self.xdt
        self.kvo = self.cacc
        B = Carver(rest)
        self.qT = B.take(BF16, 8, 512)
        self.s_sb = B.take(F32, 4, 272)
        self.pe_sb = B.take(F32, 4, 272)
        self.pn = B.take(BF16, 4, 272)
        self.pT = B.take(BF16, 8, 384)
        self.pnpad = B.take(BF16, 2, 128)
        self.kTbuf = B.take(BF16, 16, 2, 128)
        self.qS = B.take(BF16, 2, 16, 32)
        att_end = B.pos
        self.sbuf_used_f32 = persistent_end + max(ssd_end, att_end)
        self.Vbuf = self.big.rearrange("p (b x) t -> p b (x t)", b=16).rearrange(
            "p b (k h d) -> p b k h d", k=4, h=2)
        self.pf = [nc.alloc_psum_tensor(f"pf{i}", [128, 512], F32) for i in range(6)]
        self.pbt = [nc.alloc_psum_tensor(f"pb{i}", [128, 1024], BF16) for i in range(2)]
        self.pf_rr = 0
        self.pb_rr = 0
        self.held = set()
        self.w_rr = 0

    def interleave(self, gens, W):
        from collections import deque
        active = deque()
        it = iter(gens)
        while True:
            while len(active) < W:
                try:
                    active.append(next(it))
                except StopIteration:
                    break
            if not active:
                break
            for _ in range(len(active)):
                g = active.popleft()
                try:
                    next(g)
                    active.append(g)
                except StopIteration:
                    pass

    def mark(self, name):
        self.marks.append(name)
        if self.stop_at is not None and name == self.stop_at:
            raise StopBuild()

    def bank(self, hold=False):
        assert len(self.held) < 6, "all PSUM banks held"
        while True:
            i = self.pf_rr
            self.pf_rr = (self.pf_rr + 1) % 6
            if i not in self.held:
                break
        if hold:
            self.held.add(i)
        return self.pf[i], ('pf', i), i

    def release(self, i):
        self.held.discard(i)

    def pbank(self):
        i = self.pb_rr
        self.pb_rr = 1 - i
        return self.pbt[i][:, 0:512], ('pb', i)

    def mm(self, out, lhsT, rhs, start, stop, rd, wr):
        nc = self.nc
        self.S.op('pe', lambda: nc.tensor.matmul(out, lhsT=lhsT, rhs=rhs, start=start, stop=stop), rd, wr)

    def tr(self, out, in_, ident, rd, wr):
        nc = self.nc
        self.S.op('pe', lambda: nc.tensor.transpose(out, in_, ident), rd, wr)

    def act(self, func, out, in_, rd, wr, bias=None, scale=None, accum_out=None):
        nc = self.nc
        kw = {}
        if bias is not None:
            kw['bias'] = bias
        if scale is not None:
            kw['scale'] = scale
        if accum_out is not None:
            kw['accum_out'] = accum_out
        self.S.op('act', lambda: nc.scalar.activation(out=out, in_=in_, func=func, **kw), rd, wr)

    def tt(self, eng, out, in0, in1, op, rd, wr):
        E = self.S.engs[eng]
        self.S.op(eng, lambda: E.tensor_tensor(out=out, in0=in0, in1=in1, op=op), rd, wr)

    def ts(self, eng, out, in0, s1, op0, rd, wr, s2=None, op1=None):
        E = self.S.engs[eng]
        if op1 is None:
            self.S.op(eng, lambda: E.tensor_scalar(out=out, in0=in0, scalar1=s1, scalar2=None, op0=op0), rd, wr)
        else:
            self.S.op(eng, lambda: E.tensor_scalar(out=out, in0=in0, scalar1=s1, scalar2=s2, op0=op0, op1=op1),
                      rd, wr)

    def stt(self, eng, out, in0, scalar, in1, op0, op1, rd, wr):
        E = self.S.engs[eng]
        self.S.op(eng, lambda: E.scalar_tensor_tensor(out=out, in0=in0, scalar=scalar, in1=in1, op0=op0, op1=op1),
                  rd, wr)

    def cp(self, eng, out, in_, rd, wr):
        if eng == 'act':
            nc = self.nc
            self.S.op('act', lambda: nc.scalar.copy(out=out, in_=in_), rd, wr)
        else:
            E = self.S.engs[eng]
            self.S.op(eng, lambda: E.tensor_copy(out=out, in_=in_), rd, wr)

    def memset(self, eng, ap, val, wr):
        E = self.S.engs[eng]
        self.S.op(eng, lambda: E.memset(ap, val), (), wr)

    def recip(self, out, in_, rd, wr):
        nc = self.nc
        self.S.op('dve', lambda: nc.vector.reciprocal(out=out, in_=in_), rd, wr)

    def make_plan(self, kinds):
        plan = []
        sids = {}

        def add(key, src, kc, ncol):
            if key not in sids:
                sids[key] = len(sids)
            plan.append((src, kc, ncol, sids[key]))

        def layer_mlp(layer):
            for half in range(2):
                for s in range(4):
                    add(('up', layer, half, s),
                        self.d_w_up[layer, :, half * 2048 + s * 512: half * 2048 + (s + 1) * 512], 8, 512)
                for s in range(4):
                    add(('down', layer, half, s),
                        self.d_w_down[layer, half * 2048:(half + 1) * 2048, s * 256:(s + 1) * 256], 16, 256)

        for kind in kinds:
            for layer in range(2):
                for s in range(8):
                    add(('xbc', layer, s), self.d_w_in[layer, :, 2048 + s * 512: 2048 + (s + 1) * 512], 8, 512)
                for s in range(4):
                    add(('z', layer, s), self.d_w_in[layer, :, s * 512:(s + 1) * 512], 8, 512)
                for s in range(4):
                    add(('out', layer, s), self.d_w_out[layer, :, s * 256:(s + 1) * 256], 16, 256)
                layer_mlp(layer)
            add(('kv',), self.d_w_kv[:, :], 8, 512)
            if kind != 'meta':
                for j in range(2):
                    for s in range(2):
                        add(('q', j, s), self.d_w_q[j, :, s * 512:(s + 1) * 512], 8, 512)
                    for s in range(2):
                        add(('o', j, s), self.d_w_o[j, :, s * 512:(s + 1) * 512], 8, 512)
                    layer_mlp(2 + j)
        self.plan = plan
        self.plan_issued = 0
        self.plan_pos = 0
        self.sid_done = set()
        self.wsc = self.nc.dram_tensor("wsc", [len(sids), 128, 4096], BF16).ap()

    def _issue_w(self, idx):
        src, kc, ncol, sid = self.plan[idx]
        b = idx % 3
        if sid not in self.sid_done:
            dst = self.wbuf[:, b, 0:kc * ncol].rearrange("p (k n) -> p k n", k=kc)
            self.S.dma('pool', dst, src.rearrange("(k p) n -> p k n", p=128), reads=(), writes=[('w', b)])
            self.S.dma('sp', self.wsc[sid], self.wbuf[:, b, :], reads=[('w', b)], writes=[('wsc', sid)])
            self.sid_done.add(sid)
        else:
            self.S.dma('pool', self.wbuf[:, b, :], self.wsc[sid], reads=[('wsc', sid)], writes=[('w', b)])

    def wnext(self):
        idx = self.plan_pos
        while self.plan_issued < min(len(self.plan), idx + 3):
            self._issue_w(self.plan_issued)
            self.plan_issued += 1
        src, kc, ncol, sid = self.plan[idx]
        b = idx % 3
        self.plan_pos += 1
        return self.wbuf[:, b, 0:kc * ncol].rearrange("p (k n) -> p k n", k=kc), ('w', b)

    def prologue(self):
        S = self.S
        loads = [
            (self.vecd, self.d_vecd), (self.convw, self.d_convw), (self.convb, self.d_convb),
            (self.gatew, self.d_gatew), (self.hvec, self.d_hvec), (self.sinks, self.d_sinks),
            (self.cm, self.d_cm), (self.seqmask, self.d_seqmask), (self.amask, self.d_amask),
        ]
        for dst, src in loads:
            S.dma('sp', dst, src, writes=['const'])
        S.dma('sp', self.sinkrows[0:32], self.d_sinkrows, writes=['const'])
        S.dma('sp', self.amask_s[0:32], self.d_amask_s, writes=['const'])
        S.dma('pool', self.identb, self.d_identf, writes=['const'])
        S.dma('pool', self.onesb, self.d_onesf, writes=['const'])
        for layer in range(2):
            S.dma('pool', self.wdt[:, layer], self.d_w_in[layer, :, 6144:6176].rearrange("(k p) n -> p k n", p=128),
                  writes=['const'])
        self.act(AF.Exp, self.aneg, self.hvec[:, :, 1, :], ['const'], ['aneg'])
        self.ts('dve', self.aneg, self.aneg, -1.0, ALU.mult, ['aneg'], ['aneg'])
        self.memset('pool', self.hst, 0.0, [('hst', 0), ('hst', 1)])
        self.memset('pool', self.ctail, 0.0, ['ctail_all'])
        self.memset('pool', self.Vpad, 0.0, ['vpad_all'])
        self.memset('pool', self.Vpm, 0.0, ['vpm'])
        self.memset('pool', self.kT_all, 0.0, ['kT_all'])
        S.barrier()

    def rmsnorm(self, T, v, out_f32=None):
        ps, pk, _ = self.bank()
        for blk in range(8):
            i = blk % 2
            self.act(AF.Square, self.sq[:, i, :T], self.h[:, blk, :T], [('h', blk)], [('sq', i)])
            self.mm(ps[:, :T], self.onesb[:, :], self.sq[:, i, :T], blk == 0, blk == 7, [('sq', i)], [pk])
        self.act(AF.Sqrt, self.rstd[:, :T], ps[:, :T], [pk], ['rstd'], bias=EPS, scale=1.0 / D)
        self.recip(self.rstd[:, :T], self.rstd[:, :T], ['rstd'], ['rstd'])
        if out_f32 is None:
            for blk in range(8):
                self.stt('dve', self.xn[:, blk, :T], self.h[:, blk, :T], self.vecd[:, v, blk:blk + 1],
                         self.rstd[:, :T], ALU.mult, ALU.mult, [('h', blk), 'rstd'], [('xn', blk)])

    def mlp(self, T, layer):
        self.rmsnorm(T, 5 + layer)
        for half in range(2):
            for s in range(4):
                wv, wk = self.wnext()
                for j in range(4):
                    jb = 4 * s + j
                    ps, pk, _ = self.bank()
                    for k in range(8):
                        self.mm(ps[:, :T], wv[:, k, j * 128:(j + 1) * 128], self.xn[:, k, :T], k == 0, k == 7,
                                [wk, ('xn', k)], [pk])
                    i = jb % 2
                    self.act(AF.Relu, self.cacc[:, i, :T], ps[:, :T], [pk], [('cacc', i)])
                    self.tt('dve', self.big[:, jb, :T], self.cacc[:, i, :T], self.cacc[:, i, :T], ALU.mult,
                            [('cacc', i)], [('big', jb)])
            for s in range(4):
                wv, wk = self.wnext()
                for j in range(2):
                    db = 2 * s + j
                    ps, pk, _ = self.bank()
                    for kc in range(16):
                        self.mm(ps[:, :T], wv[:, kc, j * 128:(j + 1) * 128], self.big[:, kc, :T], kc == 0, kc == 15,
                                [wk, ('big', kc)], [pk])
                    self.tt('dve', self.h[:, db, :T], self.h[:, db, :T], ps[:, :T], ALU.add,
                            [pk, ('h', db)], [('h', db)])

    def zyv(self, c, blk, L):
        z4 = self.zy.rearrange("p c (b t) -> p c b t", b=16)
        return z4[:, c, blk, 0:L]

    def ssd_layer(self, T, layer, kind):
        L = min(T, 128)
        nch = (T + 127) // 128
        sample = (kind == 'sample')
        self.rmsnorm(T, layer)
        self.mark(f'{kind}.ssd{layer}.norm')
        slabs = {}

        def conv_chain(blk):
            sidx, j = blk // 4, blk % 4
            if sidx not in slabs:
                slabs[sidx] = self.wnext()
            wv, wk = slabs[sidx]
            ps, pk, psi = self.bank(hold=True)
            for k in range(8):
                self.mm(ps[:, :T], wv[:, k, j * 128:(j + 1) * 128], self.xn[:, k, :T], k == 0, k == 7,
                        [wk, ('xn', k)], [pk])
            yield
            i = blk % 2
            cw = self.convw[:, layer, blk, :]
            cb = self.convb[:, layer, blk:blk + 1]
            if not sample:
                xp = self.xpre[:, i, :]
                self.cp('pool', xp[:, 0:3], self.ctail[:, layer, blk, :], [('ctail', layer, blk), 'ctail_all'],
                        [('xpre', i)])
                self.cp('act', xp[:, 3:3 + T], ps[:, :T], [pk], [('xpre', i)])
                self.act(AF.Identity, self.cacc[:, i, :T], ps[:, :T], [pk], [('cacc', i)],
                         bias=cb, scale=cw[:, 3:4])
                self.release(psi)
                yield
                for k in range(3):
                    self.stt('dve', self.cacc[:, i, :T], xp[:, k:k + T], cw[:, k:k + 1], self.cacc[:, i, :T],
                             ALU.mult, ALU.add, [('xpre', i), ('cacc', i)], [('cacc', i)])
                yield
                self.act(AF.Silu, self.big[:, blk, :T], self.cacc[:, i, :T], [('cacc', i)], [('big', blk)])
                self.cp('pool', self.ctail[:, layer, blk, :], xp[:, T:T + 3], [('xpre', i)],
                        [('ctail', layer, blk)])
            else:
                xp3 = self.xpre[:, i, 0:176].rearrange("p (b k) -> p b k", b=16)
                ps3 = ps[:, 0:128].rearrange("p (b t) -> p b t", b=16)
                ca3 = self.cacc[:, i, 0:128].rearrange("p (b t) -> p b t", b=16)
                self.S.dma('sp', xp3[:, :, 0:3],
                           self.d_sconvT[layer, blk * 128:(blk + 1) * 128, :].rearrange("p (b k) -> p b k", b=16),
                           reads=(), writes=[('xpre', i)])
                self.cp('act', xp3[:, :, 3:11], ps3, [pk], [('xpre', i)])
                self.act(AF.Identity, self.cacc[:, i, :128], ps[:, :128], [pk], [('cacc', i)],
                         bias=cb, scale=cw[:, 3:4])
                self.release(psi)
                yield
                for k in range(3):
                    self.stt('dve', ca3, xp3[:, :, k:k + 8], cw[:, k:k + 1], ca3,
                             ALU.mult, ALU.add, [('xpre', i), ('cacc', i)], [('cacc', i)])
                yield
                self.act(AF.Silu, self.big[:, blk, :128], self.cacc[:, i, :128], [('cacc', i)], [('big', blk)])
                self.S.dma('sp', self.o_sconv[:, layer, blk, :, :], xp3[:, :, 8:11],
                           reads=[('xpre', i)], writes=[('o_sconv', layer, blk)])

        self.interleave((conv_chain(blk) for blk in range(32)), 2)
        self.mark(f'{kind}.ssd{layer}.conv')
        for s in range(4):
            wv, wk = self.wnext()
            for ci in range(nch):
                ps, pk, _ = self.bank()
                for k in range(8):
                    self.mm(ps[:L, :512], self.xn[:, k, ci * 128:ci * 128 + L], wv[:, k, :], k == 0, k == 7,
                            [wk, ('xn', k)], [pk])
                self.act(AF.Silu, self.zy[:L, ci, s * 512:(s + 1) * 512], ps[:L, :512], [pk], [('zy', ci)])
        self.mark(f'{kind}.ssd{layer}.z')
        if not sample:
            self.cp('act', self.hbf[:, :], self.hst[:, layer, :], [('hst', layer)], ['hbf'])
        for ci in range(nch):
            self.ssd_chunk(layer, ci, L, sample)
            self.mark(f'{kind}.ssd{layer}.chunk{ci}')
        wo_plan = []
        for s in range(4):
            wv, wk = self.wnext()
            for j in range(2):
                db = 2 * s + j
                ps, pk, _ = self.bank()
                for c in range(nch):
                    for kc in range(16):
                        self.mm(ps[:, c * 128:c * 128 + L], wv[:, kc, j * 128:(j + 1) * 128], self.zyv(c, kc, L),
                                kc == 0, kc == 15, [wk, ('zy', c)], [pk])
                self.tt('dve', self.h[:, db, :T], self.h[:, db, :T], ps[:, :T], ALU.add, [pk, ('h', db)], [('h', db)])

    def ssd_chunk(self, layer, ci, L, sample):
        sm = self.sm
        c0 = ci * 128
        cols = slice(c0, c0 + L)
        TRI = self.cm[:, 3 if sample else 0, :]
        U = self.cm[:, 4 if sample else 1, :]
        ONES = self.cm[:, 5 if sample else 2, :]
        xnk = [('xn', k) for k in range(8)]
        ps, pk, _ = self.bank()
        for k in range(8):
            self.mm(ps[:L, 0:32], self.xn[:, k, cols], self.wdt[:, layer, k, :], k == 0, k == 7, [('xn', k)], [pk])
        self.tt('dve', sm['dtb'][:L], ps[:L, 0:32], self.hvec[:L, layer, 0, :], ALU.add, [pk], ['dtb'])
        self.act(AF.Exp, sm['e1'][:L], sm['dtb'][:L], ['dtb'], ['e1'])
        self.act(AF.Ln, sm['dt'][:L], sm['e1'][:L], ['e1'], ['dt'], bias=1.0)
        self.tt('dve', sm['dtA'][:L], sm['dt'][:L], self.aneg[:L, layer, :], ALU.mult, ['dt', 'aneg'], ['dtA'])
        self.mark(f'c{layer}.{ci}.dt')
        ps, pk, _ = self.bank()
        self.mm(ps[:L, 0:32], TRI[:L, :L], sm['dtA'][:L], True, True, ['dtA'], [pk])
        self.mm(ps[:, 32:64], ONES[:L, :], sm['dtA'][:L], True, True, ['dtA'], [pk])
        self.cp('act', sm['acs'][:L], ps[:L, 0:32], [pk], ['acs'])
        self.act(AF.Exp, sm['e'][:L], ps[:L, 0:32], [pk], ['e'])
        self.act(AF.Exp, sm['cd'][:, :], ps[:, 32:64], [pk], ['cd'])
        self.tt('dve', sm['dd'][:L], ps[:L, 32:64], sm['acs'][:L], ALU.subtract, [pk, 'acs'], ['dd'])
        self.act(AF.Exp, sm['dte'][:L], sm['dd'][:L], ['dd'], ['dte'])
        self.tt('dve', sm['dtw'][:L], sm['dt'][:L], sm['dte'][:L], ALU.mult, ['dt', 'dte'], ['dtw'])
        self.mark(f'c{layer}.{ci}.cum')
        def x_chain(q):
            pt, ptk = self.pbank()
            for j in range(4):
                blk = 4 * q + j
                self.tr(pt[:L, j * 128:(j + 1) * 128], self.big[:, blk, cols], self.identb[:, :], [('big', blk)], [ptk])
            yield
            self.cp('act', self.x_tm[:L, q * 512:(q + 1) * 512], pt[:L, :512], [ptk], ['x_tm'])
            pt3 = pt[:L, :512].rearrange("p (h d) -> p h d", h=8)
            self.tt('dve', self.xdt[:L, q * 512:(q + 1) * 512].rearrange("p (h d) -> p h d", h=8), pt3,
                    sm['dt'][:L, 8 * q:8 * q + 8].unsqueeze(2).broadcast_to([L, 8, 64]), ALU.mult,
                    [ptk, 'dt'], ['xdt'])
            self.tt('dve', self.xw[:L, q * 512:(q + 1) * 512].rearrange("p (h d) -> p h d", h=8), pt3,
                    sm['dtw'][:L, 8 * q:8 * q + 8].unsqueeze(2).broadcast_to([L, 8, 64]), ALU.mult,
                    [ptk, 'dtw'], ['xw'])

        def b_chain(q):
            pt, ptk = self.pbank()
            for j in range(4):
                blk = 16 + 4 * q + j
                self.tr(pt[:L, j * 128:(j + 1) * 128], self.big[:, blk, cols], self.identb[:, :], [('big', blk)], [ptk])
            yield
            self.cp('act', self.B_tm[:L, q * 512:(q + 1) * 512], pt[:L, :512], [ptk], ['B_tm'])

        self.interleave([x_chain(0), x_chain(1), b_chain(0), x_chain(2), x_chain(3), b_chain(1)], 2)
        self.mark(f'c{layer}.{ci}.tr')
        for half in range(2):
            ps, pk, _ = self.bank()
            for j in range(4):
                g = 4 * half + j
                self.mm(ps[:L, j * 128:j * 128 + L], self.big[:, 16 + g, cols], self.big[:, 24 + g, cols], True, True,
                        [('big', 16 + g), ('big', 24 + g)], [pk])
            self.tt('dve', self.Gm[:L, 4 * half:4 * half + 4, :L],
                    ps[:L, :].rearrange("p (j l) -> p j l", j=4)[:, :, :L],
                    TRI[:L, :L].unsqueeze(1).broadcast_to([L, 4, L]), ALU.mult, [pk], ['Gm'])
        held = []
        if sample:
            offb = []
            for q in range(4):
                ps, pk, bi = self.bank(hold=True)
                offb.append((ps, pk, bi))
            self.memset('pool', self.CTm[:, :, :], 0.0, ['CTm'])
            self.tt('dve', self.dtAm[:, :, :], sm['dtA'][:, :].unsqueeze(1).broadcast_to([128, 16, 32]),
                    self.seqmask[:, :].unsqueeze(2).broadcast_to([128, 16, 32]), ALU.mult, ['dtA'], ['dtAm'])
            ps, pk, _ = self.bank()
            self.mm(ps[:, :512], self.cm[:, 2, :], self.dtAm.rearrange("p b h -> p (b h)"), True, True, ['dtAm'], [pk])
            self.act(AF.Exp, self.cdall.rearrange("p b h -> p (b h)"), ps[:, :512], [pk], ['cdall'])
            for b in range(16):
                i = b % 2
                st = self.hst[:, i, :]
                self.S.dma('sp', st, self.d_sssmT[layer, b], reads=(), writes=[('hst', i)])
                self.cp('act', self.hbf[:, :], st, [('hst', i)], ['hbf'])
                bc8 = slice(b * 8, b * 8 + 8)
                self.cp('pool', self.CTm[:, :, bc8], self.big[:, 24:32, bc8],
                        [('big', 24 + g) for g in range(8)], ['CTm'])
                self.ts('pool', self.Bm.rearrange("p g n -> p (g n)"), self.B_tm[:, :],
                        self.seqmask[:, b:b + 1], ALU.mult, ['B_tm'], ['Bm'])
                for g in range(8):
                    ps, pk, _bi = offb[g // 2]
                    self.mm(ps[:, (g % 2) * 256:(g % 2) * 256 + 256], self.CTm[:, g, :],
                            self.hbf[:, g * 256:(g + 1) * 256], b == 0 and g % 2 == 0, b == 15, ['CTm', 'hbf'], [pk])
                self.memset('pool', self.CTm[:, :, bc8], 0.0, ['CTm'])
                for q in range(4):
                    ps, pk, _ = self.bank()
                    for gg in range(2):
                        g = 2 * q + gg
                        self.mm(ps[:, gg * 256:(gg + 1) * 256], self.Bm[:, g, :], self.xw[:, g * 256:(g + 1) * 256],
                                True, True, ['Bm', 'xw'], [pk])
                    stq = st[:, q * 512:(q + 1) * 512].rearrange("p (h d) -> p h d", h=8)
                    self.tt('dve', stq, stq,
                            self.cdall[:, b, 8 * q:8 * q + 8].unsqueeze(2).broadcast_to([128, 8, 64]), ALU.mult,
                            [('hst', i), 'cdall'], [('hst', i)])
                    self.tt('dve', st[:, q * 512:(q + 1) * 512], st[:, q * 512:(q + 1) * 512], ps[:, :512], ALU.add,
                            [pk, ('hst', i)], [('hst', i)])
                self.S.dma('sp', self.o_sssm[layer, b], st, reads=[('hst', i)], writes=[('o_sssm', layer, b)])
        self.mark(f'c{layer}.{ci}.G')
        def group_chain(g):
            i = g % 3
            self.tt('pool', self.Lg[:L, i, :, :L], U[:L, :L].unsqueeze(1).broadcast_to([L, 4, L]),
                    sm['dtA'][:L, 4 * g:4 * g + 4].unsqueeze(2).broadcast_to([L, 4, L]), ALU.mult,
                    ['dtA'], [('Lg', i)])
            yield
            ps, pk, psi = self.bank(hold=True)
            for j in range(4):
                self.mm(ps[:L, j * 128:j * 128 + L], self.Lg[:L, i, j, :L], TRI[:L, :L], True, True, [('Lg', i)], [pk])
            yield
            self.act(AF.Exp, self.ex[:L, i, :, :L], ps[:L, :].rearrange("p (j l) -> p j l", j=4)[:, :, :L],
                     [pk], [('ex', i)])
            self.release(psi)
            yield
            self.tt('dve', self.LT[:L, i, :, :L], self.ex[:L, i, :, :L],
                    self.Gm[:L, g, :L].unsqueeze(1).broadcast_to([L, 4, L]), ALU.mult,
                    [('ex', i), 'Gm'], [('LT', i)])
            yield
            psd, pdk, psdi = self.bank(hold=True)
            for j in range(4):
                hh = 4 * g + j
                self.mm(psd[:L, j * 64:(j + 1) * 64], self.LT[:L, i, j, :L], self.xdt[:L, hh * 64:(hh + 1) * 64],
                        True, True, [('LT', i), 'xdt'], [pdk])
            psoi = None
            if sample:
                pso, pok, _bi = offb[g // 2]
                off_ap = pso[:L, (g % 2) * 256:(g % 2) * 256 + 256]
            else:
                pso, pok, psoi = self.bank(hold=True)
                self.mm(pso[:L, 0:256], self.big[:, 24 + g, cols], self.hbf[:, g * 256:(g + 1) * 256], True, True,
                        [('big', 24 + g), 'hbf'], [pok])
                off_ap = pso[:L, 0:256]
            yield
            tg = self.t[:L, g * 256:(g + 1) * 256]
            self.cp('act', tg, psd[:L, 0:256], [pdk], [('t', g)])
            self.release(psdi)
            self.tt('pool', self.xd[:L, i, :].rearrange("p (h d) -> p h d", h=4),
                    self.x_tm[:L, g * 256:(g + 1) * 256].rearrange("p (h d) -> p h d", h=4),
                    self.hvec[:L, layer, 2, 4 * g:4 * g + 4].unsqueeze(2).broadcast_to([L, 4, 64]), ALU.mult,
                    ['x_tm'], [('xd', i)])
            yield
            for j in range(4):
                hh = 4 * g + j
                self.stt('dve', tg[:, j * 64:(j + 1) * 64], off_ap[:, j * 64:(j + 1) * 64], sm['e'][:L, hh:hh + 1],
                         tg[:, j * 64:(j + 1) * 64], ALU.mult, ALU.add, [pok, 'e', ('t', g)], [('t', g)])
            if psoi is not None:
                self.release(psoi)
            yield
            self.tt('pool', tg, tg, self.xd[:L, i, :], ALU.add, [('xd', i), ('t', g)], [('t', g)])
            self.tt('pool', tg, tg, self.zy[:L, ci, g * 256:(g + 1) * 256], ALU.mult, [('t', g), ('zy', ci)],
                    [('t', g)])
            yield
            self.act(AF.Square, self.sqj[:L, i, :], tg, [('t', g)], [('sqj', i), ('ss', g)],
                     accum_out=self.ss[:L, g:g + 1])

        def state_chain(q):
            st = self.hst[:, layer, :]
            ps, pk, psi = self.bank(hold=True)
            for gg in range(2):
                g = 2 * q + gg
                self.mm(ps[:, gg * 256:(gg + 1) * 256], self.B_tm[:L, g * 128:(g + 1) * 128],
                        self.xw[:L, g * 256:(g + 1) * 256], True, True, ['B_tm', 'xw'], [pk])
            yield
            stq = st[:, q * 512:(q + 1) * 512].rearrange("p (h d) -> p h d", h=8)
            self.tt('dve', stq, stq, sm['cd'][:, 8 * q:8 * q + 8].unsqueeze(2).broadcast_to([128, 8, 64]),
                    ALU.mult, [('hst', layer), 'cd'], [('hst', layer)])
            yield
            self.tt('dve', st[:, q * 512:(q + 1) * 512], st[:, q * 512:(q + 1) * 512], ps[:, :512], ALU.add,
                    [pk, ('hst', layer)], [('hst', layer)])
            self.release(psi)

        chains = [group_chain(g) for g in range(8)]
        if not sample:
            chains = chains[:4] + [state_chain(0), state_chain(1)] + chains[4:] + [state_chain(2), state_chain(3)]
        self.interleave(chains, 1 if sample else 3)
        if sample:
            for (ps, pk, bi) in offb:
                self.release(bi)
        self.mark(f'c{layer}.{ci}.groups')
        tkeys = [('t', g) for g in range(8)]
        sskeys = [('ss', g) for g in range(8)]
        self.act(AF.Sqrt, self.rsg[:L, :], self.ss[:L, :], sskeys, ['rsg'], bias=EPS, scale=1.0 / 256)
        self.recip(self.rsg[:L, :], self.rsg[:L, :], ['rsg'], ['rsg'])
        z4 = self.zy.rearrange("p c (b t) -> p c b t", b=16)

        def yn_chain(q):
            self.tt('dve', self.gn[:L, q * 512:(q + 1) * 512].rearrange("p (g c) -> p g c", g=2),
                    self.t[:L, q * 512:(q + 1) * 512].rearrange("p (g c) -> p g c", g=2),
                    self.rsg[:L, 2 * q:2 * q + 2].unsqueeze(2).broadcast_to([L, 2, 256]), ALU.mult,
                    [('t', 2 * q), ('t', 2 * q + 1), 'rsg'], ['xdt'])
            yield
            pt, ptk = self.pbank()
            for j in range(4):
                blk = 4 * q + j
                self.tr(pt[:, j * 128:j * 128 + L], self.gn[:L, blk * 128:(blk + 1) * 128], self.identb[:L, :L],
                        ['xdt'], [ptk])
            self.tt('dve', z4[:, ci, 4 * q:4 * q + 4, 0:L],
                    pt[:, :512].rearrange("p (j l) -> p j l", j=4)[:, :, :L],
                    self.gatew[:, layer, 4 * q:4 * q + 4].unsqueeze(2).broadcast_to([128, 4, L]), ALU.mult,
                    [ptk], [('zy', ci)])

        self.interleave([yn_chain(q) for q in range(4)], 2)
        self.mark(f'c{layer}.{ci}.yn')
        if not sample:
            self.cp('act', self.hbf[:, :], self.hst[:, layer, :], [('hst', layer)], ['hbf'])

    def kv_proj(self, T, kind, last):
        L = min(T, 128)
        nch = (T + 127) // 128
        self.rmsnorm(T, 2)
        wv, wk = self.wnext()
        for kp in range(2):
            ps, pk, _ = self.bank()
            for k in range(8):
                self.mm(ps[:, :T], wv[:, k, kp * 128:(kp + 1) * 128], self.xn[:, k, :T], k == 0, k == 7,
                        [wk, ('xn', k)], [pk])
            if kind == 'meta':
                self.cp('act', self.kTm[:, kp, :T], ps[:, :T], [pk], ['kTm'])
            else:
                self.cp('act', self.kT_all[:, kp, 128:128 + T], ps[:, :T], [pk], ['kT_all'])
        for ci in range(nch):
            cols = slice(ci * 128, ci * 128 + L)
            ps, pk, _ = self.bank()
            for k in range(8):
                self.mm(ps[:L, 0:256], self.xn[:, k, cols], wv[:, k, 256:512], k == 0, k == 7, [wk, ('xn', k)], [pk])
            ps4 = ps[:L, 0:256].rearrange("p (k d) -> p k d", k=4)
            if kind == 'meta':
                self.cp('act', self.Vpm[:L, :, 0, 0:64], ps4, [pk], ['vpm'])
                self.cp('dve', self.Vpm[:L, :, 1, 64:128], ps4, [pk], ['vpm'])
            else:
                self.cp('act', self.Vpad[:L, ci + 1, :, 0, 0:64], ps4, [pk], [('vpad', ci + 1), 'vpad_all'])
                self.cp('dve', self.Vpad[:L, ci + 1, :, 1, 64:128], ps4, [pk], [('vpad', ci + 1), 'vpad_all'])
            want_out = (kind == 'sample') or (kind == 'prompt' and last and ci == nch - 1)
            if want_out:
                ps2, pk2, _ = self.bank()
                for k in range(8):
                    self.mm(ps2[:L, 0:256], self.xn[:, k, cols], wv[:, k, 0:256], k == 0, k == 7, [wk, ('xn', k)],
                            [pk2])
                self.cp('act', self.kvo[:, 0, 0:256], ps2[:, 0:256], [pk2], [('cacc', 0)])
                self.cp('dve', self.kvo[:, 1, 0:256], ps[:, 0:256], [pk], [('cacc', 1)])
                if kind == 'prompt':
                    self.S.dma('sp', self.o_pk, self.kvo[:, 0, 0:256], reads=[('cacc', 0)], writes=['o_pk'])
                    self.S.dma('sp', self.o_pv, self.kvo[:, 1, 0:256], reads=[('cacc', 1)], writes=['o_pv'])
                else:
                    for b in range(16):
                        self.S.dma('sp', self.o_sk[b, 120:128, :], self.kvo[b * 8:(b + 1) * 8, 0, 0:256],
                                   reads=[('cacc', 0)], writes=[('o_sk_new', b)])
                        self.S.dma('sp', self.o_sv[b, 120:128, :], self.kvo[b * 8:(b + 1) * 8, 1, 0:256],
                                   reads=[('cacc', 1)], writes=[('o_sv_new', b)])

    def q_proj(self, T, j):
        self.rmsnorm(T, 3 + j)
        for s in range(2):
            wv, wk = self.wnext()
            for jj in range(4):
                m = 4 * s + jj
                ps, pk, _ = self.bank()
                for k in range(8):
                    self.mm(ps[:, :T], wv[:, k, jj * 128:(jj + 1) * 128], self.xn[:, k, :T], k == 0, k == 7,
                            [wk, ('xn', k)], [pk])
                self.cp('act', self.qT[:, m, :T], ps[:, :T], [pk], [('qT', m)])

    def softmax_rows(self, M, N, i, sc, sck, maskap, sink_ap):
        a = self.att_small
        mx, negm, rsum, es, den, rden = (a[:M, i, c:c + 1] for c in range(6))
        ak = ('asm', i)
        self.stt('dve', self.s_sb[:M, i, :N], sc, SCALE, maskap, ALU.mult, ALU.add, [sck], [('s_sb', i)])
        yield
        nc = self.nc
        s_in = self.s_sb[:M, i, :N]
        self.S.op('dve', lambda: nc.vector.reduce_max(out=mx, in_=s_in, axis=AX.X), [('s_sb', i)], [ak])
        self.tt('dve', mx, mx, sink_ap, ALU.max, [ak], [ak])
        self.ts('dve', negm, mx, -1.0, ALU.mult, [ak], [ak])
        yield
        self.act(AF.Exp, self.pe_sb[:M, i, :N], self.s_sb[:M, i, :N], [('s_sb', i), ak], [('pe', i), ('rs', i)],
                 bias=negm, accum_out=rsum)
        self.act(AF.Exp, es, sink_ap, [ak], [('es', i)], bias=negm)
        yield
        self.tt('dve', den, rsum, es, ALU.add, [('rs', i), ('es', i)], [('den', i)])
        self.recip(rden, den, [('den', i)], [('den', i)])
        self.ts('dve', self.pn[:M, i, :N], self.pe_sb[:M, i, :N], rden, ALU.mult, [('pe', i), ('den', i)],
                [('pn', i)])
        yield

    def attn_prompt(self, T, j, first_tile):
        nch = T // 128
        z4 = self.zy.rearrange("p c (b t) -> p c b t", b=16)
        pvs = {}
        counter = [0]

        def head_chain(ci, m, half):
            cols = slice(ci * 128, ci * 128 + 128)
            which = 1 if (first_tile and ci == 0) else 0
            kp, r = m // 4, m % 4
            kvh = 2 * kp + half
            head = kvh * 4 + r
            it = counter[0]
            counter[0] += 1
            i = it % 4
            i4 = it % 8
            hs = slice(half * 64, half * 64 + 64)
            sc, sck, sci = self.bank(hold=True)
            q = self.qT[hs, m, cols]
            self.mm(sc[:, 0:16], q, self.kTm[hs, kp, 0:16], True, True, [('qT', m), 'kTm'], [sck])
            self.mm(sc[:, 16:272], q, self.kT_all[hs, kp, ci * 128:ci * 128 + 256], True, True,
                    [('qT', m), 'kT_all'], [sck])
            yield
            first = True
            for _ in self.softmax_rows(128, 272, i, sc[:, 0:272], sck, self.amask[:, which, :],
                                       self.sinks[:, j, head:head + 1]):
                if first:
                    self.release(sci)
                    first = False
                yield
            pt, ptk = self.pbank()
            self.tr(pt[0:16, 0:128], self.pn[:, i, 0:16], self.identb[:, :], [('pn', i)], [ptk])
            self.tr(pt[:, 128:256], self.pn[:, i, 16:144], self.identb[:, :], [('pn', i)], [ptk])
            self.tr(pt[:, 256:384], self.pn[:, i, 144:272], self.identb[:, :], [('pn', i)], [ptk])
            self.cp('act', self.pT[:, i4, :], pt[:, 0:384], [ptk], [('pT', i4)])
            yield
            if (ci, m) not in pvs:
                pv, pvk, pvi = self.bank(hold=True)
                pvs[(ci, m)] = [pv, pvk, 0, pvi]
            ent = pvs[(ci, m)]
            pv, pvk = ent[0], ent[1]
            segs = [(self.Vpm[0:16, kvh, half, :], self.pT[0:16, i4, 0:128], ['vpm']),
                    (self.Vpad[:, ci, kvh, half, :], self.pT[:, i4, 128:256], [('vpad', ci), 'vpad_all']),
                    (self.Vpad[:, ci + 1, kvh, half, :], self.pT[:, i4, 256:384], [('vpad', ci + 1), 'vpad_all'])]
            for lh, rh, keys in segs:
                self.mm(pv[:, 0:128], lh, rh, ent[2] == 0, ent[2] == 5, keys + [('pT', i4)], [pvk])
                ent[2] += 1
            if ent[2] == 6:
                yield
                self.cp('act', z4[:, ci, m, 0:128], pv[:, 0:128], [pvk], [('zy', ci)])
                self.release(ent[3])

        gens = (head_chain(ci, m, half) for ci in range(nch) for m in range(8) for half in range(2))
        self.interleave(gens, 4)

    def attn_sample(self, j):
        z4 = self.zy.rearrange("p c (b t) -> p c b t", b=16)
        it = 0
        for kp in range(2):
            self.cp('pool', self.qS[:, kp].rearrange("p b (r t) -> p b r t", r=4),
                    self.qT[:, kp * 4:kp * 4 + 4, 0:128].rearrange("p r (b t) -> p b r t", b=16),
                    [('qT', kp * 4 + r) for r in range(4)], ['qS'])
        for b in range(16):
            bc = slice(b * 8, b * 8 + 8)
            for kp in range(2):
                pv, pvk, _ = self.bank()
                used = []
                for half in range(2):
                    kvh = 2 * kp + half
                    i = it % 2
                    i4 = it % 4
                    it += 1
                    used.append(i4)
                    hs = slice(half * 64, half * 64 + 64)
                    sc, sck, _ = self.bank()
                    q = self.qS[hs, kp, b, :]
                    self.mm(sc[:32, 0:16], q, self.kTm[hs, kp, 0:16], True, True, ['qS', 'kTm'], [sck])
                    self.mm(sc[:32, 16:144], q, self.kTbuf[hs, b, kp, :], True, True, ['kTbuf'], [sck])
                    self.mm(sc[:32, 144:152], q, self.kT_all[hs, kp, 128 + b * 8:128 + b * 8 + 8], True, True,
                            ['kT_all'], [sck])
                    for _ in self.softmax_rows(32, 152, i, sc[:32, 0:152], sck, self.amask_s[0:32, :],
                                               self.sinkrows[0:32, j, kvh:kvh + 1]):
                        pass
                    self.cp('pool', self.pnpad[:32, i, bc], self.pn[:32, i, 144:152], [('pn', i)], [('pnpad', i)])
                    pt, ptk = self.pbank()
                    self.tr(pt[0:16, 0:32], self.pn[:32, i, 0:16], self.identb[:32, :32], [('pn', i)], [ptk])
                    self.tr(pt[:, 32:64], self.pn[:32, i, 16:144], self.identb[:32, :32], [('pn', i)], [ptk])
                    self.tr(pt[:, 64:96], self.pnpad[:32, i, :], self.identb[:32, :32], [('pnpad', i)], [ptk])
                    self.cp('act', self.pT[:, i4, 0:96], pt[:, 0:96], [ptk], [('pT', i4)])
                    self.memset('pool', self.pnpad[:32, i, bc], 0.0, [('pnpad', i)])
                for r in range(4):
                    rc = slice(r * 8, r * 8 + 8)
                    for half in range(2):
                        kvh = 2 * kp + half
                        i4 = used[half]
                        self.mm(pv[:, rc], self.Vpm[0:16, kvh, half, :], self.pT[0:16, i4, r * 8:r * 8 + 8],
                                half == 0, False, ['vpm', ('pT', i4)], [pvk])
                        self.mm(pv[:, rc], self.Vbuf[:, b, kvh, half, :], self.pT[:, i4, 32 + r * 8:32 + r * 8 + 8],
                                False, False, [('big', 2 * b), ('big', 2 * b + 1), ('pT', i4)], [pvk])
                        self.mm(pv[:, rc], self.Vpad[:, 1, kvh, half, :], self.pT[:, i4, 64 + r * 8:64 + r * 8 + 8],
                                False, half == 1, [('vpad', 1), 'vpad_all', ('pT', i4)], [pvk])
                self.cp('act', z4[:, 0, kp * 4:kp * 4 + 4, bc],
                        pv[:, 0:32].rearrange("p (r t) -> p r t", r=4), [pvk], [('zy', 0)])

    def attn_layer(self, T, j, kind, first_tile):
        nch = (T + 127) // 128
        self.q_proj(T, j)
        if kind == 'prompt':
            self.attn_prompt(T, j, first_tile)
        else:
            self.load_vbuf()
            self.attn_sample(j)
        for s in range(2):
            wv, wk = self.wnext()
            for jj in range(4):
                db = 4 * s + jj
                ps, pk, _ = self.bank()
                for c in range(nch):
                    for mc in range(8):
                        self.mm(ps[:, c * 128:(c + 1) * 128], wv[:, mc, jj * 128:(jj + 1) * 128],
                                self.zyv(c, mc, 128), mc == 0, mc == 7, [wk, ('zy', c)], [pk])
                self.tt('dve', self.h[:, db, :T], self.h[:, db, :T], ps[:, :T], ALU.add, [pk, ('h', db)], [('h', db)])

    def final_out(self, T, dst):
        self.rmsnorm(T, 9, out_f32=True)
        for blk in range(8):
            i = blk % 2
            self.stt('dve', self.cacc[:, i, :T], self.h[:, blk, :T], self.vecd[:, 9, blk:blk + 1],
                     self.rstd[:, :T], ALU.mult, ALU.mult, [('h', blk), 'rstd'], [('cacc', i)])
            self.S.dma('sp', dst[blk * 128:(blk + 1) * 128, :], self.cacc[:, i, :T], reads=[('cacc', i)],
                       writes=[('o_y', blk, id(dst))])

    def run_tile(self, kind, T, src, dst, first_tile=False, last=False):
        S = self.S
        S.dma('sp', self.h[:, :, :T], src.rearrange("(b p) t -> p b t", p=128), reads=(),
              writes=[('h', b) for b in range(8)])
        for layer in range(2):
            self.ssd_layer(T, layer, kind)
            self.mark(f'{kind}.ssd{layer}.out')
            self.mlp(T, layer)
            self.mark(f'{kind}.mlp{layer}')
        self.kv_proj(T, kind, last)
        self.mark(f'{kind}.kv')
        if kind == 'meta':
            return
        S.barrier()
        if kind == 'sample':
            self.load_sample_cache()
        for j in range(2):
            self.attn_layer(T, j, kind, first_tile)
            self.mark(f'{kind}.attn{j}')
            self.mlp(T, 2 + j)
            self.mark(f'{kind}.mlp{2 + j}')
        self.final_out(T, dst)
        self.mark(f'{kind}.final')
        if kind == 'prompt':
            nch = T // 128
            self.cp('pool', self.kT_all[:, :, 0:128], self.kT_all[:, :, T:T + 128], ['kT_all'], ['kT_all'])
            self.cp('pool', self.Vpad[:, 0], self.Vpad[:, nch], [('vpad', nch), 'vpad_all'], [('vpad', 0), 'vpad_all'])
        S.barrier()

    def load_sample_cache(self):
        S = self.S
        self.memset('pool', self.pnpad[:, :, :], 0.0, [('pnpad', 0), ('pnpad', 1)])
        for b in range(16):
            S.dma('pool', self.kTbuf[:, b, :, :], self.d_ckT[b].rearrange("(k p) w -> p k w", p=128), reads=(),
                  writes=['kTbuf'])
        S.dma('sp', self.o_sk[:, 0:120, :], self.d_ck[:, 8:128, :], reads=(), writes=['o_sk_old'])
        S.dma('sp', self.o_sv[:, 0:120, :], self.d_cv[:, 8:128, :], reads=(), writes=['o_sv_old'])

    def load_vbuf(self):
        S = self.S
        bigk = [('big', b) for b in range(32)]
        self.memset('pool', self.big[:, :, :], 0.0, bigk)
        for b in range(16):
            v4 = self.d_cv[b].rearrange("w (k d) -> w k d", k=4)
            S.dma('pool', self.Vbuf[:, b, :, 0, 0:64], v4, reads=(), writes=[('big', 2 * b), ('big', 2 * b + 1)])
            S.dma('pool', self.Vbuf[:, b, :, 1, 64:128], v4, reads=(), writes=[('big', 2 * b), ('big', 2 * b + 1)])

    def build(self):
        S = self.S
        kinds = ['meta'] + ['prompt'] * self.NT + (['sample'] if self.do_sample else [])
        self.make_plan(kinds)
        try:
            self._build_body()
        except StopBuild:
            pass
        S.barrier()
        S.emit()
        return self.nc

    def _build_body(self):
        S = self.S
        self.prologue()
        self.mark('prologue')
        self.run_tile('meta', 16, self.d_metaT, None)
        S.barrier()
        for ti in range(self.NT):
            self.run_tile('prompt', 512, self.d_xT[:, ti * 512:(ti + 1) * 512],
                          self.o_ypT[:, ti * 512:(ti + 1) * 512], first_tile=(ti == 0), last=(ti == self.NT - 1))
        S.dma('sp', self.o_pconv, self.ctail, reads=[('ctail', l, b) for l in range(2) for b in range(32)],
              writes=['o_pconv'])
        for layer in range(2):
            S.dma('sp', self.o_pssm[layer], self.hst[:, layer, :], reads=[('hst', layer)], writes=[('o_pssm', layer)])
        S.barrier()
        if self.do_sample:
            self.run_tile('sample', 128, self.d_xsT, self.o_ysT)


def _consts():
    k = np.arange(128)
    seq = k // 8
    tri = (k[:, None] <= k[None, :]).astype(np.float32)
    U = (k[:, None] > k[None, :]).astype(np.float32)
    ones = np.ones((128, 128), np.float32)
    same = (seq[:, None] == seq[None, :]).astype(np.float32)
    cm = np.stack([tri, U, ones, tri * same, U * same, same], axis=1)
    seqmask = (seq[:, None] == np.arange(16)[None, :]).astype(np.float32)
    qi = k[:, None]
    cj = k[None, :]
    prev = np.where(cj > qi, 0.0, NEG).astype(np.float32)
    cur = np.where(cj <= qi, 0.0, NEG).astype(np.float32)
    meta0 = np.zeros((128, 16), np.float32)
    am0 = np.concatenate([meta0, prev, cur], axis=1)
    am1 = np.concatenate([meta0, np.full((128, 128), NEG, np.float32), cur], axis=1)
    amask = np.stack([am0, am1], axis=1)
    t = np.tile(np.arange(8), 4)[:, None]
    bufm = np.where(np.arange(128)[None, :] > t, 0.0, NEG).astype(np.float32)
    newm = np.where(np.arange(8)[None, :] <= t, 0.0, NEG).astype(np.float32)
    amask_s = np.concatenate([np.zeros((32, 16), np.float32), bufm, newm], axis=1)
    return dict(cm=cm, seqmask=seqmask, amask=amask, amask_s=amask_s,
                identf=np.eye(128, dtype=np.float32), onesf=ones)


def _pvec(v):
    return np.ascontiguousarray(np.asarray(v, np.float32).reshape(8, 128).T)


def prepare_inputs(inp, NT):
    f = lambda a: np.ascontiguousarray(np.asarray(a, dtype=np.float32))
    NTOK = NT * 512
    shared = _consts()
    vecs = [inp['a_norm_w'][0], inp['a_norm_w'][1], inp['kv_norm_w'], inp['b_norm_w'][0], inp['b_norm_w'][1],
            inp['mlp_norm_w'][0], inp['mlp_norm_w'][1], inp['mlp_norm_w'][2], inp['mlp_norm_w'][3],
            inp['final_norm_w']]
    shared['vecd'] = f(np.stack([_pvec(v) for v in vecs], axis=1))
    cw = np.asarray(inp['a_conv_w'], np.float32)
    shared['convw'] = f(cw.reshape(2, 4, 32, 128).transpose(3, 0, 2, 1))
    shared['convb'] = f(np.asarray(inp['a_conv_b'], np.float32).reshape(2, 32, 128).transpose(2, 0, 1))
    shared['gatew'] = f(np.asarray(inp['a_gate_norm_w'], np.float32).reshape(2, 16, 128).transpose(2, 0, 1))
    hv = np.stack([inp['a_dt_bias'], inp['a_log'], inp['a_d_skip']], axis=1)
    shared['hvec'] = f(np.broadcast_to(np.asarray(hv, np.float32)[None], (128, 2, 3, 32)))
    sk = np.asarray(inp['attn_sinks'], np.float32)
    shared['sinks'] = f(np.broadcast_to(sk[None], (128, 2, 16)))
    sr = sk.reshape(2, 4, 4)
    shared['sinkrows'] = f(np.repeat(sr.transpose(2, 0, 1), 8, axis=0))
    shared['metaT'] = f(np.asarray(inp['meta_tokens'], np.float32).T)
    shared['w_in'] = f(inp['a_in_proj'])
    shared['w_out'] = f(inp['a_out_proj'])
    shared['w_kv'] = f(inp['w_kv'])
    wq = np.asarray(inp['w_q'], np.float32).reshape(2, D, 2, 2, 4, 64)
    shared['w_q'] = f(wq.transpose(0, 1, 2, 4, 3, 5).reshape(2, D, D))
    wo = np.asarray(inp['w_o'], np.float32).reshape(2, 2, 2, 4, 64, D)
    shared['w_o'] = f(wo.transpose(0, 1, 3, 2, 4, 5).reshape(2, D, D))
    shared['w_up'] = f(inp['w_up'])
    shared['w_down'] = f(inp['w_down'])
    xp = np.asarray(inp['x_prompt'], np.float32)
    xs = np.asarray(inp['x_sample'], np.float32)
    sc = np.asarray(inp['state_conv'], np.float32)
    ss = np.asarray(inp['state_ssm'], np.float32)
    ck = np.asarray(inp['cache_k_win'], np.float32)
    cv = np.asarray(inp['cache_v_win'], np.float32)
    maps = []
    for c in range(N_CORES):
        m = dict(shared)
        m['xT'] = f(xp[c, :NTOK].T) if c < 2 else np.zeros((D, NTOK), np.float32)
        sl = slice(c * 16, (c + 1) * 16)
        m['xsT'] = f(xs[sl].reshape(128, D).T)
        m['sconvT'] = f(sc[:, sl].transpose(0, 3, 1, 2).reshape(2, CONV_DIM, 48))
        m['sssmT'] = f(ss[:, sl].reshape(2, 16, DI, 128).transpose(0, 1, 3, 2))
        m['ck'] = f(ck[sl].reshape(16, 128, 256))
        m['cv'] = f(cv[sl].reshape(16, 128, 256))
        m['ckT'] = f(ck[sl].reshape(16, 128, 256).transpose(0, 2, 1))
        maps.append(m)
    return maps


_PROGRAMS = {}


def get_program(NT, do_sample=True, stop_at=None):
    key = (NT, do_sample, stop_at)
    if key not in _PROGRAMS:
        _PROGRAMS[key] = Builder(NT, do_sample, stop_at).build()
    return _PROGRAMS[key]


def assemble(results, NT):
    NTOK = NT * 512
    y_prompt = np.stack([results[b]['ypT'].T for b in range(2)], axis=0)
    y_sample = np.concatenate([results[c]['ysT'].T.reshape(16, 8, D) for c in range(N_CORES)], axis=0)
    pconv = np.stack([results[b]['pconv'].transpose(1, 3, 2, 0).reshape(2, 3, CONV_DIM) for b in range(2)], axis=1)
    pssm = np.stack([results[b]['pssm'].transpose(0, 2, 1).reshape(2, NH, 64, 128) for b in range(2)], axis=1)
    pk = np.stack([results[b]['pk'].reshape(128, 4, 64) for b in range(2)], axis=0)
    pv = np.stack([results[b]['pv'].reshape(128, 4, 64) for b in range(2)], axis=0)
    sconv = np.concatenate(
        [results[c]['sconv_o'].transpose(1, 3, 4, 2, 0).reshape(2, 16, 3, CONV_DIM) for c in range(N_CORES)], axis=1)
    sssm = np.concatenate(
        [results[c]['sssm_o'].transpose(0, 1, 3, 2).reshape(2, 16, NH, 64, 128) for c in range(N_CORES)], axis=1)
    sk = np.concatenate([results[c]['sk_o'].reshape(16, 128, 4, 64) for c in range(N_CORES)], axis=0)
    sv = np.concatenate([results[c]['sv_o'].reshape(16, 128, 4, 64) for c in range(N_CORES)], axis=0)
    outs = (y_prompt, y_sample, pconv, pssm, pk, pv, sconv, sssm, sk, sv)
    return tuple(np.ascontiguousarray(o, dtype=np.float32) for o in outs)


def run(inputs, NT, do_sample=True, stop_at=None):
    nc = get_program(NT, do_sample, stop_at)
    maps = prepare_inputs(inputs, NT)
    res = run_bass_kernel_spmd(nc, maps, core_ids=list(range(N_CORES)))
    return assemble(res.results, NT)


def kernel(**inputs):
    return run(inputs, SEQ // 512)
```
